# Optimizing a Trainium2 kernel written in Bass

```python
import math
import jax, jax.numpy as jnp
from jax import lax
import numpy as np

D_MODEL = 1024
BATCH = 8
SEQ = 4096
DEPTH = 2

MLA_HEADS = 6
MLA_NOPE = 64
MLA_ROPE = 32
MLA_V = 64
MLA_Q_RANK = 256
MLA_KV_RANK = 128
ROPE_THETA = 10000.0
Q_BLOCK = 128
MLA_WIDTH = MLA_HEADS * MLA_V

HY_GROUPS = 6
HY_GROUP_W = 64
HY_WIDTH = HY_GROUPS * HY_GROUP_W
HY_ORDER = 2
HY_SHORT = 3
HY_BANDS = 8
HY_EMB = 1 + 2 * HY_BANDS
HY_FFN = 64
HY_FILTER_SCALE = 0.05
HY_FAST_DECAY = 0.3
HY_SLOW_DECAY = 1.5
HY_TARGET = 1e-2

NA_HEADS = 4
NA_HEAD_DIM = 64
NA_WIDTH = NA_HEADS * NA_HEAD_DIM
GRID_W = 64
NA_KH = 8
NA_KW = 16

D_MIX = MLA_WIDTH + HY_WIDTH + NA_WIDTH
IN_SIZES = (MLA_Q_RANK, MLA_KV_RANK, MLA_ROPE, 3 * HY_WIDTH, NA_WIDTH, NA_WIDTH, NA_WIDTH)
IN_COLS = MLA_Q_RANK + MLA_KV_RANK + MLA_ROPE + 3 * HY_WIDTH + 3 * NA_WIDTH
D_FF = ((8 * D_MODEL // 3 + 255) // 256) * 256
NORM_EPS = 1e-6

kernel_name = "hymba_style_mla_hyena_natten_encoder"


def rms_norm(x, g):
    xf = x.astype(jnp.float32)
    y = xf * lax.rsqrt(jnp.mean(xf * xf, axis=-1, keepdims=True) + NORM_EPS)
    return (y * g.astype(jnp.float32)).astype(x.dtype)


def split_cols(a, sizes):
    idx = np.cumsum(np.array(sizes))[:-1].tolist()
    return jnp.split(a, idx, axis=-1)


def rope_tables(S):
    pos = jnp.arange(S, dtype=jnp.float32)
    inv = ROPE_THETA ** (-jnp.arange(0, MLA_ROPE, 2, dtype=jnp.float32) / MLA_ROPE)
    ang = pos[:, None] * inv[None, :]
    return jnp.cos(ang), jnp.sin(ang)


def apply_rope(x, cos, sin):
    xf = x.astype(jnp.float32)
    half = xf.shape[-1] // 2
    x1, x2 = xf[..., :half], xf[..., half:]
    return jnp.concatenate([x1 * cos - x2 * sin, x1 * sin + x2 * cos], axis=-1).astype(x.dtype)


def mla_mixer(c_q, c_kv, k_pe, q_norm_g, w_uq, kv_norm_g, w_ukv):
    B, S, _ = c_q.shape
    q = (rms_norm(c_q, q_norm_g) @ w_uq).reshape(B, S, MLA_HEADS, MLA_NOPE + MLA_ROPE)
    kv = (rms_norm(c_kv, kv_norm_g) @ w_ukv).reshape(B, S, MLA_HEADS, MLA_NOPE + MLA_V)
    q_nope, q_pe = q[..., :MLA_NOPE], q[..., MLA_NOPE:]
    k_nope, v = kv[..., :MLA_NOPE], kv[..., MLA_NOPE:]
    cos, sin = rope_tables(S)
    q_pe = apply_rope(q_pe, cos[:, None, :], sin[:, None, :])
    k_pe = apply_rope(k_pe, cos, sin)
    scale = 1.0 / math.sqrt(MLA_NOPE + MLA_ROPE)
    nb = S // Q_BLOCK
    qn = q_nope.reshape(B, nb, Q_BLOCK, MLA_HEADS, MLA_NOPE).transpose(1, 0, 2, 3, 4)
    qp = q_pe.reshape(B, nb, Q_BLOCK, MLA_HEADS, MLA_ROPE).transpose(1, 0, 2, 3, 4)

    def block(args):
        qn_b, qp_b = args
        s = (jnp.einsum('bqhd,bkhd->bhqk', qn_b, k_nope)
             + jnp.einsum('bqhr,bkr->bhqk', qp_b, k_pe))
        p = jax.nn.softmax(s.astype(jnp.float32) * scale, axis=-1).astype(v.dtype)
        return jnp.einsum('bhqk,bkhd->bqhd', p, v)

    o = lax.map(block, (qn, qp))
    return o.transpose(1, 0, 2, 3, 4).reshape(B, S, MLA_WIDTH)


def hyena_filters(L, w1, b1, f1, w2, b2, f2, w3):
    t_idx = jnp.arange(L, dtype=jnp.float32)[:, None]
    t_norm = jnp.linspace(0.0, 1.0, L, dtype=jnp.float32)[:, None]
    bands = jnp.linspace(1e-4, HY_BANDS - 1, HY_BANDS, dtype=jnp.float32)[None, :]
    ang = 2.0 * math.pi * t_idx * bands / L
    z = jnp.concatenate([t_norm, jnp.cos(ang), jnp.sin(ang)], axis=-1)
    f32 = jnp.float32
    h = jnp.sin(f1.astype(f32) * (z @ w1.astype(f32) + b1.astype(f32)))
    h = jnp.sin(f2.astype(f32) * (h @ w2.astype(f32) + b2.astype(f32)))
    h = (h @ w3.astype(f32)).reshape(L, HY_ORDER, 2, HY_WIDTH)
    deltas = jnp.linspace(math.log(HY_TARGET) / HY_SLOW_DECAY,
                          math.log(HY_TARGET) / HY_FAST_DECAY, HY_WIDTH, dtype=jnp.float32)
    decay = jnp.exp(-t_norm * jnp.abs(deltas)[None, :])
    return h * decay[:, None, None, :]


def bidir_long_conv(u, h_f, h_b, skip):
    L = u.shape[1]
    n = 2 * L
    Hf = jnp.fft.rfft(h_f, n=n, axis=0)[None]
    Hb = jnp.fft.rfft(h_b, n=n, axis=0)[None]
    y_f = jnp.fft.irfft(jnp.fft.rfft(u, n=n, axis=1) * Hf, n=n, axis=1)[:, :L]
    ur = u[:, ::-1]
    y_b = jnp.fft.irfft(jnp.fft.rfft(ur, n=n, axis=1) * Hb, n=n, axis=1)[:, :L][:, ::-1]
    return y_f + y_b + skip[None, None, :] * u


def hyena_mixer(xh, conv_w, conv_b, w1, b1, f1, w2, b2, f2, w3, skip):
    B, L, _ = xh.shape
    xp = jnp.pad(xh, ((0, 0), (1, 1), (0, 0)))
    uc = conv_w[0] * xp[:, :-2] + conv_w[1] * xp[:, 1:-1] + conv_w[2] * xp[:, 2:] + conv_b
    v, x1, x2 = jnp.split(uc.astype(jnp.float32), 3, axis=-1)
    h = hyena_filters(L, w1, b1, f1, w2, b2, f2, w3)
    sk = skip.astype(jnp.float32)
    z = v
    for o, gate in enumerate((x1, x2)):
        z = gate * bidir_long_conv(z, h[:, o, 0], h[:, o, 1], sk[o])
    return z.astype(xh.dtype)


def natten_mixer(q, k, v, rpb):
    B, S, _ = q.shape
    R = S // GRID_W
    KH = min(NA_KH, R)
    KW = NA_KW
    qg = q.reshape(B, R, GRID_W, NA_HEADS, NA_HEAD_DIM)
    kg = k.reshape(B, R, GRID_W, NA_HEADS, NA_HEAD_DIM)
    vg = v.reshape(B, R, GRID_W, NA_HEADS, NA_HEAD_DIM)
    r = jnp.arange(R)
    row_idx = jnp.clip(r - KH // 2, 0, R - KH)[:, None] + jnp.arange(KH)[None, :]
    c = jnp.arange(GRID_W)
    col_idx = jnp.clip(c - KW // 2, 0, GRID_W - KW)[:, None] + jnp.arange(KW)[None, :]
    k_rows = kg[:, row_idx]
    v_rows = vg[:, row_idx]
    onehot = (col_idx[:, :, None] == c[None, None, :]).astype(q.dtype)
    s_rows = jnp.einsum('brwhd,brkvhd->brhwkv', qg, k_rows)
    s = jnp.einsum('brhwkv,wjv->brhwkj', s_rows, onehot)
    dr = row_idx - r[:, None] + (NA_KH - 1)
    dc = col_idx - c[:, None] + (NA_KW - 1)
    bias = rpb[:, dr[:, None, :, None], dc[None, :, None, :]]
    bias = bias.transpose(1, 0, 2, 3, 4)[None]
    logits = s.astype(jnp.float32) * (1.0 / math.sqrt(NA_HEAD_DIM)) + bias.astype(jnp.float32)
    shp = logits.shape
    p = jax.nn.softmax(logits.reshape(shp[:-2] + (KH * KW,)), axis=-1).reshape(shp).astype(v.dtype)
    p_rows = jnp.einsum('brhwkj,wjv->brhwkv', p, onehot)
    o = jnp.einsum('brhwkv,brkvhd->brwhd', p_rows, v_rows)
    return o.reshape(B, S, NA_WIDTH)


def setup_inputs(seed: int = 0) -> dict:
    key = jax.random.key(seed)
    ks = iter(jax.random.split(key, 32))
    f32 = jnp.float32

    def w(shape, fan_in, scale=1.0):
        return jax.random.normal(next(ks), shape, f32) * (scale * fan_in ** -0.5)

    def gain(shape):
        return jnp.ones(shape, f32) + 0.02 * jax.random.normal(next(ks), shape, f32)

    def small(shape, s=0.02):
        return s * jax.random.normal(next(ks), shape, f32)

    return {
        "x": jax.random.normal(next(ks), (BATCH, SEQ, D_MODEL), f32),
        "norm1_g": gain((DEPTH, D_MODEL)),
        "w_in": w((DEPTH, D_MODEL, IN_COLS), D_MODEL),
        "mla_q_norm_g": gain((DEPTH, MLA_Q_RANK)),
        "mla_w_uq": w((DEPTH, MLA_Q_RANK, MLA_HEADS * (MLA_NOPE + MLA_ROPE)), MLA_Q_RANK),
        "mla_kv_norm_g": gain((DEPTH, MLA_KV_RANK)),
        "mla_w_ukv": w((DEPTH, MLA_KV_RANK, MLA_HEADS * (MLA_NOPE + MLA_V)), MLA_KV_RANK),
        "hy_conv_w": w((DEPTH, HY_SHORT, 3 * HY_WIDTH), HY_SHORT),
        "hy_conv_b": small((DEPTH, 3 * HY_WIDTH)),
        "hy_filt_w1": w((DEPTH, HY_EMB, HY_FFN), HY_EMB),
        "hy_filt_b1": small((DEPTH, HY_FFN)),
        "hy_filt_freq1": gain((DEPTH, HY_FFN)),
        "hy_filt_w2": w((DEPTH, HY_FFN, HY_FFN), HY_FFN),
        "hy_filt_b2": small((DEPTH, HY_FFN)),
        "hy_filt_freq2": gain((DEPTH, HY_FFN)),
        "hy_filt_w3": w((DEPTH, HY_FFN, HY_ORDER * 2 * HY_WIDTH), HY_FFN, HY_FILTER_SCALE),
        "hy_skip": small((DEPTH, HY_ORDER, HY_WIDTH), 0.1),
        "na_rpb": small((DEPTH, NA_HEADS, 2 * NA_KH - 1, 2 * NA_KW - 1)),
        "mix_norm_g": gain((DEPTH, D_MIX)),
        "w_out": w((DEPTH, D_MIX, D_MODEL), D_MIX),
        "norm2_g": gain((DEPTH, D_MODEL)),
        "ffn_w_gate": w((DEPTH, D_MODEL, D_FF), D_MODEL),
        "ffn_w_up": w((DEPTH, D_MODEL, D_FF), D_MODEL),
        "ffn_w_down": w((DEPTH, D_FF, D_MODEL), D_FF),
        "final_norm_g": gain((D_MODEL,)),
    }


def reference(x, norm1_g, w_in, mla_q_norm_g, mla_w_uq, mla_kv_norm_g, mla_w_ukv,
              hy_conv_w, hy_conv_b, hy_filt_w1, hy_filt_b1, hy_filt_freq1, hy_filt_w2,
              hy_filt_b2, hy_filt_freq2, hy_filt_w3, hy_skip, na_rpb, mix_norm_g, w_out,
              norm2_g, ffn_w_gate, ffn_w_up, ffn_w_down, final_norm_g):
    for l in range(DEPTH):
        h = rms_norm(x, norm1_g[l])
        proj = h @ w_in[l]
        c_q, c_kv, k_pe, hy_in, na_q, na_k, na_v = split_cols(proj, IN_SIZES)
        y_a = mla_mixer(c_q, c_kv, k_pe, mla_q_norm_g[l], mla_w_uq[l],
                        mla_kv_norm_g[l], mla_w_ukv[l])
        y_b = hyena_mixer(hy_in, hy_conv_w[l], hy_conv_b[l], hy_filt_w1[l], hy_filt_b1[l],
                          hy_filt_freq1[l], hy_filt_w2[l], hy_filt_b2[l], hy_filt_freq2[l],
                          hy_filt_w3[l], hy_skip[l])
        y_c = natten_mixer(na_q, na_k, na_v, na_rpb[l])
        g = mix_norm_g[l]
        y = jnp.concatenate([
            rms_norm(y_a, g[:MLA_WIDTH]),
            rms_norm(y_b, g[MLA_WIDTH:MLA_WIDTH + HY_WIDTH]),
            rms_norm(y_c, g[MLA_WIDTH + HY_WIDTH:]),
        ], axis=-1)
        x = x + y @ w_out[l]
        h2 = rms_norm(x, norm2_g[l])
        x = x + (jax.nn.silu(h2 @ ffn_w_gate[l]) * (h2 @ ffn_w_up[l])) @ ffn_w_down[l]
    return rms_norm(x, final_norm_g)
```

```python
import contextlib
import math
import numpy as np
import ml_dtypes
import concourse.bass as bass
import concourse.mybir as mybir
from concourse.bass_utils import run_bass_kernel_spmd

F32 = mybir.dt.float32
BF16 = mybir.dt.bfloat16
ALU = mybir.AluOpType
AF = mybir.ActivationFunctionType

S = 4096
D = 1024
NL = 2
DFF = 2816
NB = 8
TB = 512
NT = 32
NKF = 9
WIN_COLS = 2496
NV = 80
SEM_ROLL = 20000
MAGIC = 12582912.0
PI_SAFE = 3.1415925


class Dep:
    __slots__ = ("name", "writers", "readers", "multi", "dsem", "dval")
    scope = None

    def __init__(self, name="", multi=False):
        self.name = name
        self.writers = {}
        self.readers = {}
        self.multi = multi
        self.dsem = {}
        self.dval = 0
        if Dep.scope is not None:
            Dep.scope[-1].append(self)


def _merge(d, t):
    k = id(t[0])
    if k not in d or d[k][1] < t[1]:
        d[k] = t


class Prog:
    ENGS = ("pe", "act", "dve", "pool", "sp")

    def __init__(self, nc):
        self.nc = nc
        self.stacks = [contextlib.ExitStack()]
        self.q = {e: [] for e in self.ENGS}
        self.esem = {}
        self.ecnt = {e: 0 for e in self.ENGS}
        self.waited = {e: {} for e in self.ENGS}
        self.nsem = 0
        self.dma_t = {}
        self.uid = 0
        self.deferred = []
        self.free_sems = {"hw": [], "sw": []}
        Dep.scope = [[]]
        for e in self.ENGS:
            self.esem[e] = self.new_sem("c_" + e)

    def new_sem(self, name):
        self.nsem += 1
        return self.stacks[0].enter_context(self.nc.semaphore(f"s{self.nsem}_{name}"))

    def push(self):
        self.stacks.append(contextlib.ExitStack())
        Dep.scope.append([])

    def pop(self):
        self.barrier()
        self.stacks.pop().close()
        for d in Dep.scope.pop():
            for kind, sv in d.dsem.items():
                if sv[1] < SEM_ROLL:
                    self.free_sems[kind].append(sv)
            d.dsem = {}

    def acquire(self, name, kind):
        if self.free_sems[kind]:
            return self.free_sems[kind].pop()
        return [self.new_sem("d_" + kind + name), 0]

    def sbuf(self, name, shape, dt):
        self.uid += 1
        return self.stacks[-1].enter_context(self.nc.sbuf_tensor(f"{name}_{self.uid}", list(shape), dt))

    def psum(self, name, shape, dt=F32):
        return self.stacks[-1].enter_context(self.nc.psum_tensor(name, list(shape), dt))

    def _collect(self, eng, reads, writes):
        tk = {}
        for d in reads:
            for t in d.writers.values():
                _merge(tk, t)
        for d in writes:
            for t in d.readers.values():
                _merge(tk, t)
            if not d.multi:
                for t in d.writers.values():
                    _merge(tk, t)
        waits = []
        w = self.waited[eng]
        for k, (sem, val) in tk.items():
            if eng == "pe" and sem is self.esem["pe"]:
                continue
            if w.get(k, 0) >= val:
                continue
            w[k] = val
            waits.append((sem, val))
        return waits

    def _record(self, t, reads, writes):
        for d in reads:
            _merge(d.readers, t)
        for d in writes:
            if d.multi:
                _merge(d.writers, t)
            else:
                d.writers = {id(t[0]): t}
                d.readers = {}

    def op(self, eng, fn, reads=(), writes=()):
        waits = self._collect(eng, reads, writes)
        if self.ecnt[eng] >= SEM_ROLL:
            self.esem[eng] = self.new_sem("c_" + eng)
            self.ecnt[eng] = 0
        self.ecnt[eng] += 1
        t = (self.esem[eng], self.ecnt[eng])
        self.q[eng].append((waits, fn, self.esem[eng], 1))
        self._record(t, reads, writes)
        return t

    def dma(self, queue, out, in_, reads=(), writes=(), slot=None, **kw):
        waits = self._collect(queue, reads, writes)
        d0 = slot if slot is not None else writes[0]
        kind = "sw" if queue == "pool" else "hw"
        w = self.waited[queue]
        sv = d0.dsem.get(kind)
        if sv is not None and w.get(id(sv[0]), 0) < sv[1]:
            w[id(sv[0])] = sv[1]
            waits.append((sv[0], sv[1]))
        if sv is None or sv[1] >= SEM_ROLL:
            sv = self.acquire(d0.name, kind)
            d0.dsem[kind] = sv
        sv[1] += 16
        dsem = sv[0]
        t = (dsem, sv[1])
        self.dma_t[id(dsem)] = t

        def fn(e, out=out, in_=in_, kw=kw):
            return e.dma_start(out=out, in_=in_, **kw)

        self.q[queue].append((waits, fn, dsem, 16))
        self._record(t, reads, writes)
        return t

    def store(self, out, in_, reads=(), writes=(), slot=None):
        self.deferred.append((out, in_, reads, writes, slot))

    def flush(self):
        for out, in_, reads, writes, slot in self.deferred:
            self.dma("sp", out, in_, reads=reads, writes=writes, slot=slot)
        self.deferred = []

    def barrier(self):
        self.flush()
        for e in self.ENGS:
            w = self.waited[e]
            waits = []
            for e2 in self.ENGS:
                if e2 == e or self.ecnt[e2] == 0:
                    continue
                sem, val = self.esem[e2], self.ecnt[e2]
                if w.get(id(sem), 0) < val:
                    w[id(sem)] = val
                    waits.append((sem, val))
            for k, (sem, val) in self.dma_t.items():
                if w.get(k, 0) < val:
                    w[k] = val
                    waits.append((sem, val))
            self.q[e].append((waits, None, None, 0))

    def emit(self):
        nc = self.nc
        q = self.q
        with nc.Block() as block:
            def run(e, lst):
                for waits, fn, sem, inc in lst:
                    for (s, v) in waits:
                        e.wait_ge(s, v)
                    if fn is not None:
                        fn(e).then_inc(sem, inc)

            @block.tensor
            def _(e):
                run(e, q["pe"])

            @block.scalar
            def _(e):
                run(e, q["act"])

            @block.vector
            def _(e):
                run(e, q["dve"])

            @block.gpsimd
            def _(e):
                run(e, q["pool"])

            @block.sync
            def _(e):
                run(e, q["sp"])

    def close(self):
        while self.stacks:
            self.stacks.pop().close()

    def mm(self, out, lhsT, rhs, start, stop, reads, writes, skip=False):
        return self.op("pe", lambda e: e.matmul(out, lhsT, rhs, start=start, stop=stop,
                                                skip_group_check=skip), reads, writes)

    def tr(self, out, in_, ident, reads, writes):
        return self.op("pe", lambda e: e.transpose(out, in_, ident), reads, writes)

    def act(self, out, in_, func, reads, writes, scale=None, bias=None, accum=None):
        kw = {}
        if scale is not None:
            kw["scale"] = scale
        if bias is not None:
            kw["bias"] = bias
        if accum is not None:
            kw["accum_out"] = accum
        return self.op("act", lambda e: e.activation(out=out, in_=in_, func=func, **kw), reads, writes)

    def tt(self, eng, out, in0, in1, op, reads, writes):
        return self.op(eng, lambda e: e.tensor_tensor(out=out, in0=in0, in1=in1, op=op), reads, writes)

    def ts(self, eng, out, in0, s1, s2, op0, op1, reads, writes):
        if op1 is None and eng == "pool" and op0 == ALU.mult:
            op1, s2 = ALU.add, 0.0
        if op1 is None:
            return self.op(eng, lambda e: e.tensor_scalar(out=out, in0=in0, scalar1=s1, scalar2=None, op0=op0),
                           reads, writes)
        return self.op(eng, lambda e: e.tensor_scalar(out=out, in0=in0, scalar1=s1, scalar2=s2, op0=op0, op1=op1),
                       reads, writes)

    def stt(self, out, in0, scalar, in1, op0, op1, reads, writes):
        return self.op("dve", lambda e: e.scalar_tensor_tensor(out=out, in0=in0, scalar=scalar, in1=in1,
                                                               op0=op0, op1=op1), reads, writes)

    def cp(self, eng, out, in_, reads, writes):
        if eng == "act":
            return self.op("act", lambda e: e.activation(out=out, in_=in_, func=AF.Copy), reads, writes)
        return self.op(eng, lambda e: e.tensor_copy(out=out, in_=in_), reads, writes)

    def memset(self, eng, ap, val, writes):
        return self.op(eng, lambda e: e.memset(ap, val), (), writes)

    def recip(self, out, in_, reads, writes):
        return self.op("dve", lambda e: e.reciprocal(out=out, in_=in_), reads, writes)


class Ring:
    def __init__(self, P, name, shape, dt, n):
        self.t = [P.sbuf(f"{name}{i}", shape, dt) for i in range(n)]
        self.d = [Dep(f"{name}{i}") for i in range(n)]
        self.i = 0
        self.n = n

    def next(self):
        i = self.i
        self.i = (i + 1) % self.n
        return self.t[i], self.d[i]


class Banks:
    def __init__(self, P):
        self.ps = P.psum("psall", [128, 4096], F32)
        self.d = [Dep(f"bank{i}") for i in range(8)]
        self.i = 0

    def next(self):
        i = self.i
        self.i = (i + 1) % 8
        return self.ps[:, i * 512:(i + 1) * 512], self.d[i]

    def get(self, i):
        return self.ps[:, i * 512:(i + 1) * 512], self.d[i]

    def next2(self):
        if self.i % 2:
            self.i = (self.i + 1) % 8
        i = self.i
        self.i = (i + 2) % 8
        return self.ps[:, i * 512:(i + 2) * 512], self.d[i], self.d[i + 1]


def build_program(dbg=None):
    nc = bass.Bass("TRN2", target_bir_lowering=False)
    P = Prog(nc)
    skind = "ExternalOutput" if dbg else "Internal"

    def din(name, shape, dt=F32):
        return nc.dram_tensor(name, list(shape), dt, kind="ExternalInput").ap()

    def dscr(name, shape, dt=F32):
        return nc.dram_tensor(name, list(shape), dt, kind=skind).ap()

    x_in = din("x", [S, D])
    w_inA = din("w_inA", [NL, D, WIN_COLS])
    w_uq2 = din("w_uq2", [NL, 256, 1152])
    w_kv2 = din("w_kv2", [NL, 128, 768])
    vecs = din("vecs", [NL, 128, NV])
    w_f1 = din("w_f1", [NL, 17, 64])
    w_f2 = din("w_f2", [NL, 64, 64])
    w_f3 = din("w_f3", [NL, 64, 1536])
    skipb = din("skipb", [NL, 128, 768])
    nabias = din("nabias", [NL, 5, 128, 2560])
    w_out = din("w_out", [NL, D, D])
    w_gate = din("w_gate", [NL, D, DFF])
    w_up = din("w_up", [NL, D, DFF])
    w_down = din("w_down", [NL, DFF, D])
    ident_in = din("ident", [128, 128])
    ropeC = din("ropeC", [32, S])
    ropeS = din("ropeS", [32, S])
    zT_in = din("zT", [17, S])
    decay_in = din("decay", [S, 384])
    CfF = din("CfF", [NKF, 128, 4096], BF16)
    SfF = din("SfF", [NKF, 128, 4096], BF16)
    CfI = din("CfI", [NT, 128, NKF * 128], BF16)
    SfI = din("SfI", [NT, 128, NKF * 128], BF16)
    wk_in = din("wk", [128, 8 * NKF])
    out_ap = nc.dram_tensor("out", [S, D], F32, kind="ExternalOutput").ap()

    xT = dscr("xT", [8, 128, S])
    QT = dscr("QT", [6, 96, S], BF16)
    KT = dscr("KT", [6, 96, S], BF16)
    Vm = dscr("Vm", [6, 128, NT, 65], BF16)
    hyT = dscr("hyT", [9, 128, S])
    naQT = dscr("naQT", [2, 128, S], BF16)
    naKT = dscr("naKT", [2, 128, S], BF16)
    naV = dscr("naV", [NT, 128, 260], BF16)
    ymix = dscr("ymix", [S, D])
    xg = dscr("xg", [2, S, 384])
    Gs = dscr("Gs", [2, NKF, 128, 4 * 768])
    d_xT = [Dep(f"xT{b}") for b in range(NB)]
    d_QT = Dep("QT", multi=True)
    d_KT = Dep("KT", multi=True)
    d_Vm = Dep("Vm", multi=True)
    d_hyT = Dep("hyT", multi=True)
    d_naQT = Dep("naQT", multi=True)
    d_naKT = Dep("naKT", multi=True)
    d_naV = Dep("naV", multi=True)
    d_ymix = Dep("ymix", multi=True)
    d_xg = Dep("xg", multi=True)
    d_Gs = Dep("Gs", multi=True)
    d_out = Dep("out", multi=True)

    banks = Banks(P)
    ident = P.sbuf("ident", [128, 128], F32)
    d_ident = Dep("ident")
    P.dma("sp", ident[:], ident_in, writes=[d_ident])
    ones_b = P.sbuf("ones_b", [128, 128], BF16)
    d_const = Dep("const")
    P.memset("dve", ones_b[:], 1.0, [d_const])
    epsc = P.sbuf("epsc", [128, 1], F32)
    P.memset("dve", epsc[:], 1e-6, [d_const])
    vec = [P.sbuf(f"vec{l}", [128, NV], F32) for l in range(NL)]
    d_vec = Dep("vec")
    for l in range(NL):
        P.dma("sp", vec[l][:], vecs[l], writes=[d_vec])
    wkt = P.sbuf("wkt", [128, 8 * NKF], F32)
    P.dma("sp", wkt[:], wk_in, writes=[d_vec])

    def stop(name):
        return dbg == name

    def finish():
        P.barrier()
        P.emit()
        P.close()
        return nc

    def rms_p1(xt, d_xt, nch, sq, d_sq):
        P.act(sq[:, 0:nch, :], xt[:, 0:nch, :], AF.Square, [d_xt], [d_sq])

    def rms_p2(xt, d_xt, nch, gcol, vl, outT, d_out_, sq, d_sq, rs, d_rs, n_feat):
        bk, d_bk = banks.next()
        for k in range(nch):
            P.mm(bk, ones_b[:, :], sq[:, k, :], k == 0, k == nch - 1, [d_sq, d_const], [d_bk])
        P.act(rs[:, :], bk, AF.Sqrt, [d_bk, d_const], [d_rs], scale=1.0 / n_feat, bias=epsc[:, 0:1])
        P.recip(rs[:, :], rs[:, :], [d_rs], [d_rs])
        for k in range(nch):
            P.stt(outT[:, k, :], xt[:, k, :], vl[:, gcol + k:gcol + k + 1], rs[:, :], ALU.mult, ALU.mult,
                  [d_xt, d_rs, d_vec], [d_out_])

    def rms_feature_major(xt, d_xt, nch, gcol, vl, outT, d_out_, sq, d_sq, rs, d_rs, n_feat):
        rms_p1(xt, d_xt, nch, sq, d_sq)
        rms_p2(xt, d_xt, nch, gcol, vl, outT, d_out_, sq, d_sq, rs, d_rs, n_feat)

    def tcols(ap2, tau):
        return ap2.rearrange("q (a p r) -> q r a p", a=8, p=128, r=4)[:, tau // 8, tau % 8, :]

    def trows(ap2, tau):
        return ap2.rearrange("(a p r) c -> r a p c", a=8, p=128, r=4)[tau // 8, tau % 8]

    class FoldBufs:
        def __init__(self):
            self.cp = Ring(P, "fcp", [128, 384], F32, 4)
            self.l1 = {n: Ring(P, "f1" + n, [128, 384], F32, 1) for n in ("P", "Q", "Pp", "Qp", "R", "T", "Rp", "Tp")}
            self.l2 = {n: Ring(P, "f2" + n, [128, 384], F32, 1) for n in
                       ("A1", "A2", "A3", "A4", "B1", "B2", "B3", "B4")}

    def fold_forward(*a, **k):
        g = fold_forward_g(*a, **k)
        try:
            while True:
                next(g)
        except StopIteration as e:
            return e.value

    def fold_forward_g(fb, src, d_src, Cb, Sb, d_C, d_S, need, l2eng=None, bases=(0, 4), cp_eng="act"):
        Cv = Cb[:, :].rearrange("p (r a f) -> p r a f", r=4, a=8)
        Sv = Sb[:, :].rearrange("p (r a f) -> p r a f", r=4, a=8)
        L1 = {}
        for ps_, (ra, rb) in enumerate(((0, 2), (1, 3))):
            bks = [banks.get(bases[ps_] + i) for i in range(4)]
            for i, (r, Mv, d_M) in enumerate(((ra, Cv, d_C), (ra, Sv, d_S), (rb, Cv, d_C), (rb, Sv, d_S))):
                for a in range(8):
                    P.mm(bks[i][0][:, 0:384], Mv[:, r, a, :], src(r * 8 + a), a == 0, a == 7, [d_M, d_src],
                         [bks[i][1]])
                    if a % 2:
                        yield
            cA, d_cA = fb.cp.next()
            cB, d_cB = fb.cp.next()
            P.cp(cp_eng, cA[:, :], bks[2][0][:, 0:384], [bks[2][1]], [d_cA])
            P.cp(cp_eng, cB[:, :], bks[3][0][:, 0:384], [bks[3][1]], [d_cB])
            names = ("P", "Q", "Pp", "Qp") if ps_ == 0 else ("R", "T", "Rp", "Tp")
            specs = ((bks[0], cA, d_cA, ALU.add), (bks[0], cA, d_cA, ALU.subtract),
                     (bks[1], cB, d_cB, ALU.add), (bks[1], cB, d_cB, ALU.subtract))
            for nm, (bk, cc, d_cc, op) in zip(names, specs):
                t_, d_t = fb.l1[nm].next()
                P.tt("dve", t_[:, :], bk[0][:, 0:384], cc[:, :], op, [bk[1], d_cc], [d_t])
                L1[nm] = (t_, d_t)
        out = {}

        def l2(nm, x, y, op):
            t_, d_t = fb.l2[nm].next()
            eng = "pool" if l2eng is None else l2eng[nm[1]]
            P.tt(eng, t_[:, :], L1[x][0][:, :], L1[y][0][:, :], op, [L1[x][1], L1[y][1]], [d_t])
            out[nm] = (t_, d_t)

        if "A" in need:
            l2("A1", "P", "R", ALU.add)
            l2("A4", "P", "R", ALU.subtract)
            l2("A2", "Q", "Tp", ALU.add)
            l2("A3", "Q", "Tp", ALU.subtract)
        if "B" in need:
            l2("B1", "Pp", "Rp", ALU.add)
            l2("B4", "Rp", "Pp", ALU.subtract)
            l2("B2", "T", "Qp", ALU.subtract)
            l2("B3", "Qp", "T", ALU.add)
        return out

    P.push()
    xin_r = Ring(P, "xin", [128, D], F32, 2)
    xtb_r = Ring(P, "xtb0", [128, 8, TB], F32, 2)
    for b in range(NB):
        xtb, d_xtb = xtb_r.next()
        for i in range(4):
            ti = b * 4 + i
            xin, d_xin = xin_r.next()
            P.dma("sp", xin[:], x_in[ti * 128:(ti + 1) * 128, :], writes=[d_xin])
            if i == 0:
                P.flush()
            for j in range(2):
                bk, d_bk = banks.next()
                for c in range(4):
                    k = j * 4 + c
                    P.tr(bk[:, c * 128:(c + 1) * 128], xin[:, k * 128:(k + 1) * 128], ident[:, :],
                         [d_xin, d_ident], [d_bk])
                P.cp("act" if j == 0 else "dve", xtb[:, j * 4:(j + 1) * 4, i * 128:(i + 1) * 128],
                     bk.rearrange("p (a b) -> p a b", a=4), [d_bk], [d_xtb])
        P.store(xT[:, :, b * TB:(b + 1) * TB].rearrange("k p s -> p k s"), xtb[:], reads=[d_xtb], writes=[d_xT[b]])
    P.pop()
    if stop("p0"):
        return finish()

    for l in range(NL):
        vl = vec[l]
        P.push()
        winA = P.sbuf("winA", [128, 8, WIN_COLS], BF16)
        d_w = Dep("winA")
        P.dma("pool", winA[:], w_inA[l].rearrange("(k p) c -> p k c", p=128), writes=[d_w])
        wuq = P.sbuf("wuq", [128, 2, 1152], BF16)
        d_wuq = Dep("wuq")
        P.dma("pool", wuq[:], w_uq2[l].rearrange("(j p) c -> p j c", p=128), writes=[d_wuq])
        wkv = P.sbuf("wkv", [128, 768], BF16)
        d_wkv = Dep("wkv")
        P.dma("pool", wkv[:], w_kv2[l], writes=[d_wkv])
        xtb_r = Ring(P, "xtbA", [128, 8, TB], F32, 2)
        sq = P.sbuf("sqA", [128, 8, TB], BF16)
        d_sq = Dep("sqA")
        rs = P.sbuf("rsA", [128, TB], F32)
        d_rs = Dep("rsA")
        sqq = P.sbuf("sqq", [128, 2, TB], BF16)
        d_sqq = Dep("sqq")
        rsq = P.sbuf("rsq", [128, TB], F32)
        d_rsq = Dep("rsq")
        sqk = P.sbuf("sqk", [128, 1, TB], BF16)
        d_sqk = Dep("sqk")
        rsk = P.sbuf("rsk", [128, TB], F32)
        d_rsk = Dep("rsk")
        hT_r = Ring(P, "hT", [128, 8, TB], BF16, 2)
        cq = P.sbuf("cq", [128, 2, TB], F32)
        d_cq = Dep("cq")
        cqn = P.sbuf("cqn", [128, 2, TB], BF16)
        d_cqn = Dep("cqn")
        ckv = P.sbuf("ckv", [128, 1, TB], F32)
        d_ckv = Dep("ckv")
        ckvn = P.sbuf("ckvn", [128, 1, TB], BF16)
        d_ckvn = Dep("ckvn")
        rC_r = Ring(P, "rC", [128, TB], F32, 2)
        rS_r = Ring(P, "rS", [128, TB], F32, 2)
        tA_r = Ring(P, "tA", [128, TB], F32, 2)
        tB_r = Ring(P, "tB", [128, TB], F32, 2)
        QTb_r = Ring(P, "QTb", [128, 6, TB], BF16, 2)
        KTb_r = Ring(P, "KTb", [128, 6, TB], BF16, 2)
        Vb_r = Ring(P, "Vb", [128, 6, 4, 65], BF16, 2)
        hyb_r = Ring(P, "hyb", [128, 3, TB], F32, 3)
        nqk_r = Ring(P, "nqk", [128, 4, TB], BF16, 2)
        nvb_r = Ring(P, "nvb", [128, 4, 260], BF16, 2)
        for r_ in (Vb_r, nvb_r):
            for t_, d_ in zip(r_.t, r_.d):
                P.memset("pool", t_[:], 1.0, [d_])
        for b in range(NB):
            sl = slice(b * TB, (b + 1) * TB)
            xtb, d_xtb = xtb_r.next()
            P.dma("sp", xtb[:], xT[:, :, sl].rearrange("k p s -> p k s"), reads=[d_xT[b]], writes=[d_xtb])
            rC, d_rC = rC_r.next()
            rS, d_rS = rS_r.next()
            P.dma("sp", rC[64:96, :], ropeC[:, sl], writes=[d_rC])
            P.dma("sp", rS[64:96, :], ropeS[:, sl], writes=[d_rS])
            P.flush()
            hT, d_hT = hT_r.next()
            rms_feature_major(xtb, d_xtb, 8, 0, vl, hT, d_hT, sq, d_sq, rs, d_rs, D)

            def proj(c0, M):
                bk, d_bk = banks.next()
                for k in range(8):
                    P.mm(bk[0:M, :], winA[:, k, c0:c0 + M], hT[:, k, :], k == 0, k == 7, [d_w, d_hT], [d_bk])
                return bk, d_bk

            for j in range(2):
                bk, d_bk = proj(j * 128, 128)
                P.cp("act", cq[:, j, :], bk, [d_bk], [d_cq])
            bk, d_bk = proj(256, 128)
            P.cp("act", ckv[:, 0, :], bk, [d_bk], [d_ckv])
            rms_p1(cq, d_cq, 2, sqq, d_sqq)
            rms_p1(ckv, d_ckv, 1, sqk, d_sqk)

            KTb, d_KTb = KTb_r.next()
            bk, d_bk = proj(384, 96)
            bk2, d_bk2 = proj(480, 96)
            tA, d_tA = tA_r.next()
            tB, d_tB = tB_r.next()
            P.tt("dve", tA[64:96, :], bk[64:96, :], rC[64:96, :], ALU.mult, [d_bk, d_rC], [d_tA])
            P.tt("dve", tB[64:96, :], bk2[64:96, :], rS[64:96, :], ALU.mult, [d_bk2, d_rS], [d_tB])
            for h in range(6):
                P.tt("pool", KTb[64:96, h, :], tA[64:96, :], tB[64:96, :], ALU.add, [d_tA, d_tB], [d_KTb])

            for g3 in range(3):
                hyb, d_hyb = hyb_r.next()
                for c3 in range(3):
                    c = g3 * 3 + c3
                    bk, d_bk = proj(576 + c * 128, 128)
                    P.cp("act" if c % 2 else "dve", hyb[:, c3, :], bk, [d_bk], [d_hyb])
                P.store(hyT[g3 * 3:(g3 + 1) * 3, :, sl].rearrange("c p s -> p c s"), hyb[:], reads=[d_hyb],
                      writes=[d_hyT], slot=d_hyb)
                if g3 == 0:
                    rms_p2(cq, d_cq, 2, 16, vl, cqn, d_cqn, sqq, d_sqq, rsq, d_rsq, 256)
                    rms_p2(ckv, d_ckv, 1, 18, vl, ckvn, d_ckvn, sqk, d_sqk, rsk, d_rsk, 128)

            nqk, d_nqk = nqk_r.next()
            for c in range(4):
                bk, d_bk = proj(1728 + c * 128, 128)
                P.cp("act" if c % 2 else "dve", nqk[:, c, :], bk, [d_bk], [d_nqk])
            P.store(naQT[:, :, sl].rearrange("c p s -> p c s"), nqk[:, 0:2, :], reads=[d_nqk], writes=[d_naQT],
                  slot=d_nqk)
            P.store(naKT[:, :, sl].rearrange("c p s -> p c s"), nqk[:, 2:4, :], reads=[d_nqk], writes=[d_naKT],
                  slot=d_nqk)
            nvb, d_nvb = nvb_r.next()
            for i in range(4):
                bk, d_bk = banks.next()
                for k in range(8):
                    P.mm(bk[:, 0:256], hT[:, k, i * 128:(i + 1) * 128], winA[:, k, 2240:2496], k == 0, k == 7,
                         [d_w, d_hT], [d_bk])
                P.cp("act" if i % 2 else "dve",
                     nvb[:, i, :].rearrange("p (h c) -> p h c", h=4)[:, :, 0:64],
                     bk[:, 0:256].rearrange("p (h c) -> p h c", h=4), [d_bk], [d_nvb])
            P.store(naV[b * 4:(b + 1) * 4].rearrange("t p c -> p t c"), nvb[:], reads=[d_nvb], writes=[d_naV],
                  slot=d_nvb)

            QTb, d_QTb = QTb_r.next()
            for h in range(6):
                bk, d_bk = banks.next()
                bk2, d_bk2 = banks.next()
                for j in range(2):
                    P.mm(bk[0:96, :], wuq[:, j, h * 192:h * 192 + 96], cqn[:, j, :], j == 0, j == 1, [d_wuq, d_cqn], [d_bk])
                for j in range(2):
                    P.mm(bk2[0:96, :], wuq[:, j, h * 192 + 96:h * 192 + 192], cqn[:, j, :], j == 0, j == 1,
                         [d_wuq, d_cqn], [d_bk2])
                P.cp("act", QTb[0:64, h, :], bk[0:64, :], [d_bk], [d_QTb])
                tA, d_tA = tA_r.next()
                tB, d_tB = tB_r.next()
                P.tt("dve", tA[64:96, :], bk[64:96, :], rC[64:96, :], ALU.mult, [d_bk, d_rC], [d_tA])
                P.tt("dve", tB[64:96, :], bk2[64:96, :], rS[64:96, :], ALU.mult, [d_bk2, d_rS], [d_tB])
                P.tt("pool", QTb[64:96, h, :], tA[64:96, :], tB[64:96, :], ALU.add, [d_tA, d_tB], [d_QTb])
            P.store(QT[:, :, sl].rearrange("h p s -> p h s"), QTb[0:96, :, :], reads=[d_QTb], writes=[d_QT],
                  slot=d_QTb)

            for h in range(6):
                bk, d_bk = banks.next()
                P.mm(bk[0:64, :], wkv[:, h * 64:(h + 1) * 64], ckvn[:, 0, :], True, True, [d_wkv, d_ckvn], [d_bk])
                P.cp("act" if h % 2 else "dve", KTb[0:64, h, :], bk[0:64, :], [d_bk], [d_KTb])
            P.store(KT[:, :, sl].rearrange("h p s -> p h s"), KTb[0:96, :, :], reads=[d_KTb], writes=[d_KT],
                  slot=d_KTb)
            Vb, d_Vb = Vb_r.next()
            for i in range(4):
                bk, d_bk = banks.next()
                P.mm(bk[:, 0:384], ckvn[:, 0, i * 128:(i + 1) * 128], wkv[:, 384:768], True, True,
                     [d_wkv, d_ckvn], [d_bk])
                P.cp("act" if i % 2 else "dve", Vb[:, :, i, 0:64],
                     bk[:, 0:384].rearrange("p (h c) -> p h c", h=6), [d_bk], [d_Vb])
            P.store(Vm[:, :, b * 4:(b + 1) * 4, :].rearrange("h p t c -> p h t c"), Vb[:], reads=[d_Vb], writes=[d_Vm],
                  slot=d_Vb)
        P.pop()
        if stop(f"A{l}"):
            return finish()

        P.push()
        hs = P.sbuf("hs", [128, NT, 768], BF16)
        hd = P.sbuf("hd", [128, NT, 768], BF16)
        d_hsd = Dep("hsd", multi=True)
        P.push()
        zT = P.sbuf("zT", [17, S], F32)
        wf1 = P.sbuf("wf1", [17, 64], F32)
        wf2 = P.sbuf("wf2", [64, 64], F32)
        wf3 = P.sbuf("wf3", [64, 1536], F32)
        d_fw = Dep("fw", multi=True)
        P.dma("sp", zT[:], zT_in, writes=[d_fw])
        P.dma("sp", wf1[:], w_f1[l], writes=[d_fw])
        P.dma("sp", wf2[:], w_f2[l], writes=[d_fw])
        P.dma("sp", wf3[:], w_f3[l], writes=[d_fw])
        hid1 = P.sbuf("hid1", [64, S], F32)
        hid2 = P.sbuf("hid2", [64, S], F32)
        d_h1 = Dep("hid1", multi=True)
        d_h2 = Dep("hid2", multi=True)
        fa_r = Ring(P, "fa", [64, TB], F32, 2)
        ft_r = Ring(P, "ft", [64, TB], F32, 2)

        def sin_block(bk, d_bk, bcol, fcol, out, d_o):
            a, d_a = fa_r.next()
            t, d_t = ft_r.next()
            P.ts("dve", a[:, :], bk[0:64, :], vl[0:64, bcol:bcol + 1], vl[0:64, fcol:fcol + 1], ALU.add, ALU.mult,
                 [d_bk, d_vec], [d_a])
            P.ts("dve", t[:, :], a[:, :], 1.0 / (2.0 * math.pi), MAGIC, ALU.mult, ALU.add, [d_a], [d_t])
            P.ts("dve", t[:, :], t[:, :], MAGIC, -2.0 * math.pi, ALU.subtract, ALU.mult, [d_t], [d_t])
            P.tt("dve", a[:, :], a[:, :], t[:, :], ALU.add, [d_a, d_t], [d_a])
            P.ts("dve", a[:, :], a[:, :], -PI_SAFE, PI_SAFE, ALU.max, ALU.min, [d_a], [d_a])
            P.act(out, a[:, :], AF.Sin, [d_a], [d_o])

        for b in range(NB):
            sl = slice(b * TB, (b + 1) * TB)
            bk, d_bk = banks.next()
            P.mm(bk[0:64, :], wf1[:, :], zT[:, sl], True, True, [d_fw], [d_bk])
            sin_block(bk, d_bk, 71, 72, hid1[:, sl], d_h1)
        for b in range(NB):
            sl = slice(b * TB, (b + 1) * TB)
            bk, d_bk = banks.next()
            P.mm(bk[0:64, :], wf2[:, :], hid1[:, sl], True, True, [d_fw, d_h1], [d_bk])
            sin_block(bk, d_bk, 73, 74, hid2[:, sl], d_h2)
        hraw_r = Ring(P, "hraw", [128, 1536], F32, 2)
        dk_r = Ring(P, "dk", [128, 384], F32, 2)
        fs_r = Ring(P, "fs", [128, 2, 384], F32, 2)
        fd_r = Ring(P, "fd", [128, 2, 384], F32, 2)
        for i in range(NT):
            hraw, d_hr = hraw_r.next()
            for n in range(3):
                bk, d_bk = banks.next()
                P.mm(bk, tcols(hid2[:, :], i), wf3[:, n * 512:(n + 1) * 512], True, True, [d_fw, d_h2], [d_bk])
                P.cp("act", hraw[:, n * 512:(n + 1) * 512], bk, [d_bk], [d_hr])
            dk, d_dk = dk_r.next()
            P.dma("sp", dk[:], trows(decay_in, i), writes=[d_dk])
            fs, d_fs = fs_r.next()
            fd, d_fd = fd_r.next()
            hv = hraw[:, :].rearrange("p (o r c) -> p o r c", o=2, r=2)
            P.tt("dve", fs[:, :, :], hv[:, :, 0, :], hv[:, :, 1, :], ALU.add, [d_hr], [d_fs])
            P.tt("dve", fd[:, :, :], hv[:, :, 0, :], hv[:, :, 1, :], ALU.subtract, [d_hr], [d_fd])
            for o in range(2):
                P.tt("pool", hs[:, i, o * 384:(o + 1) * 384], fs[:, o, :], dk[:, :], ALU.mult, [d_fs, d_dk], [d_hsd])
                P.tt("pool", hd[:, i, o * 384:(o + 1) * 384], fd[:, o, :], dk[:, :], ALU.mult, [d_fd, d_dk], [d_hsd])
        P.pop()
        if stop(f"HF{l}"):
            dbg_h = nc.dram_tensor("dbg_hs", [NT, 128, 768], F32, kind="ExternalOutput").ap()
            dbg_h2 = nc.dram_tensor("dbg_hd", [NT, 128, 768], F32, kind="ExternalOutput").ap()
            d_dbg = Dep("dbg", multi=True)
            P.store(dbg_h.rearrange("t p c -> p t c"), hs[:], reads=[d_hsd], writes=[d_dbg])
            P.store(dbg_h2.rearrange("t p c -> p t c"), hd[:], reads=[d_hsd], writes=[d_dbg])
            return finish()
        P.push()
        skb = P.sbuf("skb", [128, 768], F32)
        d_skb = Dep("skb")
        P.dma("sp", skb[:], skipb[l], writes=[d_skb])
        Cb_r = Ring(P, "CbF", [128, 4096], BF16, 1)
        Sb_r = Ring(P, "SbF", [128, 4096], BF16, 1)
        gst_r = Ring(P, "gst", [128, 4, 768], F32, 1)
        gtmp_r = Ring(P, "gtmp", [128, 384], F32, 2)
        fb = FoldBufs()

        def a2_gen():
            for kc in range(NKF):
                Cb, d_C = Cb_r.next()
                Sb, d_S = Sb_r.next()
                P.dma("sp", Cb[:], CfF[kc], writes=[d_C])
                P.dma("sp", Sb[:], SfF[kc], writes=[d_S])
                for o in range(2):
                    gst, d_gst = gst_r.next()
                    oa = yield from fold_forward_g(fb, lambda tau, o=o: hs[:, tau, o * 384:(o + 1) * 384], d_hsd,
                                                   Cb, Sb, d_C, d_S, "A", bases=(4, 4), cp_eng="dve")
                    for j in range(4):
                        gt_, d_gt_ = gtmp_r.next()
                        aj, d_aj = oa[f"A{j + 1}"]
                        P.tt("pool", gt_[:, :], aj[:, :], skb[:, o * 384:(o + 1) * 384], ALU.add, [d_aj, d_skb],
                             [d_gt_])
                        P.ts("pool", gst[:, j, 0:384], gt_[:, :], wkt[:, kc * 4 + j:kc * 4 + j + 1], None, ALU.mult,
                             None, [d_gt_, d_vec], [d_gst])
                    ob = yield from fold_forward_g(fb, lambda tau, o=o: hd[:, tau, o * 384:(o + 1) * 384], d_hsd,
                                                   Cb, Sb, d_C, d_S, "B", bases=(4, 4), cp_eng="dve")
                    for j in range(4):
                        bj, d_bj = ob[f"B{j + 1}"]
                        P.ts("pool", gst[:, j, 384:768], bj[:, :],
                             wkt[:, 4 * NKF + kc * 4 + j:4 * NKF + kc * 4 + j + 1], None, ALU.mult, None,
                             [d_bj, d_vec], [d_gst])
                    P.dma("sp", Gs[o, kc], gst[:].rearrange("p j c -> p (j c)"), reads=[d_gst], writes=[d_Gs],
                          slot=d_gst)

        a2g = a2_gen()

        def a2_step():
            try:
                next(a2g)
            except StopIteration:
                pass

        Vh_r = Ring(P, "Vh", [128, NT, 65], BF16, 2)
        KTh_r = Ring(P, "KTh", [96, S], BF16, 1)
        QTh_r = Ring(P, "QTh", [96, S], BF16, 1)
        pT_r = Ring(P, "pT", [128, TB], BF16, 6)
        rd_r = Ring(P, "rdM", [128, 4], F32, 2)
        yas_r = Ring(P, "yas", [128, 4, 64], F32, 2)
        sc_m = 1.0 / math.sqrt(96.0)
        s_rot = 0
        for h in range(6):
            KTh, d_K = KTh_r.next()
            QTh, d_Q = QTh_r.next()
            Vh, d_V = Vh_r.next()
            P.dma("sp", KTh[:], KT[h], reads=[d_KT], writes=[d_K])
            P.dma("sp", QTh[:], QT[h], reads=[d_QT], writes=[d_Q])
            P.dma("sp", Vh[:], Vm[h], reads=[d_Vm], writes=[d_V])
            for qb in range(NB):
                po, d_po = banks.get(3)
                pts = []
                for step in range(NT + 2):
                    if step < NT:
                        kt = step
                        sb, d_sb = banks.get(s_rot % 3)
                        s_rot += 1
                        P.mm(sb, KTh[:, kt * 128:(kt + 1) * 128], QTh[:, qb * TB:(qb + 1) * TB], True, True,
                             [d_K, d_Q], [d_sb])
                        pT, d_pT = pT_r.next()
                        P.act(pT[:, :], sb, AF.Exp, [d_sb], [d_pT], scale=sc_m)
                        pts.append((pT, d_pT))
                        a2_step()
                    if step >= 2:
                        kt = step - 2
                        pT, d_pT = pts[kt]
                        for j in range(4):
                            P.mm(po[:, j * 65:(j + 1) * 65], pT[:, j * 128:(j + 1) * 128],
                                 Vh[:, kt, :], kt == 0 and j == 0, kt == NT - 1 and j == 3,
                                 [d_pT, d_V], [d_po], skip=True)
                rdt, d_rd = rd_r.next()
                P.recip(rdt[:, 0:4], po[:, 0:260].rearrange("p (j c) -> p j c", c=65)[:, :, 64], [d_po], [d_rd])
                yas, d_yas = yas_r.next()
                for j in range(4):
                    P.ts("dve", yas[:, j, :], po[:, j * 65:j * 65 + 64], rdt[:, j:j + 1],
                         None, ALU.mult, None, [d_po, d_rd], [d_yas])
                P.dma("sp", ymix[qb * TB:(qb + 1) * TB, h * 64:(h + 1) * 64].rearrange("(t p) c -> p t c", p=128),
                      yas[:], reads=[d_yas], writes=[d_ymix], slot=d_yas)
        for _ in a2g:
            pass
        P.pop()
        P.pop()
        if stop(f"M{l}"):
            return finish()
        P.push()
        nq = P.sbuf("nq", [128, 2, S], BF16)
        nk = P.sbuf("nk", [128, 2, S], BF16)
        nv = P.sbuf("nv", [128, NT, 260], BF16)
        nbt = P.sbuf("nbt", [128, 5, 2560], F32)
        yc = P.sbuf("yc", [128, NT, 256], F32)
        d_nin = Dep("nin", multi=True)
        d_yc = Dep("yc", multi=True)
        P.dma("sp", nq[:], naQT.rearrange("c p s -> p c s"), reads=[d_naQT], writes=[d_nin])
        P.dma("sp", nk[:], naKT.rearrange("c p s -> p c s"), reads=[d_naKT], writes=[d_nin])
        P.dma("sp", nv[:], naV.rearrange("t p c -> p t c"), reads=[d_naV], writes=[d_nin])
        P.dma("sp", nbt[:], nabias[l].rearrange("t p c -> p t c"), writes=[d_nin])
        tS_r = Ring(P, "tS", [128, 640], F32, 3)
        pN_r = Ring(P, "pN", [128, 640], BF16, 3)
        rdn_r = Ring(P, "rdN", [128, 4], F32, 2)
        items = [(m, h) for m in range(NT) for h in range(4)]
        pend = None
        for idx in range(len(items) + 1):
            if idx < len(items):
                m, h = items[idx]
                ty = {0: 0, 1: 1, 30: 2, 31: 3}.get(m, 4)
                c0 = min(max(m - 2, 0), 27)
                j, pb = h // 2, 64 * (h % 2)
                bi = 2 * (idx % 3)
                s2 = banks.ps[:, bi * 512:(bi + 2) * 512]
                d_a, d_b = banks.d[bi], banks.d[bi + 1]
                for i in range(5):
                    P.mm(s2[:, i * 128:(i + 1) * 128], nk[pb:pb + 64, j, (c0 + i) * 128:(c0 + i + 1) * 128],
                         nq[pb:pb + 64, j, m * 128:(m + 1) * 128], True, True, [d_nin], [d_a if i < 4 else d_b])
                tS, d_tS = tS_r.next()
                P.stt(tS[:, :], s2[:, 0:640], 0.125, nbt[:, ty, h * 640:(h + 1) * 640], ALU.mult, ALU.add,
                      [d_a, d_b, d_nin], [d_tS])
                pN, d_pN = pN_r.next()
                P.act(pN[:, :], tS[:, :], AF.Exp, [d_tS], [d_pN])
                cur = (m, h, c0, pN, d_pN)
            if pend is not None:
                m, h, c0, pN, d_pN = pend
                po, d_po = banks.get(6 + m % 2)
                for i in range(5):
                    P.mm(po[:, h * 65:(h + 1) * 65], pN[:, i * 128:(i + 1) * 128], nv[:, c0 + i, h * 65:(h + 1) * 65],
                         h == 0 and i == 0, h == 3 and i == 4, [d_pN, d_nin], [d_po], skip=True)
                if h == 3:
                    rdt, d_rd = rdn_r.next()
                    P.recip(rdt[:, 0:4], po[:, 0:260].rearrange("p (j c) -> p j c", c=65)[:, :, 64], [d_po], [d_rd])
                    for hh in range(4):
                        P.ts("dve", yc[:, m, hh * 64:(hh + 1) * 64], po[:, hh * 65:hh * 65 + 64], rdt[:, hh:hh + 1],
                             None, ALU.mult, None, [d_po, d_rd], [d_yc])
            pend = cur if idx < len(items) else None
        for q4 in range(4):
            P.store(ymix[q4 * 1024:(q4 + 1) * 1024, 768:1024].rearrange("(t p) c -> p t c", p=128),
                  yc[:, q4 * 8:(q4 + 1) * 8, :], reads=[d_yc], writes=[d_ymix])
        P.pop()
        if stop(f"N{l}"):
            return finish()


        P.push()
        u_tok = P.sbuf("u_tok", [128, NT, 384], BF16)
        d_u = Dep("u_tok", multi=True)
        P.push()
        hyc_r = Ring(P, "hyc", [128, S + 2], F32, 2)
        ucT_r = Ring(P, "ucT", [128, S], F32, 2)
        xst_r = Ring(P, "xst", [128, NT, 128], F32, 2)
        for t_, d_ in zip(hyc_r.t, hyc_r.d):
            P.memset("pool", t_[:, 0:1], 0.0, [d_])
            P.memset("pool", t_[:, S + 1:S + 2], 0.0, [d_])
        for c in range(9):
            hyc, d_hyc = hyc_r.next()
            P.dma("sp", hyc[:, 1:S + 1], hyT[c], reads=[d_hyT], writes=[d_hyc])
            P.flush()
            ucT, d_uc = ucT_r.next()
            w0 = vl[:, 35 + c:36 + c]
            w1 = vl[:, 44 + c:45 + c]
            w2 = vl[:, 53 + c:54 + c]
            bb = vl[:, 62 + c:63 + c]
            P.ts("dve", ucT[:, :], hyc[:, 1:S + 1], w1, bb, ALU.mult, ALU.add, [d_hyc, d_vec], [d_uc])
            P.stt(ucT[:, :], hyc[:, 0:S], w0, ucT[:, :], ALU.mult, ALU.add, [d_hyc, d_uc, d_vec], [d_uc])
            P.stt(ucT[:, :], hyc[:, 2:S + 2], w2, ucT[:, :], ALU.mult, ALU.add, [d_hyc, d_uc, d_vec], [d_uc])
            if c >= 3:
                xst, d_xst = xst_r.next()
            for g4 in range(8):
                bk, d_bk = banks.next()
                for q in range(4):
                    ti = g4 * 4 + q
                    P.tr(bk[:, q * 128:(q + 1) * 128], tcols(ucT[:, :], ti), ident[:, :],
                         [d_uc, d_ident], [d_bk])
                bv = bk.rearrange("p (a b) -> p a b", a=4)
                if c < 3:
                    P.cp("act" if g4 % 2 else "dve", u_tok[:, g4 * 4:(g4 + 1) * 4, c * 128:(c + 1) * 128], bv,
                         [d_bk], [d_u])
                else:
                    P.cp("act" if g4 % 2 else "dve", xst[:, g4 * 4:(g4 + 1) * 4, :], bv, [d_bk], [d_xst])
            if c >= 3:
                o, cc = (c - 3) // 3, (c - 3) % 3
                for q4 in range(4):
                    P.store(xg[o, q4 * 1024:(q4 + 1) * 1024, cc * 128:(cc + 1) * 128].rearrange(
                        "(t p) c -> p t c", p=128), xst[:, q4 * 8:(q4 + 1) * 8, :], reads=[d_xst], writes=[d_xg],
                          slot=d_xst)
        P.pop()
        if stop(f"HP{l}"):
            dbg_u = nc.dram_tensor("dbg_u", [NT, 128, 384], F32, kind="ExternalOutput").ap()
            d_dbg = Dep("dbg", multi=True)
            P.store(dbg_u.rearrange("t p c -> p t c"), u_tok[:], reads=[d_u], writes=[d_dbg])
            return finish()

        Ec = P.sbuf("Ec", [128, 4, NKF, 384], BF16)
        Es = P.sbuf("Es", [128, 4, NKF, 384], BF16)
        d_E = Dep("EcEs", multi=True)
        Cb_r = Ring(P, "CbF", [128, 4096], BF16, 2)
        Sb_r = Ring(P, "SbF", [128, 4096], BF16, 2)
        Ci_r = Ring(P, "CbI", [128, NKF * 128], BF16, 2)
        Si_r = Ring(P, "SbI", [128, NKF * 128], BF16, 2)
        gb_r = Ring(P, "gb", [128, 4, 768], F32, 1)
        fb = FoldBufs()
        tm_r = [Ring(P, f"cv{i}", [128, 384], F32, 1) for i in range(4)]
        tm2_r = [Ring(P, f"cw{i}", [128, 384], F32, 1) for i in range(4)]
        Y_r = {n: Ring(P, "Y" + n, [128, 384], F32, 1) for n in ("r1", "r2", "r3", "r4", "n1", "n2", "n3", "n4")}
        I_r = {n: Ring(P, "I" + n, [128, 384], F32, 1) for n in ("U", "V", "Up", "Vp", "W", "X", "Wp", "Xp")}
        gt_r = Ring(P, "gate", [128, 384], F32, 2)
        yo_r = Ring(P, "yo", [128, 384], F32, 2)
        for o in range(2):
            for kc in range(NKF):
                Cb, d_C = Cb_r.next()
                Sb, d_S = Sb_r.next()
                P.dma("sp", Cb[:], CfF[kc], writes=[d_C])
                P.dma("sp", Sb[:], SfF[kc], writes=[d_S])
                gb, d_gb = gb_r.next()
                P.dma("sp", gb[:].rearrange("p j c -> p (j c)"), Gs[o, kc], reads=[d_Gs], writes=[d_gb])
                ejs = {"1": "dve", "2": "pool", "3": "pool", "4": "dve"}
                ab = fold_forward(fb, lambda tau: u_tok[:, tau, :], d_u, Cb, Sb, d_C, d_S, "AB", l2eng=ejs)
                Y = {}
                for j in range(4):
                    aj, d_aj = ab[f"A{j + 1}"]
                    bj, d_bj = ab[f"B{j + 1}"]
                    ej = ejs[str(j + 1)]
                    t1, t2, t3, t4 = [r.next() for r in (tm_r if ej == "dve" else tm2_r)]
                    P.tt(ej, t1[0][:, :], aj[:, :], gb[:, j, 0:384], ALU.mult, [d_aj, d_gb], [t1[1]])
                    P.tt(ej, t2[0][:, :], bj[:, :], gb[:, j, 384:768], ALU.mult, [d_bj, d_gb], [t2[1]])
                    P.tt(ej, t3[0][:, :], bj[:, :], gb[:, j, 0:384], ALU.mult, [d_bj, d_gb], [t3[1]])
                    P.tt(ej, t4[0][:, :], aj[:, :], gb[:, j, 384:768], ALU.mult, [d_aj, d_gb], [t4[1]])
                    yr, d_yr = Y_r[f"r{j + 1}"].next()
                    P.tt(ej, yr[:, :], t1[0][:, :], t2[0][:, :], ALU.add, [t1[1], t2[1]], [d_yr])
                    yn, d_yn = Y_r[f"n{j + 1}"].next()
                    P.tt(ej, yn[:, :], t3[0][:, :], t4[0][:, :], ALU.subtract, [t3[1], t4[1]], [d_yn])
                    Y[f"r{j + 1}"] = (yr, d_yr)
                    Y[f"n{j + 1}"] = (yn, d_yn)
                I = {}

                def i1(nm, x, y, op):
                    t_, d_t = I_r[nm].next()
                    P.tt(ejs[x[1]], t_[:, :], Y[x][0][:, :], Y[y][0][:, :], op, [Y[x][1], Y[y][1]], [d_t])
                    I[nm] = (t_, d_t)

                i1("U", "r2", "r3", ALU.add)
                i1("V", "r2", "r3", ALU.subtract)
                i1("Up", "n2", "n3", ALU.add)
                i1("Vp", "n2", "n3", ALU.subtract)
                i1("W", "r1", "r4", ALU.add)
                i1("X", "r1", "r4", ALU.subtract)
                i1("Wp", "n1", "n4", ALU.subtract)
                i1("Xp", "n1", "n4", ALU.add)

                def i2(dst, r, x, y, op):
                    P.tt("dve" if dst is Ec else "pool", dst[:, r, kc, :], I[x][0][:, :], I[y][0][:, :], op,
                         [I[x][1], I[y][1]], [d_E])

                i2(Ec, 0, "W", "U", ALU.add)
                i2(Ec, 1, "X", "Up", ALU.add)
                i2(Ec, 2, "W", "U", ALU.subtract)
                i2(Ec, 3, "X", "Up", ALU.subtract)
                i2(Es, 0, "Wp", "Vp", ALU.subtract)
                i2(Es, 1, "Xp", "V", ALU.add)
                i2(Es, 2, "Wp", "Vp", ALU.add)
                i2(Es, 3, "Xp", "V", ALU.subtract)
            for tau in range(NT):
                r = tau // 8
                Ci, d_Ci = Ci_r.next()
                Si, d_Si = Si_r.next()
                P.dma("sp", Ci[:], CfI[tau], writes=[d_Ci])
                P.dma("sp", Si[:], SfI[tau], writes=[d_Si])
                gt, d_gt = gt_r.next()
                P.dma("sp", gt[:], xg[o, tau * 128:(tau + 1) * 128, :], reads=[d_xg], writes=[d_gt])
                P.flush()
                by, d_by = banks.get(tau % 2)
                for kc in range(NKF):
                    P.mm(by[:, 0:384], Ci[:, kc * 128:(kc + 1) * 128], Ec[:, r, kc, :], kc == 0, False, [d_Ci, d_E], [d_by])
                    P.mm(by[:, 0:384], Si[:, kc * 128:(kc + 1) * 128], Es[:, r, kc, :], False, kc == NKF - 1,
                         [d_Si, d_E], [d_by])
                if o == 0:
                    P.tt("dve", u_tok[:, tau, :], by[:, 0:384], gt[:, :], ALU.mult, [d_by, d_gt], [d_u])
                else:
                    yo, d_yo = yo_r.next()
                    P.tt("dve", yo[:, :], by[:, 0:384], gt[:, :], ALU.mult, [d_by, d_gt], [d_yo])
                    P.store(trows(ymix, tau)[:, 384:768], yo[:, :], reads=[d_yo], writes=[d_ymix], slot=d_yo)
            if o == 0 and stop(f"HC{l}"):
                dbg_u = nc.dram_tensor("dbg_u", [NT, 128, 384], F32, kind="ExternalOutput").ap()
                d_dbg = Dep("dbg", multi=True)
                P.store(dbg_u.rearrange("t p c -> p t c"), u_tok[:], reads=[d_u], writes=[d_dbg])
                return finish()
        P.pop()
        if stop(f"H{l}"):
            return finish()

        P.push()
        wg = P.sbuf("wg", [128, 8, DFF], BF16)
        wu = P.sbuf("wu", [128, 8, DFF], BF16)
        d_wg = Dep("wg")
        d_wu = Dep("wu")
        P.push()
        wo = P.sbuf("wo", [128, 8, D], BF16)
        d_wo = Dep("wo")
        P.dma("pool", wo[:], w_out[l].rearrange("(k p) c -> p k c", p=128), writes=[d_wo])
        P.dma("pool", wg[:], w_gate[l].rearrange("(k p) c -> p k c", p=128), writes=[d_wg])
        P.dma("pool", wu[:], w_up[l].rearrange("(k p) c -> p k c", p=128), writes=[d_wu])
        ym_r = Ring(P, "ym", [128, 4, D], F32, 2)
        xtb_r = Ring(P, "xtbD", [128, 8, TB], F32, 2)
        yn_r = Ring(P, "yn", [128, D], F32, 2)
        yT_r = Ring(P, "yT", [128, 8, TB], BF16, 2)
        ssq_r = Ring(P, "ssq", [128, 12], F32, 2)
        rsd_r = Ring(P, "rsd", [128, 12], F32, 2)
        junk = P.sbuf("junk", [128, 384], BF16)
        d_junk = Dep("junk", multi=True)
        groups = [(0, 384), (384, 768), (768, 1024)]
        for b in range(NB):
            ym, d_ym = ym_r.next()
            P.dma("sp", ym[:], ymix[b * TB:(b + 1) * TB, :].rearrange("(t p) c -> p t c", p=128), reads=[d_ymix],
                  writes=[d_ym])
            xtb, d_xtb = xtb_r.next()
            P.dma("sp", xtb[:], xT[:, :, b * TB:(b + 1) * TB].rearrange("k p s -> p k s"), reads=[d_xT[b]],
                  writes=[d_xtb])
            P.flush()
            ssq, d_ssq = ssq_r.next()
            rsd, d_rsd = rsd_r.next()
            for i in range(4):
                for gi, (c0, c1) in enumerate(groups):
                    P.act(junk[:, 0:c1 - c0], ym[:, i, c0:c1], AF.Square, [d_ym], [d_junk, d_ssq],
                          accum=ssq[:, i * 3 + gi:i * 3 + gi + 1])
            sv = ssq[:, :].rearrange("p (i g) -> p i g", g=3)
            rv = rsd[:, :].rearrange("p (i g) -> p i g", g=3)
            for gi, (c0, c1) in enumerate(groups):
                P.act(rv[:, :, gi], sv[:, :, gi], AF.Sqrt, [d_ssq, d_const], [d_rsd], scale=1.0 / (c1 - c0),
                      bias=epsc[:, 0:1])
            P.recip(rsd[:, :], rsd[:, :], [d_rsd], [d_rsd])
            yT, d_yT = yT_r.next()
            for i in range(4):
                yn, d_yn = yn_r.next()
                for gi, (c0, c1) in enumerate(groups):
                    P.ts("dve" if gi < 2 else "pool", yn[:, c0:c1], ym[:, i, c0:c1], rsd[:, i * 3 + gi:i * 3 + gi + 1],
                         None, ALU.mult, None, [d_ym, d_rsd], [d_yn])
                for j in range(2):
                    bk, d_bk = banks.next()
                    for c in range(4):
                        k = j * 4 + c
                        P.tr(bk[:, c * 128:(c + 1) * 128], yn[:, k * 128:(k + 1) * 128], ident[:, :],
                             [d_yn, d_ident], [d_bk])
                    for c in range(4):
                        k = j * 4 + c
                        if c % 2:
                            P.act(yT[:, k, i * 128:(i + 1) * 128], bk[:, c * 128:(c + 1) * 128], AF.Copy,
                                  [d_bk, d_vec], [d_yT], scale=vl[:, 19 + k:20 + k])
                        else:
                            P.ts("dve", yT[:, k, i * 128:(i + 1) * 128], bk[:, c * 128:(c + 1) * 128],
                                 vl[:, 19 + k:20 + k], None, ALU.mult, None, [d_bk, d_vec], [d_yT])
            for mch in range(8):
                bk, d_bk = banks.next()
                for k in range(8):
                    P.mm(bk, wo[:, k, mch * 128:(mch + 1) * 128], yT[:, k, :], k == 0, k == 7, [d_wo, d_yT], [d_bk])
                P.tt("dve", xtb[:, mch, :], bk, xtb[:, mch, :], ALU.add, [d_bk, d_xtb], [d_xtb])
            P.store(xT[:, :, b * TB:(b + 1) * TB].rearrange("k p s -> p k s"), xtb[:], reads=[d_xtb],
                  writes=[d_xT[b]])
        P.pop()
        if stop(f"D1{l}"):
            return finish()

        wd = P.sbuf("wd", [128, 22, D], BF16)
        d_wd = Dep("wd")
        P.dma("pool", wd[:], w_down[l].rearrange("(k p) c -> p k c", p=128), writes=[d_wd])
        xtb = P.sbuf("xtbF", [128, 8, TB], F32)
        d_xtb = Dep("xtbF")
        h2T = P.sbuf("h2T", [128, 8, TB], BF16)
        d_h2 = Dep("h2T")
        actT = P.sbuf("actT", [128, 22, TB], BF16)
        d_act = Dep("actT")
        rs = P.sbuf("rsF", [128, TB], F32)
        d_rs = Dep("rsF")
        sg_r = Ring(P, "sg", [128, TB], F32, 2)
        last = (l == NL - 1)
        if last:
            ot = P.sbuf("ot", [128, 4, D], F32)
            d_ot = Dep("ot")
        for b in range(NB):
            P.dma("sp", xtb[:], xT[:, :, b * TB:(b + 1) * TB].rearrange("k p s -> p k s"), reads=[d_xT[b]],
                  writes=[d_xtb])
            rms_feature_major(xtb, d_xtb, 8, 8, vl, h2T, d_h2, actT, d_act, rs, d_rs, D)
            for f in range(22):
                bg, d_bg = banks.next()
                bu, d_bu = banks.next()
                for k in range(8):
                    P.mm(bg, wg[:, k, f * 128:(f + 1) * 128], h2T[:, k, :], k == 0, k == 7, [d_wg, d_h2], [d_bg])
                for k in range(8):
                    P.mm(bu, wu[:, k, f * 128:(f + 1) * 128], h2T[:, k, :], k == 0, k == 7, [d_wu, d_h2], [d_bu])
                sg, d_sg = sg_r.next()
                P.act(sg[:, :], bg, AF.Silu, [d_bg], [d_sg])
                P.tt("dve", actT[:, f, :], bu, sg[:, :], ALU.mult, [d_bu, d_sg], [d_act])
            for mch in range(8):
                bk, d_bk = banks.next()
                for f in range(22):
                    P.mm(bk, wd[:, f, mch * 128:(mch + 1) * 128], actT[:, f, :], f == 0, f == 21, [d_wd, d_act], [d_bk])
                P.tt("dve", xtb[:, mch, :], bk, xtb[:, mch, :], ALU.add, [d_bk, d_xtb], [d_xtb])
            if not last:
                P.store(xT[:, :, b * TB:(b + 1) * TB].rearrange("k p s -> p k s"), xtb[:], reads=[d_xtb],
                      writes=[d_xT[b]])
                P.flush()
            else:
                rms_feature_major(xtb, d_xtb, 8, 27, vl, xtb, d_xtb, actT, d_act, rs, d_rs, D)
                for i in range(4):
                    for j in range(2):
                        bk, d_bk = banks.next()
                        for c in range(4):
                            k = j * 4 + c
                            P.tr(bk[:, c * 128:(c + 1) * 128], xtb[:, k, i * 128:(i + 1) * 128], ident[:, :],
                                 [d_xtb, d_ident], [d_bk])
                        P.cp("act" if j else "dve", ot[:, i, j * 512:(j + 1) * 512], bk, [d_bk], [d_ot])
                P.store(out_ap[b * TB:(b + 1) * TB, :].rearrange("(t p) c -> p t c", p=128), ot[:], reads=[d_ot],
                      writes=[d_out])
                P.flush()
        P.pop()
        if stop(f"D2{l}"):
            return finish()

    return finish()


_CONST = {}


def host_constants():
    if _CONST:
        return _CONST
    f32 = np.float32
    c = _CONST
    c["ident"] = np.eye(128, dtype=f32)
    pos = np.arange(S, dtype=f32)
    inv = (np.float32(10000.0) ** (-np.arange(0, 32, 2, dtype=f32) / np.float32(32))).astype(f32)
    ang = (pos[:, None] * inv[None, :]).astype(f32)
    cos, sin = np.cos(ang).astype(f32), np.sin(ang).astype(f32)
    c["ropeC"] = np.ascontiguousarray(np.concatenate([cos, cos], 1).T)
    c["ropeS"] = np.ascontiguousarray(np.concatenate([-sin, sin], 1).T)
    t_idx = np.arange(S, dtype=f32)[:, None]
    t_norm = np.linspace(0.0, 1.0, S, dtype=f32)[:, None]
    bands = np.linspace(1e-4, 7, 8, dtype=f32)[None, :]
    angz = (np.float32(2.0 * math.pi) * t_idx * bands / np.float32(S)).astype(f32)
    z = np.concatenate([t_norm, np.cos(angz), np.sin(angz)], -1).astype(f32)
    c["zT"] = np.ascontiguousarray(z.T)
    deltas = np.linspace(math.log(1e-2) / 1.5, math.log(1e-2) / 0.3, 384, dtype=f32)
    c["decay"] = np.exp(-t_norm * np.abs(deltas)[None, :]).astype(f32)
    pp = np.arange(128, dtype=np.int64)
    tt = (512 * np.arange(8)[None, None, :] + 4 * pp[:, None, None] + np.arange(4)[None, :, None])
    kk = (128 * np.arange(NKF)[:, None] + pp[None, :])
    prod = (tt[None, :, :, :, None] * kk[:, None, None, None, :]) % 8192
    th = prod.astype(np.float64) * (2.0 * math.pi / 8192.0)
    c["CfF"] = np.cos(th).astype(f32).astype(ml_dtypes.bfloat16).reshape(NKF, 128, 4096)
    c["SfF"] = np.sin(th).astype(f32).astype(ml_dtypes.bfloat16).reshape(NKF, 128, 4096)
    tau = np.arange(NT)
    t2 = (512 * (tau % 8)[:, None] + 4 * pp[None, :] + (tau // 8)[:, None])
    kq = (128 * np.arange(NKF)[None, :] + pp[:, None])
    prod = (t2[:, None, None, :] * kq[None, :, :, None]) % 8192
    th = prod.astype(np.float64) * (2.0 * math.pi / 8192.0)
    c["CfI"] = np.cos(th).astype(f32).astype(ml_dtypes.bfloat16).reshape(NT, 128, NKF * 128)
    c["SfI"] = np.sin(th).astype(f32).astype(ml_dtypes.bfloat16).reshape(NT, 128, NKF * 128)
    wk = np.zeros((128, 8 * NKF), f32)
    for kc in range(NKF):
        for p in range(128):
            k = kc * 128 + p
            if k > 1024:
                continue
            orbit = [k, 2048 - k, 2048 + k, 4096 - k]
            w = [(1.0 if kp in (0, 4096) else 2.0) / 8192.0 for kp in orbit]
            if k == 0:
                w[2] = 0.0
            if k == 1024:
                w[1] = 0.0
                w[3] = 0.0
            for j in range(4):
                wk[p, kc * 4 + j] = w[j]
                wk[p, 4 * NKF + kc * 4 + j] = -w[j]
    c["wk"] = wk
    return c


def host_layout(inputs):
    f32 = np.float32
    g = {k: np.asarray(v) for k, v in inputs.items()}
    w_in = g["w_in"]
    zpad = np.zeros((NL, D, 64), f32)
    kpe = w_in[:, :, 384:416]
    kpe_sw = np.concatenate([kpe[:, :, 16:32], kpe[:, :, 0:16]], -1)
    w_inA = np.concatenate([w_in[:, :, 0:384], zpad, kpe, zpad, kpe_sw, w_in[:, :, 416:2336]], -1)
    assert w_inA.shape[-1] == WIN_COLS
    wuq = g["mla_w_uq"].reshape(NL, 256, 6, 96)
    zq = np.zeros((NL, 256, 6, 64), f32)
    w_uq2 = np.concatenate([wuq, zq, wuq[..., 80:96], wuq[..., 64:80]], -1).reshape(NL, 256, 1152)
    wkv = g["mla_w_ukv"].reshape(NL, 128, 6, 128)
    w_kv2 = np.concatenate([wkv[..., 0:64].reshape(NL, 128, 384), wkv[..., 64:128].reshape(NL, 128, 384)], -1)
    vecs = np.zeros((NL, 128, NV), f32)
    for l in range(NL):
        vecs[l, :, 0:8] = g["norm1_g"][l].reshape(8, 128).T
        vecs[l, :, 8:16] = g["norm2_g"][l].reshape(8, 128).T
        vecs[l, :, 16:18] = g["mla_q_norm_g"][l].reshape(2, 128).T
        vecs[l, :, 18:19] = g["mla_kv_norm_g"][l].reshape(1, 128).T
        vecs[l, :, 19:27] = g["mix_norm_g"][l].reshape(8, 128).T
        vecs[l, :, 27:35] = g["final_norm_g"].reshape(8, 128).T
        for j in range(3):
            vecs[l, :, 35 + j * 9:35 + (j + 1) * 9] = g["hy_conv_w"][l, j].reshape(9, 128).T
        vecs[l, :, 62:71] = g["hy_conv_b"][l].reshape(9, 128).T
        vecs[l, 0:64, 71] = g["hy_filt_b1"][l]
        vecs[l, 0:64, 72] = g["hy_filt_freq1"][l]
        vecs[l, 0:64, 73] = g["hy_filt_b2"][l]
        vecs[l, 0:64, 74] = g["hy_filt_freq2"][l]
    skipb = np.broadcast_to(g["hy_skip"].reshape(NL, 1, 768), (NL, 128, 768))
    rpb = g["na_rpb"]
    nab = np.full((NL, 5, 128, 4, 5, 128), -30000.0, f32)
    types = [(0, 0), (1, 0), (30, 27), (31, 27), (2, 0)]
    qf = np.arange(128)
    for ti, (m, c0) in enumerate(types):
        rq = 2 * m + qf // 64
        wq = qf % 64
        r0 = np.clip(rq - 4, 0, 56)
        cc0 = np.clip(wq - 8, 0, 48)
        for i in range(5):
            kt = (c0 + i) * 128 + np.arange(128)
            rk = kt // 64
            wkk = kt % 64
            inwin = ((rk[:, None] >= r0[None, :]) & (rk[:, None] < r0[None, :] + 8) &
                     (wkk[:, None] >= cc0[None, :]) & (wkk[:, None] < cc0[None, :] + 16))
            dr = np.clip(rk[:, None] - rq[None, :] + 7, 0, 14)
            dc = np.clip(wkk[:, None] - wq[None, :] + 15, 0, 30)
            for l in range(NL):
                for h in range(4):
                    vals = rpb[l, h][dr, dc]
                    nab[l, ti, :, h, i, :] = np.where(inwin, vals, f32(-30000.0))
    nabias = nab.reshape(NL, 5, 128, 2560)
    shared = dict(w_inA=w_inA, w_uq2=w_uq2, w_kv2=w_kv2, vecs=vecs, w_f1=g["hy_filt_w1"], w_f2=g["hy_filt_w2"],
                  w_f3=g["hy_filt_w3"], skipb=skipb, nabias=nabias, w_out=g["w_out"], w_gate=g["ffn_w_gate"],
                  w_up=g["ffn_w_up"], w_down=g["ffn_w_down"])
    shared = {k: np.ascontiguousarray(v, dtype=f32) for k, v in shared.items()}
    shared.update(host_constants())
    return shared


_NC = {}


def kernel(**inputs):
    shared = host_layout(inputs)
    x = np.asarray(inputs["x"], dtype=np.float32)
    if "nc" not in _NC:
        _NC["nc"] = build_program()
    nc = _NC["nc"]
    in_maps = []
    for c in range(8):
        m = dict(shared)
        m["x"] = np.ascontiguousarray(x[c])
        in_maps.append(m)
    res = run_bass_kernel_spmd(nc, in_maps, core_ids=list(range(8)))
    return np.stack([res.results[c]["out"] for c in range(8)], 0).astype(np.float32)
```

```python
import contextlib
import math
import numpy as np
import ml_dtypes
import concourse.bass as bass
import concourse.mybir as mybir
from concourse.bass_utils import run_bass_kernel_spmd

F32 = mybir.dt.float32
BF16 = mybir.dt.bfloat16
ALU = mybir.AluOpType
AF = mybir.ActivationFunctionType

S = 4096
D = 1024
NL = 2
DFF = 2816
NB = 8
TB = 512
NT = 32
NKF = 9
WIN_COLS = 2496
NV = 80
SEM_ROLL = 20000
MAGIC = 12582912.0
PI_SAFE = 3.1415925


class Dep:
    __slots__ = ("name", "writers", "readers", "multi", "dsem", "dval")
    scope = None

    def __init__(self, name="", multi=False):
        self.name = name
        self.writers = {}
        self.readers = {}
        self.multi = multi
        self.dsem = {}
        self.dval = 0
        if Dep.scope is not None:
            Dep.scope[-1].append(self)


def _merge(d, t):
    k = id(t[0])
    if k not in d or d[k][1] < t[1]:
        d[k] = t


class Prog:
    ENGS = ("pe", "act", "dve", "pool", "sp")

    def __init__(self, nc):
        self.nc = nc
        self.stacks = [contextlib.ExitStack()]
        self.q = {e: [] for e in self.ENGS}
        self.esem = {}
        self.ecnt = {e: 0 for e in self.ENGS}
        self.waited = {e: {} for e in self.ENGS}
        self.nsem = 0
        self.dma_t = {}
        self.uid = 0
        self.deferred = []
        self.free_sems = {"hw": [], "sw": []}
        Dep.scope = [[]]
        for e in self.ENGS:
            self.esem[e] = self.new_sem("c_" + e)

    def new_sem(self, name):
        self.nsem += 1
        return self.stacks[0].enter_context(self.nc.semaphore(f"s{self.nsem}_{name}"))

    def push(self):
        self.stacks.append(contextlib.ExitStack())
        Dep.scope.append([])

    def pop(self):
        self.barrier()
        self.stacks.pop().close()
        for d in Dep.scope.pop():
            for kind, sv in d.dsem.items():
                if sv[1] < SEM_ROLL:
                    self.free_sems[kind].append(sv)
            d.dsem = {}

    def acquire(self, name, kind):
        if self.free_sems[kind]:
            return self.free_sems[kind].pop()
        return [self.new_sem("d_" + kind + name), 0]

    def sbuf(self, name, shape, dt):
        self.uid += 1
        return self.stacks[-1].enter_context(self.nc.sbuf_tensor(f"{name}_{self.uid}", list(shape), dt))

    def psum(self, name, shape, dt=F32):
        return self.stacks[-1].enter_context(self.nc.psum_tensor(name, list(shape), dt))

    def _collect(self, eng, reads, writes):
        tk = {}
        for d in reads:
            for t in d.writers.values():
                _merge(tk, t)
        for d in writes:
            for t in d.readers.values():
                _merge(tk, t)
            if not d.multi:
                for t in d.writers.values():
                    _merge(tk, t)
        waits = []
        w = self.waited[eng]
        for k, (sem, val) in tk.items():
            if eng == "pe" and sem is self.esem["pe"]:
                continue
            if w.get(k, 0) >= val:
                continue
            w[k] = val
            waits.append((sem, val))
        return waits

    def _record(self, t, reads, writes):
        for d in reads:
            _merge(d.readers, t)
        for d in writes:
            if d.multi:
                _merge(d.writers, t)
            else:
                d.writers = {id(t[0]): t}
                d.readers = {}

    def op(self, eng, fn, reads=(), writes=()):
        waits = self._collect(eng, reads, writes)
        if self.ecnt[eng] >= SEM_ROLL:
            self.esem[eng] = self.new_sem("c_" + eng)
            self.ecnt[eng] = 0
        self.ecnt[eng] += 1
        t = (self.esem[eng], self.ecnt[eng])
        self.q[eng].append((waits, fn, self.esem[eng], 1))
        self._record(t, reads, writes)
        return t

    def dma(self, queue, out, in_, reads=(), writes=(), slot=None, **kw):
        waits = self._collect(queue, reads, writes)
        d0 = slot if slot is not None else writes[0]
        kind = "sw" if queue == "pool" else "hw"
        w = self.waited[queue]
        sv = d0.dsem.get(kind)
        if sv is not None and w.get(id(sv[0]), 0) < sv[1]:
            w[id(sv[0])] = sv[1]
            waits.append((sv[0], sv[1]))
        if sv is None or sv[1] >= SEM_ROLL:
            sv = self.acquire(d0.name, kind)
            d0.dsem[kind] = sv
        sv[1] += 16
        dsem = sv[0]
        t = (dsem, sv[1])
        self.dma_t[id(dsem)] = t

        def fn(e, out=out, in_=in_, kw=kw):
            return e.dma_start(out=out, in_=in_, **kw)

        self.q[queue].append((waits, fn, dsem, 16))
        self._record(t, reads, writes)
        return t

    def store(self, out, in_, reads=(), writes=(), slot=None):
        self.deferred.append((out, in_, reads, writes, slot))

    def flush(self):
        for out, in_, reads, writes, slot in self.deferred:
            self.dma("sp", out, in_, reads=reads, writes=writes, slot=slot)
        self.deferred = []

    def barrier(self):
        self.flush()
        for e in self.ENGS:
            w = self.waited[e]
            waits = []
            for e2 in self.ENGS:
                if e2 == e or self.ecnt[e2] == 0:
                    continue
                sem, val = self.esem[e2], self.ecnt[e2]
                if w.get(id(sem), 0) < val:
                    w[id(sem)] = val
                    waits.append((sem, val))
            for k, (sem, val) in self.dma_t.items():
                if w.get(k, 0) < val:
                    w[k] = val
                    waits.append((sem, val))
            self.q[e].append((waits, None, None, 0))

    def emit(self):
        nc = self.nc
        q = self.q
        with nc.Block() as block:
            def run(e, lst):
                for waits, fn, sem, inc in lst:
                    for (s, v) in waits:
                        e.wait_ge(s, v)
                    if fn is not None:
                        fn(e).then_inc(sem, inc)

            @block.tensor
            def _(e):
                run(e, q["pe"])

            @block.scalar
            def _(e):
                run(e, q["act"])

            @block.vector
            def _(e):
                run(e, q["dve"])

            @block.gpsimd
            def _(e):
                run(e, q["pool"])

            @block.sync
            def _(e):
                run(e, q["sp"])

    def close(self):
        while self.stacks:
            self.stacks.pop().close()

    def mm(self, out, lhsT, rhs, start, stop, reads, writes, skip=False):
        return self.op("pe", lambda e: e.matmul(out, lhsT, rhs, start=start, stop=stop,
                                                skip_group_check=skip), reads, writes)

    def tr(self, out, in_, ident, reads, writes):
        return self.op("pe", lambda e: e.transpose(out, in_, ident), reads, writes)

    def act(self, out, in_, func, reads, writes, scale=None, bias=None, accum=None):
        kw = {}
        if scale is not None:
            kw["scale"] = scale
        if bias is not None:
            kw["bias"] = bias
        if accum is not None:
            kw["accum_out"] = accum
        return self.op("act", lambda e: e.activation(out=out, in_=in_, func=func, **kw), reads, writes)

    def tt(self, eng, out, in0, in1, op, reads, writes):
        return self.op(eng, lambda e: e.tensor_tensor(out=out, in0=in0, in1=in1, op=op), reads, writes)

    def ts(self, eng, out, in0, s1, s2, op0, op1, reads, writes):
        if op1 is None and eng == "pool" and op0 == ALU.mult:
            op1, s2 = ALU.add, 0.0
        if op1 is None:
            return self.op(eng, lambda e: e.tensor_scalar(out=out, in0=in0, scalar1=s1, scalar2=None, op0=op0),
                           reads, writes)
        return self.op(eng, lambda e: e.tensor_scalar(out=out, in0=in0, scalar1=s1, scalar2=s2, op0=op0, op1=op1),
                       reads, writes)

    def stt(self, out, in0, scalar, in1, op0, op1, reads, writes):
        return self.op("dve", lambda e: e.scalar_tensor_tensor(out=out, in0=in0, scalar=scalar, in1=in1,
                                                               op0=op0, op1=op1), reads, writes)

    def cp(self, eng, out, in_, reads, writes):
        if eng == "act":
            return self.op("act", lambda e: e.activation(out=out, in_=in_, func=AF.Copy), reads, writes)
        return self.op(eng, lambda e: e.tensor_copy(out=out, in_=in_), reads, writes)

    def memset(self, eng, ap, val, writes):
        return self.op(eng, lambda e: e.memset(ap, val), (), writes)

    def recip(self, out, in_, reads, writes):
        return self.op("dve", lambda e: e.reciprocal(out=out, in_=in_), reads, writes)


class Ring:
    def __init__(self, P, name, shape, dt, n):
        self.t = [P.sbuf(f"{name}{i}", shape, dt) for i in range(n)]
        self.d = [Dep(f"{name}{i}") for i in range(n)]
        self.i = 0
        self.n = n

    def next(self):
        i = self.i
        self.i = (i + 1) % self.n
        return self.t[i], self.d[i]


class Banks:
    def __init__(self, P):
        self.ps = P.psum("psall", [128, 4096], F32)
        self.d = [Dep(f"bank{i}") for i in range(8)]
        self.i = 0

    def next(self):
        i = self.i
        self.i = (i + 1) % 8
        return self.ps[:, i * 512:(i + 1) * 512], self.d[i]

    def get(self, i):
        return self.ps[:, i * 512:(i + 1) * 512], self.d[i]

    def next2(self):
        if self.i % 2:
            self.i = (self.i + 1) % 8
        i = self.i
        self.i = (i + 2) % 8
        return self.ps[:, i * 512:(i + 2) * 512], self.d[i], self.d[i + 1]


def build_program(dbg=None):
    nc = bass.Bass("TRN2", target_bir_lowering=False)
    P = Prog(nc)
    skind = "ExternalOutput" if dbg else "Internal"

    def din(name, shape, dt=F32):
        return nc.dram_tensor(name, list(shape), dt, kind="ExternalInput").ap()

    def dscr(name, shape, dt=F32):
        return nc.dram_tensor(name, list(shape), dt, kind=skind).ap()

    x_in = din("x", [S, D])
    w_inA = din("w_inA", [NL, D, WIN_COLS])
    w_uq2 = din("w_uq2", [NL, 256, 1152])
    w_kv2 = din("w_kv2", [NL, 128, 768])
    vecs = din("vecs", [NL, 128, NV])
    w_f1 = din("w_f1", [NL, 17, 64])
    w_f2 = din("w_f2", [NL, 64, 64])
    w_f3 = din("w_f3", [NL, 64, 1536])
    skipb = din("skipb", [NL, 128, 768])
    nabias = din("nabias", [NL, 5, 128, 2560])
    w_out = din("w_out", [NL, D, D])
    w_gate = din("w_gate", [NL, D, DFF])
    w_up = din("w_up", [NL, D, DFF])
    w_down = din("w_down", [NL, DFF, D])
    ident_in = din("ident", [128, 128])
    ropeC = din("ropeC", [32, S])
    ropeS = din("ropeS", [32, S])
    zT_in = din("zT", [17, S])
    decay_in = din("decay", [S, 384])
    CfF = din("CfF", [NKF, 128, 4096], BF16)
    SfF = din("SfF", [NKF, 128, 4096], BF16)
    CfI = din("CfI", [NT, 128, NKF * 128], BF16)
    SfI = din("SfI", [NT, 128, NKF * 128], BF16)
    wk_in = din("wk", [128, 8 * NKF])
    out_ap = nc.dram_tensor("out", [S, D], F32, kind="ExternalOutput").ap()

    xT = dscr("xT", [8, 128, S])
    QT = dscr("QT", [6, 96, S], BF16)
    KT = dscr("KT", [6, 96, S], BF16)
    Vm = dscr("Vm", [6, 128, NT, 65], BF16)
    hyT = dscr("hyT", [9, 128, S])
    naQT = dscr("naQT", [2, 128, S], BF16)
    naKT = dscr("naKT", [2, 128, S], BF16)
    naV = dscr("naV", [NT, 128, 260], BF16)
    ymix = dscr("ymix", [S, D])
    xg = dscr("xg", [2, S, 384])
    Gs = dscr("Gs", [2, NKF, 128, 4 * 768])
    d_xT = [Dep(f"xT{b}") for b in range(NB)]
    d_QT = Dep("QT", multi=True)
    d_KT = Dep("KT", multi=True)
    d_Vm = Dep("Vm", multi=True)
    d_hyT = Dep("hyT", multi=True)
    d_naQT = Dep("naQT", multi=True)
    d_naKT = Dep("naKT", multi=True)
    d_naV = Dep("naV", multi=True)
    d_ymix = Dep("ymix", multi=True)
    d_xg = Dep("xg", multi=True)
    d_Gs = Dep("Gs", multi=True)
    d_out = Dep("out", multi=True)

    banks = Banks(P)
    ident = P.sbuf("ident", [128, 128], F32)
    d_ident = Dep("ident")
    P.dma("sp", ident[:], ident_in, writes=[d_ident])
    ones_b = P.sbuf("ones_b", [128, 128], BF16)
    d_const = Dep("const")
    P.memset("dve", ones_b[:], 1.0, [d_const])
    epsc = P.sbuf("epsc", [128, 1], F32)
    P.memset("dve", epsc[:], 1e-6, [d_const])
    vec = [P.sbuf(f"vec{l}", [128, NV], F32) for l in range(NL)]
    d_vec = Dep("vec")
    for l in range(NL):
        P.dma("sp", vec[l][:], vecs[l], writes=[d_vec])
    wkt = P.sbuf("wkt", [128, 8 * NKF], F32)
    P.dma("sp", wkt[:], wk_in, writes=[d_vec])

    def stop(name):
        return dbg == name

    def finish():
        P.barrier()
        P.emit()
        P.close()
        return nc

    def rms_p1(xt, d_xt, nch, sq, d_sq):
        P.act(sq[:, 0:nch, :], xt[:, 0:nch, :], AF.Square, [d_xt], [d_sq])

    def rms_p2(xt, d_xt, nch, gcol, vl, outT, d_out_, sq, d_sq, rs, d_rs, n_feat):
        bk, d_bk = banks.next()
        for k in range(nch):
            P.mm(bk, ones_b[:, :], sq[:, k, :], k == 0, k == nch - 1, [d_sq, d_const], [d_bk])
        P.act(rs[:, :], bk, AF.Sqrt, [d_bk, d_const], [d_rs], scale=1.0 / n_feat, bias=epsc[:, 0:1])
        P.recip(rs[:, :], rs[:, :], [d_rs], [d_rs])
        for k in range(nch):
            P.stt(outT[:, k, :], xt[:, k, :], vl[:, gcol + k:gcol + k + 1], rs[:, :], ALU.mult, ALU.mult,
                  [d_xt, d_rs, d_vec], [d_out_])

    def rms_feature_major(xt, d_xt, nch, gcol, vl, outT, d_out_, sq, d_sq, rs, d_rs, n_feat):
        rms_p1(xt, d_xt, nch, sq, d_sq)
        rms_p2(xt, d_xt, nch, gcol, vl, outT, d_out_, sq, d_sq, rs, d_rs, n_feat)

    def tcols(ap2, tau):
        return ap2.rearrange("q (a p r) -> q r a p", a=8, p=128, r=4)[:, tau // 8, tau % 8, :]

    def trows(ap2, tau):
        return ap2.rearrange("(a p r) c -> r a p c", a=8, p=128, r=4)[tau // 8, tau % 8]

    class FoldBufs:
        def __init__(self):
            self.cp = Ring(P, "fcp", [128, 384], F32, 4)
            self.l1 = {n: Ring(P, "f1" + n, [128, 384], F32, 1) for n in ("P", "Q", "Pp", "Qp", "R", "T", "Rp", "Tp")}
            self.l2 = {n: Ring(P, "f2" + n, [128, 384], F32, 1) for n in
                       ("A1", "A2", "A3", "A4", "B1", "B2", "B3", "B4")}

    def fold_forward(*a, **k):
        g = fold_forward_g(*a, **k)
        try:
            while True:
                next(g)
        except StopIteration as e:
            return e.value

    def fold_forward_g(fb, src, d_src, Cb, Sb, d_C, d_S, need, l2eng=None, bases=(0, 4), cp_eng="act"):
        Cv = Cb[:, :].rearrange("p (r a f) -> p r a f", r=4, a=8)
        Sv = Sb[:, :].rearrange("p (r a f) -> p r a f", r=4, a=8)
        L1 = {}
        hp = 0
        for ps_, (ra, rb) in enumerate(((0, 2), (1, 3))):
            for hf, (Mv, d_M) in enumerate(((Cv, d_C), (Sv, d_S))):
                if bases[0] == bases[1]:
                    bb = bases[0] + 2 * (hp % 2)
                else:
                    bb = bases[ps_] + 2 * hf
                hp += 1
                bka, bkb = banks.get(bb), banks.get(bb + 1)
                for (r, bk) in ((ra, bka), (rb, bkb)):
                    for a in range(8):
                        P.mm(bk[0][:, 0:384], Mv[:, r, a, :], src(r * 8 + a), a == 0, a == 7, [d_M, d_src], [bk[1]])
                        if a % 2:
                            yield
                cc, d_cc = fb.cp.next()
                P.cp(cp_eng, cc[:, :], bkb[0][:, 0:384], [bkb[1]], [d_cc])
                if ps_ == 0:
                    names = ("P", "Q") if hf == 0 else ("Pp", "Qp")
                else:
                    names = ("R", "T") if hf == 0 else ("Rp", "Tp")
                for nm, op in zip(names, (ALU.add, ALU.subtract)):
                    t_, d_t = fb.l1[nm].next()
                    P.tt("dve", t_[:, :], bka[0][:, 0:384], cc[:, :], op, [bka[1], d_cc], [d_t])
                    L1[nm] = (t_, d_t)
        out = {}

        def l2(nm, x, y, op):
            t_, d_t = fb.l2[nm].next()
            eng = "pool" if l2eng is None else l2eng[nm[1]]
            P.tt(eng, t_[:, :], L1[x][0][:, :], L1[y][0][:, :], op, [L1[x][1], L1[y][1]], [d_t])
            out[nm] = (t_, d_t)

        if "A" in need:
            l2("A1", "P", "R", ALU.add)
            l2("A4", "P", "R", ALU.subtract)
            l2("A2", "Q", "Tp", ALU.add)
            l2("A3", "Q", "Tp", ALU.subtract)
        if "B" in need:
            l2("B1", "Pp", "Rp", ALU.add)
            l2("B4", "Rp", "Pp", ALU.subtract)
            l2("B2", "T", "Qp", ALU.subtract)
            l2("B3", "Qp", "T", ALU.add)
        return out

    P.push()
    xin_r = Ring(P, "xin", [128, D], F32, 2)
    xtb_r = Ring(P, "xtb0", [128, 8, TB], F32, 2)
    for b in range(NB):
        xtb, d_xtb = xtb_r.next()
        for i in range(4):
            ti = b * 4 + i
            xin, d_xin = xin_r.next()
            P.dma("sp", xin[:], x_in[ti * 128:(ti + 1) * 128, :], writes=[d_xin])
            if i == 0:
                P.flush()
            for j in range(2):
                bk, d_bk = banks.next()
                for c in range(4):
                    k = j * 4 + c
                    P.tr(bk[:, c * 128:(c + 1) * 128], xin[:, k * 128:(k + 1) * 128], ident[:, :],
                         [d_xin, d_ident], [d_bk])
                P.cp("act" if j == 0 else "dve", xtb[:, j * 4:(j + 1) * 4, i * 128:(i + 1) * 128],
                     bk.rearrange("p (a b) -> p a b", a=4), [d_bk], [d_xtb])
        P.store(xT[:, :, b * TB:(b + 1) * TB].rearrange("k p s -> p k s"), xtb[:], reads=[d_xtb], writes=[d_xT[b]])
    P.pop()
    if stop("p0"):
        return finish()

    for l in range(NL):
        vl = vec[l]
        P.push()
        winA = P.sbuf("winA", [128, 8, WIN_COLS], BF16)
        d_w = Dep("winA")
        P.dma("pool", winA[:], w_inA[l].rearrange("(k p) c -> p k c", p=128), writes=[d_w])
        wuq = P.sbuf("wuq", [128, 2, 1152], BF16)
        d_wuq = Dep("wuq")
        P.dma("pool", wuq[:], w_uq2[l].rearrange("(j p) c -> p j c", p=128), writes=[d_wuq])
        wkv = P.sbuf("wkv", [128, 768], BF16)
        d_wkv = Dep("wkv")
        P.dma("pool", wkv[:], w_kv2[l], writes=[d_wkv])
        xtb_r = Ring(P, "xtbA", [128, 8, TB], F32, 2)
        sq = P.sbuf("sqA", [128, 8, TB], BF16)
        d_sq = Dep("sqA")
        rs = P.sbuf("rsA", [128, TB], F32)
        d_rs = Dep("rsA")
        sqq = P.sbuf("sqq", [128, 2, TB], BF16)
        d_sqq = Dep("sqq")
        rsq = P.sbuf("rsq", [128, TB], F32)
        d_rsq = Dep("rsq")
        sqk = P.sbuf("sqk", [128, 1, TB], BF16)
        d_sqk = Dep("sqk")
        rsk = P.sbuf("rsk", [128, TB], F32)
        d_rsk = Dep("rsk")
        hT_r = Ring(P, "hT", [128, 8, TB], BF16, 2)
        cq = P.sbuf("cq", [128, 2, TB], F32)
        d_cq = Dep("cq")
        cqn = P.sbuf("cqn", [128, 2, TB], BF16)
        d_cqn = Dep("cqn")
        ckv = P.sbuf("ckv", [128, 1, TB], F32)
        d_ckv = Dep("ckv")
        ckvn = P.sbuf("ckvn", [128, 1, TB], BF16)
        d_ckvn = Dep("ckvn")
        rC_r = Ring(P, "rC", [128, TB], F32, 2)
        rS_r = Ring(P, "rS", [128, TB], F32, 2)
        tA_r = Ring(P, "tA", [128, TB], F32, 2)
        tB_r = Ring(P, "tB", [128, TB], F32, 2)
        QTb_r = Ring(P, "QTb", [128, 6, TB], BF16, 2)
        KTb_r = Ring(P, "KTb", [128, 6, TB], BF16, 2)
        Vb_r = Ring(P, "Vb", [128, 6, 4, 65], BF16, 2)
        hyb_r = Ring(P, "hyb", [128, 3, TB], F32, 3)
        nqk_r = Ring(P, "nqk", [128, 4, TB], BF16, 2)
        nvb_r = Ring(P, "nvb", [128, 4, 260], BF16, 2)
        for r_ in (Vb_r, nvb_r):
            for t_, d_ in zip(r_.t, r_.d):
                P.memset("pool", t_[:], 1.0, [d_])
        for b in range(NB):
            sl = slice(b * TB, (b + 1) * TB)
            xtb, d_xtb = xtb_r.next()
            P.dma("sp", xtb[:], xT[:, :, sl].rearrange("k p s -> p k s"), reads=[d_xT[b]], writes=[d_xtb])
            rC, d_rC = rC_r.next()
            rS, d_rS = rS_r.next()
            P.dma("sp", rC[64:96, :], ropeC[:, sl], writes=[d_rC])
            P.dma("sp", rS[64:96, :], ropeS[:, sl], writes=[d_rS])
            P.flush()
            hT, d_hT = hT_r.next()
            rms_feature_major(xtb, d_xtb, 8, 0, vl, hT, d_hT, sq, d_sq, rs, d_rs, D)

            def proj(c0, M):
                bk, d_bk = banks.next()
                for k in range(8):
                    P.mm(bk[0:M, :], winA[:, k, c0:c0 + M], hT[:, k, :], k == 0, k == 7, [d_w, d_hT], [d_bk])
                return bk, d_bk

            for j in range(2):
                bk, d_bk = proj(j * 128, 128)
                P.cp("act", cq[:, j, :], bk, [d_bk], [d_cq])
            bk, d_bk = proj(256, 128)
            P.cp("act", ckv[:, 0, :], bk, [d_bk], [d_ckv])
            rms_p1(cq, d_cq, 2, sqq, d_sqq)
            rms_p1(ckv, d_ckv, 1, sqk, d_sqk)

            KTb, d_KTb = KTb_r.next()
            bk, d_bk = proj(384, 96)
            bk2, d_bk2 = proj(480, 96)
            tA, d_tA = tA_r.next()
            tB, d_tB = tB_r.next()
            P.tt("dve", tA[64:96, :], bk[64:96, :], rC[64:96, :], ALU.mult, [d_bk, d_rC], [d_tA])
            P.tt("dve", tB[64:96, :], bk2[64:96, :], rS[64:96, :], ALU.mult, [d_bk2, d_rS], [d_tB])
            for h in range(6):
                P.tt("pool", KTb[64:96, h, :], tA[64:96, :], tB[64:96, :], ALU.add, [d_tA, d_tB], [d_KTb])

            for g3 in range(3):
                hyb, d_hyb = hyb_r.next()
                for c3 in range(3):
                    c = g3 * 3 + c3
                    bk, d_bk = proj(576 + c * 128, 128)
                    P.cp("act" if c % 2 else "dve", hyb[:, c3, :], bk, [d_bk], [d_hyb])
                P.store(hyT[g3 * 3:(g3 + 1) * 3, :, sl].rearrange("c p s -> p c s"), hyb[:], reads=[d_hyb],
                      writes=[d_hyT], slot=d_hyb)
                if g3 == 0:
                    rms_p2(cq, d_cq, 2, 16, vl, cqn, d_cqn, sqq, d_sqq, rsq, d_rsq, 256)
                    rms_p2(ckv, d_ckv, 1, 18, vl, ckvn, d_ckvn, sqk, d_sqk, rsk, d_rsk, 128)

            nqk, d_nqk = nqk_r.next()
            for c in range(4):
                bk, d_bk = proj(1728 + c * 128, 128)
                P.cp("act" if c % 2 else "dve", nqk[:, c, :], bk, [d_bk], [d_nqk])
            P.store(naQT[:, :, sl].rearrange("c p s -> p c s"), nqk[:, 0:2, :], reads=[d_nqk], writes=[d_naQT],
                  slot=d_nqk)
            P.store(naKT[:, :, sl].rearrange("c p s -> p c s"), nqk[:, 2:4, :], reads=[d_nqk], writes=[d_naKT],
                  slot=d_nqk)
            nvb, d_nvb = nvb_r.next()
            for i in range(4):
                bk, d_bk = banks.next()
                for k in range(8):
                    P.mm(bk[:, 0:256], hT[:, k, i * 128:(i + 1) * 128], winA[:, k, 2240:2496], k == 0, k == 7,
                         [d_w, d_hT], [d_bk])
                P.cp("act" if i % 2 else "dve",
                     nvb[:, i, :].rearrange("p (h c) -> p h c", h=4)[:, :, 0:64],
                     bk[:, 0:256].rearrange("p (h c) -> p h c", h=4), [d_bk], [d_nvb])
            P.store(naV[b * 4:(b + 1) * 4].rearrange("t p c -> p t c"), nvb[:], reads=[d_nvb], writes=[d_naV],
                  slot=d_nvb)

            QTb, d_QTb = QTb_r.next()
            for h in range(6):
                bk, d_bk = banks.next()
                bk2, d_bk2 = banks.next()
                for j in range(2):
                    P.mm(bk[0:96, :], wuq[:, j, h * 192:h * 192 + 96], cqn[:, j, :], j == 0, j == 1, [d_wuq, d_cqn], [d_bk])
                for j in range(2):
                    P.mm(bk2[0:96, :], wuq[:, j, h * 192 + 96:h * 192 + 192], cqn[:, j, :], j == 0, j == 1,
                         [d_wuq, d_cqn], [d_bk2])
                P.cp("act", QTb[0:64, h, :], bk[0:64, :], [d_bk], [d_QTb])
                tA, d_tA = tA_r.next()
                tB, d_tB = tB_r.next()
                P.tt("dve", tA[64:96, :], bk[64:96, :], rC[64:96, :], ALU.mult, [d_bk, d_rC], [d_tA])
                P.tt("dve", tB[64:96, :], bk2[64:96, :], rS[64:96, :], ALU.mult, [d_bk2, d_rS], [d_tB])
                P.tt("pool", QTb[64:96, h, :], tA[64:96, :], tB[64:96, :], ALU.add, [d_tA, d_tB], [d_QTb])
            P.store(QT[:, :, sl].rearrange("h p s -> p h s"), QTb[0:96, :, :], reads=[d_QTb], writes=[d_QT],
                  slot=d_QTb)

            for h in range(6):
                bk, d_bk = banks.next()
                P.mm(bk[0:64, :], wkv[:, h * 64:(h + 1) * 64], ckvn[:, 0, :], True, True, [d_wkv, d_ckvn], [d_bk])
                P.cp("act" if h % 2 else "dve", KTb[0:64, h, :], bk[0:64, :], [d_bk], [d_KTb])
            P.store(KT[:, :, sl].rearrange("h p s -> p h s"), KTb[0:96, :, :], reads=[d_KTb], writes=[d_KT],
                  slot=d_KTb)
            Vb, d_Vb = Vb_r.next()
            for i in range(4):
                bk, d_bk = banks.next()
                P.mm(bk[:, 0:384], ckvn[:, 0, i * 128:(i + 1) * 128], wkv[:, 384:768], True, True,
                     [d_wkv, d_ckvn], [d_bk])
                P.cp("act" if i % 2 else "dve", Vb[:, :, i, 0:64],
                     bk[:, 0:384].rearrange("p (h c) -> p h c", h=6), [d_bk], [d_Vb])
            P.store(Vm[:, :, b * 4:(b + 1) * 4, :].rearrange("h p t c -> p h t c"), Vb[:], reads=[d_Vb], writes=[d_Vm],
                  slot=d_Vb)
        P.pop()
        if stop(f"A{l}"):
            return finish()

        P.push()
        hs = P.sbuf("hs", [128, NT, 768], BF16)
        hd = P.sbuf("hd", [128, NT, 768], BF16)
        d_hsd = Dep("hsd", multi=True)
        P.push()
        zT = P.sbuf("zT", [17, S], F32)
        wf1 = P.sbuf("wf1", [17, 64], F32)
        wf2 = P.sbuf("wf2", [64, 64], F32)
        wf3 = P.sbuf("wf3", [64, 1536], F32)
        d_fw = Dep("fw", multi=True)
        P.dma("sp", zT[:], zT_in, writes=[d_fw])
        P.dma("sp", wf1[:], w_f1[l], writes=[d_fw])
        P.dma("sp", wf2[:], w_f2[l], writes=[d_fw])
        P.dma("sp", wf3[:], w_f3[l], writes=[d_fw])
        hid1 = P.sbuf("hid1", [64, S], F32)
        hid2 = P.sbuf("hid2", [64, S], F32)
        d_h1 = Dep("hid1", multi=True)
        d_h2 = Dep("hid2", multi=True)
        fa_r = Ring(P, "fa", [64, TB], F32, 2)
        ft_r = Ring(P, "ft", [64, TB], F32, 2)

        def sin_block(bk, d_bk, bcol, fcol, out, d_o):
            a, d_a = fa_r.next()
            t, d_t = ft_r.next()
            P.ts("dve", a[:, :], bk[0:64, :], vl[0:64, bcol:bcol + 1], vl[0:64, fcol:fcol + 1], ALU.add, ALU.mult,
                 [d_bk, d_vec], [d_a])
            P.ts("dve", t[:, :], a[:, :], 1.0 / (2.0 * math.pi), MAGIC, ALU.mult, ALU.add, [d_a], [d_t])
            P.ts("dve", t[:, :], t[:, :], MAGIC, -2.0 * math.pi, ALU.subtract, ALU.mult, [d_t], [d_t])
            P.tt("dve", a[:, :], a[:, :], t[:, :], ALU.add, [d_a, d_t], [d_a])
            P.ts("dve", a[:, :], a[:, :], -PI_SAFE, PI_SAFE, ALU.max, ALU.min, [d_a], [d_a])
            P.act(out, a[:, :], AF.Sin, [d_a], [d_o])

        for b in range(NB):
            sl = slice(b * TB, (b + 1) * TB)
            bk, d_bk = banks.next()
            P.mm(bk[0:64, :], wf1[:, :], zT[:, sl], True, True, [d_fw], [d_bk])
            sin_block(bk, d_bk, 71, 72, hid1[:, sl], d_h1)
        for b in range(NB):
            sl = slice(b * TB, (b + 1) * TB)
            bk, d_bk = banks.next()
            P.mm(bk[0:64, :], wf2[:, :], hid1[:, sl], True, True, [d_fw, d_h1], [d_bk])
            sin_block(bk, d_bk, 73, 74, hid2[:, sl], d_h2)
        hraw_r = Ring(P, "hraw", [128, 1536], F32, 2)
        dk_r = Ring(P, "dk", [128, 384], F32, 2)
        fs_r = Ring(P, "fs", [128, 2, 384], F32, 2)
        fd_r = Ring(P, "fd", [128, 2, 384], F32, 2)
        for i in range(NT):
            hraw, d_hr = hraw_r.next()
            for n in range(3):
                bk, d_bk = banks.next()
                P.mm(bk, tcols(hid2[:, :], i), wf3[:, n * 512:(n + 1) * 512], True, True, [d_fw, d_h2], [d_bk])
                P.cp("act", hraw[:, n * 512:(n + 1) * 512], bk, [d_bk], [d_hr])
            dk, d_dk = dk_r.next()
            P.dma("sp", dk[:], trows(decay_in, i), writes=[d_dk])
            fs, d_fs = fs_r.next()
            fd, d_fd = fd_r.next()
            hv = hraw[:, :].rearrange("p (o r c) -> p o r c", o=2, r=2)
            P.tt("dve", fs[:, :, :], hv[:, :, 0, :], hv[:, :, 1, :], ALU.add, [d_hr], [d_fs])
            P.tt("dve", fd[:, :, :], hv[:, :, 0, :], hv[:, :, 1, :], ALU.subtract, [d_hr], [d_fd])
            for o in range(2):
                P.tt("pool", hs[:, i, o * 384:(o + 1) * 384], fs[:, o, :], dk[:, :], ALU.mult, [d_fs, d_dk], [d_hsd])
                P.tt("pool", hd[:, i, o * 384:(o + 1) * 384], fd[:, o, :], dk[:, :], ALU.mult, [d_fd, d_dk], [d_hsd])
        P.pop()
        if stop(f"HF{l}"):
            dbg_h = nc.dram_tensor("dbg_hs", [NT, 128, 768], F32, kind="ExternalOutput").ap()
            dbg_h2 = nc.dram_tensor("dbg_hd", [NT, 128, 768], F32, kind="ExternalOutput").ap()
            d_dbg = Dep("dbg", multi=True)
            P.store(dbg_h.rearrange("t p c -> p t c"), hs[:], reads=[d_hsd], writes=[d_dbg])
            P.store(dbg_h2.rearrange("t p c -> p t c"), hd[:], reads=[d_hsd], writes=[d_dbg])
            return finish()
        P.push()
        skb = P.sbuf("skb", [128, 768], F32)
        d_skb = Dep("skb")
        P.dma("sp", skb[:], skipb[l], writes=[d_skb])
        Cb_r = Ring(P, "CbF", [128, 4096], BF16, 1)
        Sb_r = Ring(P, "SbF", [128, 4096], BF16, 1)
        gst_r = Ring(P, "gst", [128, 4, 768], F32, 1)
        gtmp_r = Ring(P, "gtmp", [128, 384], F32, 2)
        fb = FoldBufs()

        def a2_gen():
            for kc in range(NKF):
                Cb, d_C = Cb_r.next()
                Sb, d_S = Sb_r.next()
                P.dma("sp", Cb[:], CfF[kc], writes=[d_C])
                P.dma("sp", Sb[:], SfF[kc], writes=[d_S])
                for o in range(2):
                    gst, d_gst = gst_r.next()
                    oa = yield from fold_forward_g(fb, lambda tau, o=o: hs[:, tau, o * 384:(o + 1) * 384], d_hsd,
                                                   Cb, Sb, d_C, d_S, "A", bases=(4, 4), cp_eng="dve")
                    for j in range(4):
                        gt_, d_gt_ = gtmp_r.next()
                        aj, d_aj = oa[f"A{j + 1}"]
                        P.tt("pool", gt_[:, :], aj[:, :], skb[:, o * 384:(o + 1) * 384], ALU.add, [d_aj, d_skb],
                             [d_gt_])
                        P.ts("pool", gst[:, j, 0:384], gt_[:, :], wkt[:, kc * 4 + j:kc * 4 + j + 1], None, ALU.mult,
                             None, [d_gt_, d_vec], [d_gst])
                    ob = yield from fold_forward_g(fb, lambda tau, o=o: hd[:, tau, o * 384:(o + 1) * 384], d_hsd,
                                                   Cb, Sb, d_C, d_S, "B", bases=(4, 4), cp_eng="dve")
                    for j in range(4):
                        bj, d_bj = ob[f"B{j + 1}"]
                        P.ts("pool", gst[:, j, 384:768], bj[:, :],
                             wkt[:, 4 * NKF + kc * 4 + j:4 * NKF + kc * 4 + j + 1], None, ALU.mult, None,
                             [d_bj, d_vec], [d_gst])
                    P.dma("sp", Gs[o, kc], gst[:].rearrange("p j c -> p (j c)"), reads=[d_gst], writes=[d_Gs],
                          slot=d_gst)

        a2g = a2_gen()

        def a2_step():
            try:
                next(a2g)
            except StopIteration:
                pass

        Vh_r = Ring(P, "Vh", [128, NT, 65], BF16, 2)
        KTh_r = Ring(P, "KTh", [96, S], BF16, 1)
        QTh_r = Ring(P, "QTh", [96, S], BF16, 1)
        pT_r = Ring(P, "pT", [128, TB], BF16, 6)
        rd_r = Ring(P, "rdM", [128, 4], F32, 2)
        yas_r = Ring(P, "yas", [128, 4, 64], F32, 2)
        sc_m = 1.0 / math.sqrt(96.0)
        s_rot = 0
        for h in range(6):
            KTh, d_K = KTh_r.next()
            QTh, d_Q = QTh_r.next()
            Vh, d_V = Vh_r.next()
            P.dma("sp", KTh[:], KT[h], reads=[d_KT], writes=[d_K])
            P.dma("sp", QTh[:], QT[h], reads=[d_QT], writes=[d_Q])
            P.dma("sp", Vh[:], Vm[h], reads=[d_Vm], writes=[d_V])
            for qb in range(NB):
                po, d_po = banks.get(3)
                pts = []
                for step in range(NT + 2):
                    if step < NT:
                        kt = step
                        sb, d_sb = banks.get(s_rot % 3)
                        s_rot += 1
                        P.mm(sb, KTh[:, kt * 128:(kt + 1) * 128], QTh[:, qb * TB:(qb + 1) * TB], True, True,
                             [d_K, d_Q], [d_sb])
                        pT, d_pT = pT_r.next()
                        P.act(pT[:, :], sb, AF.Exp, [d_sb], [d_pT], scale=sc_m)
                        pts.append((pT, d_pT))
                        a2_step()
                    if step >= 2:
                        kt = step - 2
                        pT, d_pT = pts[kt]
                        for j in range(4):
                            P.mm(po[:, j * 65:(j + 1) * 65], pT[:, j * 128:(j + 1) * 128],
                                 Vh[:, kt, :], kt == 0 and j == 0, kt == NT - 1 and j == 3,
                                 [d_pT, d_V], [d_po], skip=True)
                rdt, d_rd = rd_r.next()
                P.recip(rdt[:, 0:4], po[:, 0:260].rearrange("p (j c) -> p j c", c=65)[:, :, 64], [d_po], [d_rd])
                yas, d_yas = yas_r.next()
                for j in range(4):
                    P.ts("dve", yas[:, j, :], po[:, j * 65:j * 65 + 64], rdt[:, j:j + 1],
                         None, ALU.mult, None, [d_po, d_rd], [d_yas])
                P.dma("sp", ymix[qb * TB:(qb + 1) * TB, h * 64:(h + 1) * 64].rearrange("(t p) c -> p t c", p=128),
                      yas[:], reads=[d_yas], writes=[d_ymix], slot=d_yas)
        for _ in a2g:
            pass
        P.pop()
        P.pop()
        if stop(f"M{l}"):
            return finish()
        P.push()
        nq = P.sbuf("nq", [128, 2, S], BF16)
        nk = P.sbuf("nk", [128, 2, S], BF16)
        nv = P.sbuf("nv", [128, NT, 260], BF16)
        nbt = P.sbuf("nbt", [128, 5, 2560], F32)
        yc = P.sbuf("yc", [128, NT, 256], F32)
        d_nin = Dep("nin", multi=True)
        d_yc = Dep("yc", multi=True)
        P.dma("sp", nq[:], naQT.rearrange("c p s -> p c s"), reads=[d_naQT], writes=[d_nin])
        P.dma("sp", nk[:], naKT.rearrange("c p s -> p c s"), reads=[d_naKT], writes=[d_nin])
        P.dma("sp", nv[:], naV.rearrange("t p c -> p t c"), reads=[d_naV], writes=[d_nin])
        P.dma("sp", nbt[:], nabias[l].rearrange("t p c -> p t c"), writes=[d_nin])
        tS_r = Ring(P, "tS", [128, 640], F32, 3)
        pN_r = Ring(P, "pN", [128, 640], BF16, 3)
        rdn_r = Ring(P, "rdN", [128, 4], F32, 2)
        items = [(m, h) for m in range(NT) for h in range(4)]
        pend = None
        for idx in range(len(items) + 1):
            if idx < len(items):
                m, h = items[idx]
                ty = {0: 0, 1: 1, 30: 2, 31: 3}.get(m, 4)
                c0 = min(max(m - 2, 0), 27)
                j, pb = h // 2, 64 * (h % 2)
                bi = 2 * (idx % 3)
                s2 = banks.ps[:, bi * 512:(bi + 2) * 512]
                d_a, d_b = banks.d[bi], banks.d[bi + 1]
                for i in range(5):
                    P.mm(s2[:, i * 128:(i + 1) * 128], nk[pb:pb + 64, j, (c0 + i) * 128:(c0 + i + 1) * 128],
                         nq[pb:pb + 64, j, m * 128:(m + 1) * 128], True, True, [d_nin], [d_a if i < 4 else d_b])
                tS, d_tS = tS_r.next()
                P.stt(tS[:, :], s2[:, 0:640], 0.125, nbt[:, ty, h * 640:(h + 1) * 640], ALU.mult, ALU.add,
                      [d_a, d_b, d_nin], [d_tS])
                pN, d_pN = pN_r.next()
                P.act(pN[:, :], tS[:, :], AF.Exp, [d_tS], [d_pN])
                cur = (m, h, c0, pN, d_pN)
            if pend is not None:
                m, h, c0, pN, d_pN = pend
                po, d_po = banks.get(6 + m % 2)
                for i in range(5):
                    P.mm(po[:, h * 65:(h + 1) * 65], pN[:, i * 128:(i + 1) * 128], nv[:, c0 + i, h * 65:(h + 1) * 65],
                         h == 0 and i == 0, h == 3 and i == 4, [d_pN, d_nin], [d_po], skip=True)
                if h == 3:
                    rdt, d_rd = rdn_r.next()
                    P.recip(rdt[:, 0:4], po[:, 0:260].rearrange("p (j c) -> p j c", c=65)[:, :, 64], [d_po], [d_rd])
                    for hh in range(4):
                        P.ts("dve", yc[:, m, hh * 64:(hh + 1) * 64], po[:, hh * 65:hh * 65 + 64], rdt[:, hh:hh + 1],
                             None, ALU.mult, None, [d_po, d_rd], [d_yc])
            pend = cur if idx < len(items) else None
        for q4 in range(4):
            P.store(ymix[q4 * 1024:(q4 + 1) * 1024, 768:1024].rearrange("(t p) c -> p t c", p=128),
                  yc[:, q4 * 8:(q4 + 1) * 8, :], reads=[d_yc], writes=[d_ymix])
        P.pop()
        if stop(f"N{l}"):
            return finish()


        P.push()
        u_tok = P.sbuf("u_tok", [128, NT, 384], BF16)
        d_u = Dep("u_tok", multi=True)
        P.push()
        hyc_r = Ring(P, "hyc", [128, S + 2], F32, 2)
        ucT_r = Ring(P, "ucT", [128, S], F32, 2)
        xst_r = Ring(P, "xst", [128, NT, 128], F32, 2)
        for t_, d_ in zip(hyc_r.t, hyc_r.d):
            P.memset("pool", t_[:, 0:1], 0.0, [d_])
            P.memset("pool", t_[:, S + 1:S + 2], 0.0, [d_])
        for c in range(9):
            hyc, d_hyc = hyc_r.next()
            P.dma("sp", hyc[:, 1:S + 1], hyT[c], reads=[d_hyT], writes=[d_hyc])
            P.flush()
            ucT, d_uc = ucT_r.next()
            w0 = vl[:, 35 + c:36 + c]
            w1 = vl[:, 44 + c:45 + c]
            w2 = vl[:, 53 + c:54 + c]
            bb = vl[:, 62 + c:63 + c]
            P.ts("dve", ucT[:, :], hyc[:, 1:S + 1], w1, bb, ALU.mult, ALU.add, [d_hyc, d_vec], [d_uc])
            P.stt(ucT[:, :], hyc[:, 0:S], w0, ucT[:, :], ALU.mult, ALU.add, [d_hyc, d_uc, d_vec], [d_uc])
            P.stt(ucT[:, :], hyc[:, 2:S + 2], w2, ucT[:, :], ALU.mult, ALU.add, [d_hyc, d_uc, d_vec], [d_uc])
            if c >= 3:
                xst, d_xst = xst_r.next()
            for g4 in range(8):
                bk, d_bk = banks.next()
                for q in range(4):
                    ti = g4 * 4 + q
                    P.tr(bk[:, q * 128:(q + 1) * 128], tcols(ucT[:, :], ti), ident[:, :],
                         [d_uc, d_ident], [d_bk])
                bv = bk.rearrange("p (a b) -> p a b", a=4)
                if c < 3:
                    P.cp("act" if g4 % 2 else "dve", u_tok[:, g4 * 4:(g4 + 1) * 4, c * 128:(c + 1) * 128], bv,
                         [d_bk], [d_u])
                else:
                    P.cp("act" if g4 % 2 else "dve", xst[:, g4 * 4:(g4 + 1) * 4, :], bv, [d_bk], [d_xst])
            if c >= 3:
                o, cc = (c - 3) // 3, (c - 3) % 3
                for q4 in range(4):
                    P.store(xg[o, q4 * 1024:(q4 + 1) * 1024, cc * 128:(cc + 1) * 128].rearrange(
                        "(t p) c -> p t c", p=128), xst[:, q4 * 8:(q4 + 1) * 8, :], reads=[d_xst], writes=[d_xg],
                          slot=d_xst)
        P.pop()
        if stop(f"HP{l}"):
            dbg_u = nc.dram_tensor("dbg_u", [NT, 128, 384], F32, kind="ExternalOutput").ap()
            d_dbg = Dep("dbg", multi=True)
            P.store(dbg_u.rearrange("t p c -> p t c"), u_tok[:], reads=[d_u], writes=[d_dbg])
            return finish()

        Ec = P.sbuf("Ec", [128, 4, NKF, 384], BF16)
        Es = P.sbuf("Es", [128, 4, NKF, 384], BF16)
        d_E = Dep("EcEs", multi=True)
        Cb_r = Ring(P, "CbF", [128, 4096], BF16, 2)
        Sb_r = Ring(P, "SbF", [128, 4096], BF16, 2)
        Ci_r = Ring(P, "CbI", [128, NKF * 128], BF16, 2)
        Si_r = Ring(P, "SbI", [128, NKF * 128], BF16, 2)
        gb_r = Ring(P, "gb", [128, 4, 768], F32, 1)
        fb = FoldBufs()
        tm_r = [Ring(P, f"cv{i}", [128, 384], F32, 1) for i in range(4)]
        tm2_r = [Ring(P, f"cw{i}", [128, 384], F32, 1) for i in range(4)]
        Y_r = {n: Ring(P, "Y" + n, [128, 384], F32, 1) for n in ("r1", "r2", "r3", "r4", "n1", "n2", "n3", "n4")}
        I_r = {n: Ring(P, "I" + n, [128, 384], F32, 1) for n in ("U", "V", "Up", "Vp", "W", "X", "Wp", "Xp")}
        gt_r = Ring(P, "gate", [128, 384], F32, 2)
        yo_r = Ring(P, "yo", [128, 384], F32, 2)
        for o in range(2):
            for kc in range(NKF):
                Cb, d_C = Cb_r.next()
                Sb, d_S = Sb_r.next()
                P.dma("sp", Cb[:], CfF[kc], writes=[d_C])
                P.dma("sp", Sb[:], SfF[kc], writes=[d_S])
                gb, d_gb = gb_r.next()
                P.dma("sp", gb[:].rearrange("p j c -> p (j c)"), Gs[o, kc], reads=[d_Gs], writes=[d_gb])
                ejs = {"1": "dve", "2": "pool", "3": "pool", "4": "dve"}
                ab = fold_forward(fb, lambda tau: u_tok[:, tau, :], d_u, Cb, Sb, d_C, d_S, "AB", l2eng=ejs)
                Y = {}
                for j in range(4):
                    aj, d_aj = ab[f"A{j + 1}"]
                    bj, d_bj = ab[f"B{j + 1}"]
                    ej = ejs[str(j + 1)]
                    t1, t2, t3, t4 = [r.next() for r in (tm_r if ej == "dve" else tm2_r)]
                    P.tt(ej, t1[0][:, :], aj[:, :], gb[:, j, 0:384], ALU.mult, [d_aj, d_gb], [t1[1]])
                    P.tt(ej, t2[0][:, :], bj[:, :], gb[:, j, 384:768], ALU.mult, [d_bj, d_gb], [t2[1]])
                    P.tt(ej, t3[0][:, :], bj[:, :], gb[:, j, 0:384], ALU.mult, [d_bj, d_gb], [t3[1]])
                    P.tt(ej, t4[0][:, :], aj[:, :], gb[:, j, 384:768], ALU.mult, [d_aj, d_gb], [t4[1]])
                    yr, d_yr = Y_r[f"r{j + 1}"].next()
                    P.tt(ej, yr[:, :], t1[0][:, :], t2[0][:, :], ALU.add, [t1[1], t2[1]], [d_yr])
                    yn, d_yn = Y_r[f"n{j + 1}"].next()
                    P.tt(ej, yn[:, :], t3[0][:, :], t4[0][:, :], ALU.subtract, [t3[1], t4[1]], [d_yn])
                    Y[f"r{j + 1}"] = (yr, d_yr)
                    Y[f"n{j + 1}"] = (yn, d_yn)
                I = {}

                def i1(nm, x, y, op):
                    t_, d_t = I_r[nm].next()
                    P.tt(ejs[x[1]], t_[:, :], Y[x][0][:, :], Y[y][0][:, :], op, [Y[x][1], Y[y][1]], [d_t])
                    I[nm] = (t_, d_t)

                i1("U", "r2", "r3", ALU.add)
                i1("V", "r2", "r3", ALU.subtract)
                i1("Up", "n2", "n3", ALU.add)
                i1("Vp", "n2", "n3", ALU.subtract)
                i1("W", "r1", "r4", ALU.add)
                i1("X", "r1", "r4", ALU.subtract)
                i1("Wp", "n1", "n4", ALU.subtract)
                i1("Xp", "n1", "n4", ALU.add)

                def i2(dst, r, x, y, op):
                    P.tt("dve" if dst is Ec else "pool", dst[:, r, kc, :], I[x][0][:, :], I[y][0][:, :], op,
                         [I[x][1], I[y][1]], [d_E])

                i2(Ec, 0, "W", "U", ALU.add)
                i2(Ec, 1, "X", "Up", ALU.add)
                i2(Ec, 2, "W", "U", ALU.subtract)
                i2(Ec, 3, "X", "Up", ALU.subtract)
                i2(Es, 0, "Wp", "Vp", ALU.subtract)
                i2(Es, 1, "Xp", "V", ALU.add)
                i2(Es, 2, "Wp", "Vp", ALU.add)
                i2(Es, 3, "Xp", "V", ALU.subtract)
            for tau in range(NT):
                r = tau // 8
                Ci, d_Ci = Ci_r.next()
                Si, d_Si = Si_r.next()
                P.dma("sp", Ci[:], CfI[tau], writes=[d_Ci])
                P.dma("sp", Si[:], SfI[tau], writes=[d_Si])
                gt, d_gt = gt_r.next()
                P.dma("sp", gt[:], xg[o, tau * 128:(tau + 1) * 128, :], reads=[d_xg], writes=[d_gt])
                P.flush()
                by, d_by = banks.get(tau % 2)
                for kc in range(NKF):
                    P.mm(by[:, 0:384], Ci[:, kc * 128:(kc + 1) * 128], Ec[:, r, kc, :], kc == 0, False, [d_Ci, d_E], [d_by])
                    P.mm(by[:, 0:384], Si[:, kc * 128:(kc + 1) * 128], Es[:, r, kc, :], False, kc == NKF - 1,
                         [d_Si, d_E], [d_by])
                if o == 0:
                    P.tt("dve", u_tok[:, tau, :], by[:, 0:384], gt[:, :], ALU.mult, [d_by, d_gt], [d_u])
                else:
                    yo, d_yo = yo_r.next()
                    P.tt("dve", yo[:, :], by[:, 0:384], gt[:, :], ALU.mult, [d_by, d_gt], [d_yo])
                    P.store(trows(ymix, tau)[:, 384:768], yo[:, :], reads=[d_yo], writes=[d_ymix], slot=d_yo)
            if o == 0 and stop(f"HC{l}"):
                dbg_u = nc.dram_tensor("dbg_u", [NT, 128, 384], F32, kind="ExternalOutput").ap()
                d_dbg = Dep("dbg", multi=True)
                P.store(dbg_u.rearrange("t p c -> p t c"), u_tok[:], reads=[d_u], writes=[d_dbg])
                return finish()
        P.pop()
        if stop(f"H{l}"):
            return finish()

        P.push()
        wg = P.sbuf("wg", [128, 8, DFF], BF16)
        wu = P.sbuf("wu", [128, 8, DFF], BF16)
        d_wg = Dep("wg")
        d_wu = Dep("wu")
        P.push()
        wo = P.sbuf("wo", [128, 8, D], BF16)
        d_wo = Dep("wo")
        P.dma("pool", wo[:], w_out[l].rearrange("(k p) c -> p k c", p=128), writes=[d_wo])
        P.dma("pool", wg[:], w_gate[l].rearrange("(k p) c -> p k c", p=128), writes=[d_wg])
        P.dma("pool", wu[:], w_up[l].rearrange("(k p) c -> p k c", p=128), writes=[d_wu])
        ym_r = Ring(P, "ym", [128, 4, D], F32, 2)
        xtb_r = Ring(P, "xtbD", [128, 8, TB], F32, 2)
        yn_r = Ring(P, "yn", [128, D], F32, 2)
        yT_r = Ring(P, "yT", [128, 8, TB], BF16, 2)
        ssq_r = Ring(P, "ssq", [128, 12], F32, 2)
        rsd_r = Ring(P, "rsd", [128, 12], F32, 2)
        junk = P.sbuf("junk", [128, 384], BF16)
        d_junk = Dep("junk", multi=True)
        groups = [(0, 384), (384, 768), (768, 1024)]
        for b in range(NB):
            ym, d_ym = ym_r.next()
            P.dma("sp", ym[:], ymix[b * TB:(b + 1) * TB, :].rearrange("(t p) c -> p t c", p=128), reads=[d_ymix],
                  writes=[d_ym])
            xtb, d_xtb = xtb_r.next()
            P.dma("sp", xtb[:], xT[:, :, b * TB:(b + 1) * TB].rearrange("k p s -> p k s"), reads=[d_xT[b]],
                  writes=[d_xtb])
            P.flush()
            ssq, d_ssq = ssq_r.next()
            rsd, d_rsd = rsd_r.next()
            for i in range(4):
                for gi, (c0, c1) in enumerate(groups):
                    P.act(junk[:, 0:c1 - c0], ym[:, i, c0:c1], AF.Square, [d_ym], [d_junk, d_ssq],
                          accum=ssq[:, i * 3 + gi:i * 3 + gi + 1])
            sv = ssq[:, :].rearrange("p (i g) -> p i g", g=3)
            rv = rsd[:, :].rearrange("p (i g) -> p i g", g=3)
            for gi, (c0, c1) in enumerate(groups):
                P.act(rv[:, :, gi], sv[:, :, gi], AF.Sqrt, [d_ssq, d_const], [d_rsd], scale=1.0 / (c1 - c0),
                      bias=epsc[:, 0:1])
            P.recip(rsd[:, :], rsd[:, :], [d_rsd], [d_rsd])
            yT, d_yT = yT_r.next()
            for i in range(4):
                yn, d_yn = yn_r.next()
                for gi, (c0, c1) in enumerate(groups):
                    P.ts("dve" if gi < 2 else "pool", yn[:, c0:c1], ym[:, i, c0:c1], rsd[:, i * 3 + gi:i * 3 + gi + 1],
                         None, ALU.mult, None, [d_ym, d_rsd], [d_yn])
                for j in range(2):
                    bk, d_bk = banks.next()
                    for c in range(4):
                        k = j * 4 + c
                        P.tr(bk[:, c * 128:(c + 1) * 128], yn[:, k * 128:(k + 1) * 128], ident[:, :],
                             [d_yn, d_ident], [d_bk])
                    for c in range(4):
                        k = j * 4 + c
                        if c % 2:
                            P.act(yT[:, k, i * 128:(i + 1) * 128], bk[:, c * 128:(c + 1) * 128], AF.Copy,
                                  [d_bk, d_vec], [d_yT], scale=vl[:, 19 + k:20 + k])
                        else:
                            P.ts("dve", yT[:, k, i * 128:(i + 1) * 128], bk[:, c * 128:(c + 1) * 128],
                                 vl[:, 19 + k:20 + k], None, ALU.mult, None, [d_bk, d_vec], [d_yT])
            for mch in range(8):
                bk, d_bk = banks.next()
                for k in range(8):
                    P.mm(bk, wo[:, k, mch * 128:(mch + 1) * 128], yT[:, k, :], k == 0, k == 7, [d_wo, d_yT], [d_bk])
                P.tt("dve", xtb[:, mch, :], bk, xtb[:, mch, :], ALU.add, [d_bk, d_xtb], [d_xtb])
            P.store(xT[:, :, b * TB:(b + 1) * TB].rearrange("k p s -> p k s"), xtb[:], reads=[d_xtb],
                  writes=[d_xT[b]])
        P.pop()
        if stop(f"D1{l}"):
            return finish()

        wd = P.sbuf("wd", [128, 22, D], BF16)
        d_wd = Dep("wd")
        P.dma("pool", wd[:], w_down[l].rearrange("(k p) c -> p k c", p=128), writes=[d_wd])
        xtb = P.sbuf("xtbF", [128, 8, TB], F32)
        d_xtb = Dep("xtbF")
        h2T = P.sbuf("h2T", [128, 8, TB], BF16)
        d_h2 = Dep("h2T")
        actT = P.sbuf("actT", [128, 22, TB], BF16)
        d_act = Dep("actT")
        rs = P.sbuf("rsF", [128, TB], F32)
        d_rs = Dep("rsF")
        sg_r = Ring(P, "sg", [128, TB], F32, 2)
        last = (l == NL - 1)
        if last:
            ot = P.sbuf("ot", [128, 4, D], F32)
            d_ot = Dep("ot")
        for b in range(NB):
            P.dma("sp", xtb[:], xT[:, :, b * TB:(b + 1) * TB].rearrange("k p s -> p k s"), reads=[d_xT[b]],
                  writes=[d_xtb])
            rms_feature_major(xtb, d_xtb, 8, 8, vl, h2T, d_h2, actT, d_act, rs, d_rs, D)
            for f in range(22):
                bg, d_bg = banks.next()
                bu, d_bu = banks.next()
                for k in range(8):
                    P.mm(bg, wg[:, k, f * 128:(f + 1) * 128], h2T[:, k, :], k == 0, k == 7, [d_wg, d_h2], [d_bg])
                for k in range(8):
                    P.mm(bu, wu[:, k, f * 128:(f + 1) * 128], h2T[:, k, :], k == 0, k == 7, [d_wu, d_h2], [d_bu])
                sg, d_sg = sg_r.next()
                P.act(sg[:, :], bg, AF.Silu, [d_bg], [d_sg])
                P.tt("dve", actT[:, f, :], bu, sg[:, :], ALU.mult, [d_bu, d_sg], [d_act])
            for mch in range(8):
                bk, d_bk = banks.next()
                for f in range(22):
                    P.mm(bk, wd[:, f, mch * 128:(mch + 1) * 128], actT[:, f, :], f == 0, f == 21, [d_wd, d_act], [d_bk])
                P.tt("dve", xtb[:, mch, :], bk, xtb[:, mch, :], ALU.add, [d_bk, d_xtb], [d_xtb])
            if not last:
                P.store(xT[:, :, b * TB:(b + 1) * TB].rearrange("k p s -> p k s"), xtb[:], reads=[d_xtb],
                      writes=[d_xT[b]])
                P.flush()
            else:
                rms_feature_major(xtb, d_xtb, 8, 27, vl, xtb, d_xtb, actT, d_act, rs, d_rs, D)
                for i in range(4):
                    for j in range(2):
                        bk, d_bk = banks.next()
                        for c in range(4):
                            k = j * 4 + c
                            P.tr(bk[:, c * 128:(c + 1) * 128], xtb[:, k, i * 128:(i + 1) * 128], ident[:, :],
                                 [d_xtb, d_ident], [d_bk])
                        P.cp("act" if j else "dve", ot[:, i, j * 512:(j + 1) * 512], bk, [d_bk], [d_ot])
                P.store(out_ap[b * TB:(b + 1) * TB, :].rearrange("(t p) c -> p t c", p=128), ot[:], reads=[d_ot],
                      writes=[d_out])
                P.flush()
        P.pop()
        if stop(f"D2{l}"):
            return finish()

    return finish()


_CONST = {}


def host_constants():
    if _CONST:
        return _CONST
    f32 = np.float32
    c = _CONST
    c["ident"] = np.eye(128, dtype=f32)
    pos = np.arange(S, dtype=f32)
    inv = (np.float32(10000.0) ** (-np.arange(0, 32, 2, dtype=f32) / np.float32(32))).astype(f32)
    ang = (pos[:, None] * inv[None, :]).astype(f32)
    cos, sin = np.cos(ang).astype(f32), np.sin(ang).astype(f32)
    c["ropeC"] = np.ascontiguousarray(np.concatenate([cos, cos], 1).T)
    c["ropeS"] = np.ascontiguousarray(np.concatenate([-sin, sin], 1).T)
    t_idx = np.arange(S, dtype=f32)[:, None]
    t_norm = np.linspace(0.0, 1.0, S, dtype=f32)[:, None]
    bands = np.linspace(1e-4, 7, 8, dtype=f32)[None, :]
    angz = (np.float32(2.0 * math.pi) * t_idx * bands / np.float32(S)).astype(f32)
    z = np.concatenate([t_norm, np.cos(angz), np.sin(angz)], -1).astype(f32)
    c["zT"] = np.ascontiguousarray(z.T)
    deltas = np.linspace(math.log(1e-2) / 1.5, math.log(1e-2) / 0.3, 384, dtype=f32)
    c["decay"] = np.exp(-t_norm * np.abs(deltas)[None, :]).astype(f32)
    pp = np.arange(128, dtype=np.int64)
    tt = (512 * np.arange(8)[None, None, :] + 4 * pp[:, None, None] + np.arange(4)[None, :, None])
    kk = (128 * np.arange(NKF)[:, None] + pp[None, :])
    prod = (tt[None, :, :, :, None] * kk[:, None, None, None, :]) % 8192
    th = prod.astype(np.float64) * (2.0 * math.pi / 8192.0)
    c["CfF"] = np.cos(th).astype(f32).astype(ml_dtypes.bfloat16).reshape(NKF, 128, 4096)
    c["SfF"] = np.sin(th).astype(f32).astype(ml_dtypes.bfloat16).reshape(NKF, 128, 4096)
    tau = np.arange(NT)
    t2 = (512 * (tau % 8)[:, None] + 4 * pp[None, :] + (tau // 8)[:, None])
    kq = (128 * np.arange(NKF)[None, :] + pp[:, None])
    prod = (t2[:, None, None, :] * kq[None, :, :, None]) % 8192
    th = prod.astype(np.float64) * (2.0 * math.pi / 8192.0)
    c["CfI"] = np.cos(th).astype(f32).astype(ml_dtypes.bfloat16).reshape(NT, 128, NKF * 128)
    c["SfI"] = np.sin(th).astype(f32).astype(ml_dtypes.bfloat16).reshape(NT, 128, NKF * 128)
    wk = np.zeros((128, 8 * NKF), f32)
    for kc in range(NKF):
        for p in range(128):
            k = kc * 128 + p
            if k > 1024:
                continue
            orbit = [k, 2048 - k, 2048 + k, 4096 - k]
            w = [(1.0 if kp in (0, 4096) else 2.0) / 8192.0 for kp in orbit]
            if k == 0:
                w[2] = 0.0
            if k == 1024:
                w[1] = 0.0
                w[3] = 0.0
            for j in range(4):
                wk[p, kc * 4 + j] = w[j]
                wk[p, 4 * NKF + kc * 4 + j] = -w[j]
    c["wk"] = wk
    return c


def host_layout(inputs):
    f32 = np.float32
    g = {k: np.asarray(v) for k, v in inputs.items()}
    w_in = g["w_in"]
    zpad = np.zeros((NL, D, 64), f32)
    kpe = w_in[:, :, 384:416]
    kpe_sw = np.concatenate([kpe[:, :, 16:32], kpe[:, :, 0:16]], -1)
    w_inA = np.concatenate([w_in[:, :, 0:384], zpad, kpe, zpad, kpe_sw, w_in[:, :, 416:2336]], -1)
    assert w_inA.shape[-1] == WIN_COLS
    wuq = g["mla_w_uq"].reshape(NL, 256, 6, 96)
    zq = np.zeros((NL, 256, 6, 64), f32)
    w_uq2 = np.concatenate([wuq, zq, wuq[..., 80:96], wuq[..., 64:80]], -1).reshape(NL, 256, 1152)
    wkv = g["mla_w_ukv"].reshape(NL, 128, 6, 128)
    w_kv2 = np.concatenate([wkv[..., 0:64].reshape(NL, 128, 384), wkv[..., 64:128].reshape(NL, 128, 384)], -1)
    vecs = np.zeros((NL, 128, NV), f32)
    for l in range(NL):
        vecs[l, :, 0:8] = g["norm1_g"][l].reshape(8, 128).T
        vecs[l, :, 8:16] = g["norm2_g"][l].reshape(8, 128).T
        vecs[l, :, 16:18] = g["mla_q_norm_g"][l].reshape(2, 128).T
        vecs[l, :, 18:19] = g["mla_kv_norm_g"][l].reshape(1, 128).T
        vecs[l, :, 19:27] = g["mix_norm_g"][l].reshape(8, 128).T
        vecs[l, :, 27:35] = g["final_norm_g"].reshape(8, 128).T
        for j in range(3):
            vecs[l, :, 35 + j * 9:35 + (j + 1) * 9] = g["hy_conv_w"][l, j].reshape(9, 128).T
        vecs[l, :, 62:71] = g["hy_conv_b"][l].reshape(9, 128).T
        vecs[l, 0:64, 71] = g["hy_filt_b1"][l]
        vecs[l, 0:64, 72] = g["hy_filt_freq1"][l]
        vecs[l, 0:64, 73] = g["hy_filt_b2"][l]
        vecs[l, 0:64, 74] = g["hy_filt_freq2"][l]
    skipb = np.broadcast_to(g["hy_skip"].reshape(NL, 1, 768), (NL, 128, 768))
    rpb = g["na_rpb"]
    nab = np.full((NL, 5, 128, 4, 5, 128), -30000.0, f32)
    types = [(0, 0), (1, 0), (30, 27), (31, 27), (2, 0)]
    qf = np.arange(128)
    for ti, (m, c0) in enumerate(types):
        rq = 2 * m + qf // 64
        wq = qf % 64
        r0 = np.clip(rq - 4, 0, 56)
        cc0 = np.clip(wq - 8, 0, 48)
        for i in range(5):
            kt = (c0 + i) * 128 + np.arange(128)
            rk = kt // 64
            wkk = kt % 64
            inwin = ((rk[:, None] >= r0[None, :]) & (rk[:, None] < r0[None, :] + 8) &
                     (wkk[:, None] >= cc0[None, :]) & (wkk[:, None] < cc0[None, :] + 16))
            dr = np.clip(rk[:, None] - rq[None, :] + 7, 0, 14)
            dc = np.clip(wkk[:, None] - wq[None, :] + 15, 0, 30)
            for l in range(NL):
                for h in range(4):
                    vals = rpb[l, h][dr, dc]
                    nab[l, ti, :, h, i, :] = np.where(inwin, vals, f32(-30000.0))
    nabias = nab.reshape(NL, 5, 128, 2560)
    shared = dict(w_inA=w_inA, w_uq2=w_uq2, w_kv2=w_kv2, vecs=vecs, w_f1=g["hy_filt_w1"], w_f2=g["hy_filt_w2"],
                  w_f3=g["hy_filt_w3"], skipb=skipb, nabias=nabias, w_out=g["w_out"], w_gate=g["ffn_w_gate"],
                  w_up=g["ffn_w_up"], w_down=g["ffn_w_down"])
    shared = {k: np.ascontiguousarray(v, dtype=f32) for k, v in shared.items()}
    shared.update(host_constants())
    return shared


_NC = {}


def kernel(**inputs):
    shared = host_layout(inputs)
    x = np.asarray(inputs["x"], dtype=np.float32)
    if "nc" not in _NC:
        _NC["nc"] = build_program()
    nc = _NC["nc"]
    in_maps = []
    for c in range(8):
        m = dict(shared)
        m["x"] = np.ascontiguousarray(x[c])
        in_maps.append(m)
    res = run_bass_kernel_spmd(nc, in_maps, core_ids=list(range(8)))
    return np.stack([res.results[c]["out"] for c in range(8)], 0).astype(np.float32)
```

```python
import contextlib
import math
import numpy as np
import ml_dtypes
import concourse.bass as bass
import concourse.mybir as mybir
from concourse.bass_utils import run_bass_kernel_spmd

F32 = mybir.dt.float32
BF16 = mybir.dt.bfloat16
ALU = mybir.AluOpType
AF = mybir.ActivationFunctionType

S = 4096
D = 1024
NL = 2
DFF = 2816
NB = 8
TB = 512
NT = 32
NKF = 9
WIN_COLS = 2496
NV = 80
SEM_ROLL = 20000
MAGIC = 12582912.0
PI_SAFE = 3.1415925


class Dep:
    __slots__ = ("name", "writers", "readers", "multi", "dsem", "dval")
    scope = None

    def __init__(self, name="", multi=False):
        self.name = name
        self.writers = {}
        self.readers = {}
        self.multi = multi
        self.dsem = {}
        self.dval = 0
        if Dep.scope is not None:
            Dep.scope[-1].append(self)


def _merge(d, t):
    k = id(t[0])
    if k not in d or d[k][1] < t[1]:
        d[k] = t


class Prog:
    ENGS = ("pe", "act", "dve", "pool", "sp")

    def __init__(self, nc):
        self.nc = nc
        self.stacks = [contextlib.ExitStack()]
        self.q = {e: [] for e in self.ENGS}
        self.esem = {}
        self.ecnt = {e: 0 for e in self.ENGS}
        self.waited = {e: {} for e in self.ENGS}
        self.nsem = 0
        self.dma_t = {}
        self.uid = 0
        self.deferred = []
        self.free_sems = {"hw": [], "sw": []}
        Dep.scope = [[]]
        for e in self.ENGS:
            self.esem[e] = self.new_sem("c_" + e)

    def new_sem(self, name):
        self.nsem += 1
        return self.stacks[0].enter_context(self.nc.semaphore(f"s{self.nsem}_{name}"))

    def push(self):
        self.stacks.append(contextlib.ExitStack())
        Dep.scope.append([])

    def pop(self):
        self.barrier()
        self.stacks.pop().close()
        for d in Dep.scope.pop():
            for kind, sv in d.dsem.items():
                if sv[1] < SEM_ROLL:
                    self.free_sems[kind].append(sv)
            d.dsem = {}

    def acquire(self, name, kind):
        if self.free_sems[kind]:
            return self.free_sems[kind].pop()
        return [self.new_sem("d_" + kind + name), 0]

    def sbuf(self, name, shape, dt):
        self.uid += 1
        return self.stacks[-1].enter_context(self.nc.sbuf_tensor(f"{name}_{self.uid}", list(shape), dt))

    def psum(self, name, shape, dt=F32):
        return self.stacks[-1].enter_context(self.nc.psum_tensor(name, list(shape), dt))

    def _collect(self, eng, reads, writes):
        tk = {}
        for d in reads:
            for t in d.writers.values():
                _merge(tk, t)
        for d in writes:
            for t in d.readers.values():
                _merge(tk, t)
            if not d.multi:
                for t in d.writers.values():
                    _merge(tk, t)
        waits = []
        w = self.waited[eng]
        for k, (sem, val) in tk.items():
            if eng == "pe" and sem is self.esem["pe"]:
                continue
            if w.get(k, 0) >= val:
                continue
            w[k] = val
            waits.append((sem, val))
        return waits

    def _record(self, t, reads, writes):
        for d in reads:
            _merge(d.readers, t)
        for d in writes:
            if d.multi:
                _merge(d.writers, t)
            else:
                d.writers = {id(t[0]): t}
                d.readers = {}

    def op(self, eng, fn, reads=(), writes=()):
        waits = self._collect(eng, reads, writes)
        if self.ecnt[eng] >= SEM_ROLL:
            self.esem[eng] = self.new_sem("c_" + eng)
            self.ecnt[eng] = 0
        self.ecnt[eng] += 1
        t = (self.esem[eng], self.ecnt[eng])
        self.q[eng].append((waits, fn, self.esem[eng], 1))
        self._record(t, reads, writes)
        return t

    def dma(self, queue, out, in_, reads=(), writes=(), slot=None, **kw):
        waits = self._collect(queue, reads, writes)
        d0 = slot if slot is not None else writes[0]
        kind = "sw" if queue == "pool" else "hw"
        w = self.waited[queue]
        sv = d0.dsem.get(kind)
        if sv is not None and w.get(id(sv[0]), 0) < sv[1]:
            w[id(sv[0])] = sv[1]
            waits.append((sv[0], sv[1]))
        if sv is None or sv[1] >= SEM_ROLL:
            sv = self.acquire(d0.name, kind)
            d0.dsem[kind] = sv
        sv[1] += 16
        dsem = sv[0]
        t = (dsem, sv[1])
        self.dma_t[id(dsem)] = t

        def fn(e, out=out, in_=in_, kw=kw):
            return e.dma_start(out=out, in_=in_, **kw)

        self.q[queue].append((waits, fn, dsem, 16))
        self._record(t, reads, writes)
        return t

    def store(self, out, in_, reads=(), writes=(), slot=None):
        self.deferred.append((out, in_, reads, writes, slot))

    def flush(self):
        for out, in_, reads, writes, slot in self.deferred:
            self.dma("sp", out, in_, reads=reads, writes=writes, slot=slot)
        self.deferred = []

    def barrier(self):
        self.flush()
        for e in self.ENGS:
            w = self.waited[e]
            waits = []
            for e2 in self.ENGS:
                if e2 == e or self.ecnt[e2] == 0:
                    continue
                sem, val = self.esem[e2], self.ecnt[e2]
                if w.get(id(sem), 0) < val:
                    w[id(sem)] = val
                    waits.append((sem, val))
            for k, (sem, val) in self.dma_t.items():
                if w.get(k, 0) < val:
                    w[k] = val
                    waits.append((sem, val))
            self.q[e].append((waits, None, None, 0))

    def emit(self):
        nc = self.nc
        q = self.q
        with nc.Block() as block:
            def run(e, lst):
                for waits, fn, sem, inc in lst:
                    for (s, v) in waits:
                        e.wait_ge(s, v)
                    if fn is not None:
                        fn(e).then_inc(sem, inc)

            @block.tensor
            def _(e):
                run(e, q["pe"])

            @block.scalar
            def _(e):
                run(e, q["act"])

            @block.vector
            def _(e):
                run(e, q["dve"])

            @block.gpsimd
            def _(e):
                run(e, q["pool"])

            @block.sync
            def _(e):
                run(e, q["sp"])

    def close(self):
        while self.stacks:
            self.stacks.pop().close()

    def mm(self, out, lhsT, rhs, start, stop, reads, writes, skip=False):
        return self.op("pe", lambda e: e.matmul(out, lhsT, rhs, start=start, stop=stop,
                                                skip_group_check=skip), reads, writes)

    def tr(self, out, in_, ident, reads, writes):
        return self.op("pe", lambda e: e.transpose(out, in_, ident), reads, writes)

    def act(self, out, in_, func, reads, writes, scale=None, bias=None, accum=None):
        kw = {}
        if scale is not None:
            kw["scale"] = scale
        if bias is not None:
            kw["bias"] = bias
        if accum is not None:
            kw["accum_out"] = accum
        return self.op("act", lambda e: e.activation(out=out, in_=in_, func=func, **kw), reads, writes)

    def tt(self, eng, out, in0, in1, op, reads, writes):
        return self.op(eng, lambda e: e.tensor_tensor(out=out, in0=in0, in1=in1, op=op), reads, writes)

    def ts(self, eng, out, in0, s1, s2, op0, op1, reads, writes):
        if op1 is None and eng == "pool" and op0 == ALU.mult:
            op1, s2 = ALU.add, 0.0
        if op1 is None:
            return self.op(eng, lambda e: e.tensor_scalar(out=out, in0=in0, scalar1=s1, scalar2=None, op0=op0),
                           reads, writes)
        return self.op(eng, lambda e: e.tensor_scalar(out=out, in0=in0, scalar1=s1, scalar2=s2, op0=op0, op1=op1),
                       reads, writes)

    def stt(self, out, in0, scalar, in1, op0, op1, reads, writes):
        return self.op("dve", lambda e: e.scalar_tensor_tensor(out=out, in0=in0, scalar=scalar, in1=in1,
                                                               op0=op0, op1=op1), reads, writes)

    def cp(self, eng, out, in_, reads, writes):
        if eng == "act":
            return self.op("act", lambda e: e.activation(out=out, in_=in_, func=AF.Copy), reads, writes)
        return self.op(eng, lambda e: e.tensor_copy(out=out, in_=in_), reads, writes)

    def memset(self, eng, ap, val, writes):
        return self.op(eng, lambda e: e.memset(ap, val), (), writes)

    def recip(self, out, in_, reads, writes):
        return self.op("dve", lambda e: e.reciprocal(out=out, in_=in_), reads, writes)


class Ring:
    def __init__(self, P, name, shape, dt, n):
        self.t = [P.sbuf(f"{name}{i}", shape, dt) for i in range(n)]
        self.d = [Dep(f"{name}{i}") for i in range(n)]
        self.i = 0
        self.n = n

    def next(self):
        i = self.i
        self.i = (i + 1) % self.n
        return self.t[i], self.d[i]


class Banks:
    def __init__(self, P):
        self.ps = P.psum("psall", [128, 4096], F32)
        self.d = [Dep(f"bank{i}") for i in range(8)]
        self.i = 0

    def next(self):
        i = self.i
        self.i = (i + 1) % 8
        return self.ps[:, i * 512:(i + 1) * 512], self.d[i]

    def get(self, i):
        return self.ps[:, i * 512:(i + 1) * 512], self.d[i]

    def next2(self):
        if self.i % 2:
            self.i = (self.i + 1) % 8
        i = self.i
        self.i = (i + 2) % 8
        return self.ps[:, i * 512:(i + 2) * 512], self.d[i], self.d[i + 1]


def build_program(dbg=None):
    nc = bass.Bass("TRN2", target_bir_lowering=False)
    P = Prog(nc)
    skind = "ExternalOutput" if dbg else "Internal"

    def din(name, shape, dt=F32):
        return nc.dram_tensor(name, list(shape), dt, kind="ExternalInput").ap()

    def dscr(name, shape, dt=F32):
        return nc.dram_tensor(name, list(shape), dt, kind=skind).ap()

    x_in = din("x", [S, D])
    w_inA = din("w_inA", [NL, D, WIN_COLS])
    w_uq2 = din("w_uq2", [NL, 256, 1152])
    w_kv2 = din("w_kv2", [NL, 128, 768])
    vecs = din("vecs", [NL, 128, NV])
    w_f1 = din("w_f1", [NL, 17, 64])
    w_f2 = din("w_f2", [NL, 64, 64])
    w_f3 = din("w_f3", [NL, 64, 1536])
    skipb = din("skipb", [NL, 128, 768])
    nabias = din("nabias", [NL, 5, 128, 2560])
    w_out = din("w_out", [NL, D, D])
    w_gate = din("w_gate", [NL, D, DFF])
    w_up = din("w_up", [NL, D, DFF])
    w_down = din("w_down", [NL, DFF, D])
    ident_in = din("ident", [128, 128])
    ropeC = din("ropeC", [32, S])
    ropeS = din("ropeS", [32, S])
    zT_in = din("zT", [17, S])
    decay_in = din("decay", [S, 384])
    CfF = din("CfF", [NKF, 128, 4096], BF16)
    SfF = din("SfF", [NKF, 128, 4096], BF16)
    CfI = din("CfI", [NT, 128, NKF * 128], BF16)
    SfI = din("SfI", [NT, 128, NKF * 128], BF16)
    wk_in = din("wk", [128, 8 * NKF])
    out_ap = nc.dram_tensor("out", [S, D], F32, kind="ExternalOutput").ap()

    xT = dscr("xT", [8, 128, S])
    QT = dscr("QT", [6, 96, S], BF16)
    KT = dscr("KT", [6, 96, S], BF16)
    Vm = dscr("Vm", [6, 128, NT, 65], BF16)
    hyT = dscr("hyT", [9, 128, S])
    naQT = dscr("naQT", [2, 128, S], BF16)
    naKT = dscr("naKT", [2, 128, S], BF16)
    naV = dscr("naV", [NT, 128, 260], BF16)
    ymix = dscr("ymix", [S, D])
    xg = dscr("xg", [2, S, 384])
    Gs = dscr("Gs", [2, NKF, 128, 4 * 768])
    d_xT = [Dep(f"xT{b}") for b in range(NB)]
    d_QT = Dep("QT", multi=True)
    d_KT = Dep("KT", multi=True)
    d_Vm = Dep("Vm", multi=True)
    d_hyT = Dep("hyT", multi=True)
    d_naQT = Dep("naQT", multi=True)
    d_naKT = Dep("naKT", multi=True)
    d_naV = Dep("naV", multi=True)
    d_ymix = Dep("ymix", multi=True)
    d_xg = Dep("xg", multi=True)
    d_Gs = Dep("Gs", multi=True)
    d_out = Dep("out", multi=True)

    banks = Banks(P)
    ident = P.sbuf("ident", [128, 128], F32)
    d_ident = Dep("ident")
    P.dma("sp", ident[:], ident_in, writes=[d_ident])
    ones_b = P.sbuf("ones_b", [128, 128], BF16)
    d_const = Dep("const")
    P.memset("dve", ones_b[:], 1.0, [d_const])
    epsc = P.sbuf("epsc", [128, 1], F32)
    P.memset("dve", epsc[:], 1e-6, [d_const])
    vec = [P.sbuf(f"vec{l}", [128, NV], F32) for l in range(NL)]
    d_vec = Dep("vec")
    for l in range(NL):
        P.dma("sp", vec[l][:], vecs[l], writes=[d_vec])
    wkt = P.sbuf("wkt", [128, 8 * NKF], F32)
    P.dma("sp", wkt[:], wk_in, writes=[d_vec])

    def stop(name):
        return dbg == name

    def finish():
        P.barrier()
        P.emit()
        P.close()
        return nc

    def rms_p1(xt, d_xt, nch, sq, d_sq):
        P.act(sq[:, 0:nch, :], xt[:, 0:nch, :], AF.Square, [d_xt], [d_sq])

    def rms_p2(xt, d_xt, nch, gcol, vl, outT, d_out_, sq, d_sq, rs, d_rs, n_feat):
        bk, d_bk = banks.next()
        for k in range(nch):
            P.mm(bk, ones_b[:, :], sq[:, k, :], k == 0, k == nch - 1, [d_sq, d_const], [d_bk])
        P.act(rs[:, :], bk, AF.Sqrt, [d_bk, d_const], [d_rs], scale=1.0 / n_feat, bias=epsc[:, 0:1])
        P.recip(rs[:, :], rs[:, :], [d_rs], [d_rs])
        for k in range(nch):
            P.stt(outT[:, k, :], xt[:, k, :], vl[:, gcol + k:gcol + k + 1], rs[:, :], ALU.mult, ALU.mult,
                  [d_xt, d_rs, d_vec], [d_out_])

    def rms_feature_major(xt, d_xt, nch, gcol, vl, outT, d_out_, sq, d_sq, rs, d_rs, n_feat):
        rms_p1(xt, d_xt, nch, sq, d_sq)
        rms_p2(xt, d_xt, nch, gcol, vl, outT, d_out_, sq, d_sq, rs, d_rs, n_feat)

    def tcols(ap2, tau):
        return ap2.rearrange("q (a p r) -> q r a p", a=8, p=128, r=4)[:, tau // 8, tau % 8, :]

    def trows(ap2, tau):
        return ap2.rearrange("(a p r) c -> r a p c", a=8, p=128, r=4)[tau // 8, tau % 8]

    class FoldBufs:
        def __init__(self):
            self.cp = Ring(P, "fcp", [128, 384], F32, 4)
            self.l1 = {n: Ring(P, "f1" + n, [128, 384], F32, 1) for n in ("P", "Q", "Pp", "Qp", "R", "T", "Rp", "Tp")}
            self.l2 = {n: Ring(P, "f2" + n, [128, 384], F32, 1) for n in
                       ("A1", "A2", "A3", "A4", "B1", "B2", "B3", "B4")}

    def fold_forward(*a, **k):
        g = fold_forward_g(*a, **k)
        try:
            while True:
                next(g)
        except StopIteration as e:
            return e.value

    def fold_forward_g(fb, src, d_src, Cb, Sb, d_C, d_S, need, l2eng=None, bases=(0, 4), cp_eng="act"):
        Cv = Cb[:, :].rearrange("p (r a f) -> p r a f", r=4, a=8)
        Sv = Sb[:, :].rearrange("p (r a f) -> p r a f", r=4, a=8)
        L1 = {}
        hp = 0
        for ps_, (ra, rb) in enumerate(((0, 2), (1, 3))):
            for hf, (Mv, d_M) in enumerate(((Cv, d_C), (Sv, d_S))):
                if bases[0] == bases[1]:
                    bb = bases[0] + 2 * (hp % 2)
                else:
                    bb = bases[ps_] + 2 * hf
                hp += 1
                bka, bkb = banks.get(bb), banks.get(bb + 1)
                for (r, bk) in ((ra, bka), (rb, bkb)):
                    for a in range(8):
                        P.mm(bk[0][:, 0:384], Mv[:, r, a, :], src(r * 8 + a), a == 0, a == 7, [d_M, d_src], [bk[1]])
                        if a % 2:
                            yield
                cc, d_cc = fb.cp.next()
                P.cp(cp_eng, cc[:, :], bkb[0][:, 0:384], [bkb[1]], [d_cc])
                if ps_ == 0:
                    names = ("P", "Q") if hf == 0 else ("Pp", "Qp")
                else:
                    names = ("R", "T") if hf == 0 else ("Rp", "Tp")
                for nm, op in zip(names, (ALU.add, ALU.subtract)):
                    t_, d_t = fb.l1[nm].next()
                    P.tt("dve", t_[:, :], bka[0][:, 0:384], cc[:, :], op, [bka[1], d_cc], [d_t])
                    L1[nm] = (t_, d_t)
        out = {}

        def l2(nm, x, y, op):
            t_, d_t = fb.l2[nm].next()
            eng = "pool" if l2eng is None else l2eng[nm[1]]
            P.tt(eng, t_[:, :], L1[x][0][:, :], L1[y][0][:, :], op, [L1[x][1], L1[y][1]], [d_t])
            out[nm] = (t_, d_t)

        if "A" in need:
            l2("A1", "P", "R", ALU.add)
            l2("A4", "P", "R", ALU.subtract)
            l2("A2", "Q", "Tp", ALU.add)
            l2("A3", "Q", "Tp", ALU.subtract)
        if "B" in need:
            l2("B1", "Pp", "Rp", ALU.add)
            l2("B4", "Rp", "Pp", ALU.subtract)
            l2("B2", "T", "Qp", ALU.subtract)
            l2("B3", "Qp", "T", ALU.add)
        return out

    P.push()
    xin_r = Ring(P, "xin", [128, D], F32, 2)
    xtb_r = Ring(P, "xtb0", [128, 8, TB], F32, 2)
    for b in range(NB):
        xtb, d_xtb = xtb_r.next()
        for i in range(4):
            ti = b * 4 + i
            xin, d_xin = xin_r.next()
            P.dma("sp", xin[:], x_in[ti * 128:(ti + 1) * 128, :], writes=[d_xin])
            if i == 0:
                P.flush()
            for j in range(2):
                bk, d_bk = banks.next()
                for c in range(4):
                    k = j * 4 + c
                    P.tr(bk[:, c * 128:(c + 1) * 128], xin[:, k * 128:(k + 1) * 128], ident[:, :],
                         [d_xin, d_ident], [d_bk])
                P.cp("act" if j == 0 else "dve", xtb[:, j * 4:(j + 1) * 4, i * 128:(i + 1) * 128],
                     bk.rearrange("p (a b) -> p a b", a=4), [d_bk], [d_xtb])
        P.store(xT[:, :, b * TB:(b + 1) * TB].rearrange("k p s -> p k s"), xtb[:], reads=[d_xtb], writes=[d_xT[b]])
    P.pop()
    if stop("p0"):
        return finish()

    for l in range(NL):
        vl = vec[l]
        P.push()
        winA = P.sbuf("winA", [128, 8, WIN_COLS], BF16)
        d_w = Dep("winA")
        P.dma("pool", winA[:], w_inA[l].rearrange("(k p) c -> p k c", p=128), writes=[d_w])
        wuq = P.sbuf("wuq", [128, 2, 1152], BF16)
        d_wuq = Dep("wuq")
        P.dma("pool", wuq[:], w_uq2[l].rearrange("(j p) c -> p j c", p=128), writes=[d_wuq])
        wkv = P.sbuf("wkv", [128, 768], BF16)
        d_wkv = Dep("wkv")
        P.dma("pool", wkv[:], w_kv2[l], writes=[d_wkv])
        xtb_r = Ring(P, "xtbA", [128, 8, TB], F32, 2)
        sq = P.sbuf("sqA", [128, 8, TB], BF16)
        d_sq = Dep("sqA")
        rs = P.sbuf("rsA", [128, TB], F32)
        d_rs = Dep("rsA")
        sqq = P.sbuf("sqq", [128, 2, TB], BF16)
        d_sqq = Dep("sqq")
        rsq = P.sbuf("rsq", [128, TB], F32)
        d_rsq = Dep("rsq")
        sqk = P.sbuf("sqk", [128, 1, TB], BF16)
        d_sqk = Dep("sqk")
        rsk = P.sbuf("rsk", [128, TB], F32)
        d_rsk = Dep("rsk")
        hT_r = Ring(P, "hT", [128, 8, TB], BF16, 2)
        cq = P.sbuf("cq", [128, 2, TB], F32)
        d_cq = Dep("cq")
        cqn = P.sbuf("cqn", [128, 2, TB], BF16)
        d_cqn = Dep("cqn")
        ckv = P.sbuf("ckv", [128, 1, TB], F32)
        d_ckv = Dep("ckv")
        ckvn = P.sbuf("ckvn", [128, 1, TB], BF16)
        d_ckvn = Dep("ckvn")
        rC_r = Ring(P, "rC", [128, TB], F32, 2)
        rS_r = Ring(P, "rS", [128, TB], F32, 2)
        tA_r = Ring(P, "tA", [128, TB], F32, 2)
        tB_r = Ring(P, "tB", [128, TB], F32, 2)
        QTb_r = Ring(P, "QTb", [128, 6, TB], BF16, 2)
        KTb_r = Ring(P, "KTb", [128, 6, TB], BF16, 2)
        Vb_r = Ring(P, "Vb", [128, 6, 4, 65], BF16, 2)
        hyb_r = Ring(P, "hyb", [128, 3, TB], F32, 3)
        nqk_r = Ring(P, "nqk", [128, 4, TB], BF16, 2)
        nvb_r = Ring(P, "nvb", [128, 4, 260], BF16, 2)
        for r_ in (Vb_r, nvb_r):
            for t_, d_ in zip(r_.t, r_.d):
                P.memset("pool", t_[:], 1.0, [d_])
        for b in range(NB):
            sl = slice(b * TB, (b + 1) * TB)
            xtb, d_xtb = xtb_r.next()
            P.dma("sp", xtb[:], xT[:, :, sl].rearrange("k p s -> p k s"), reads=[d_xT[b]], writes=[d_xtb])
            rC, d_rC = rC_r.next()
            rS, d_rS = rS_r.next()
            P.dma("sp", rC[64:96, :], ropeC[:, sl], writes=[d_rC])
            P.dma("sp", rS[64:96, :], ropeS[:, sl], writes=[d_rS])
            P.flush()
            hT, d_hT = hT_r.next()
            rms_feature_major(xtb, d_xtb, 8, 0, vl, hT, d_hT, sq, d_sq, rs, d_rs, D)

            def proj(c0, M):
                bk, d_bk = banks.next()
                for k in range(8):
                    P.mm(bk[0:M, :], winA[:, k, c0:c0 + M], hT[:, k, :], k == 0, k == 7, [d_w, d_hT], [d_bk])
                return bk, d_bk

            for j in range(2):
                bk, d_bk = proj(j * 128, 128)
                P.cp("act", cq[:, j, :], bk, [d_bk], [d_cq])
            bk, d_bk = proj(256, 128)
            P.cp("act", ckv[:, 0, :], bk, [d_bk], [d_ckv])
            rms_p1(cq, d_cq, 2, sqq, d_sqq)
            rms_p1(ckv, d_ckv, 1, sqk, d_sqk)

            KTb, d_KTb = KTb_r.next()
            bk, d_bk = proj(384, 96)
            bk2, d_bk2 = proj(480, 96)
            tA, d_tA = tA_r.next()
            tB, d_tB = tB_r.next()
            P.tt("dve", tA[64:96, :], bk[64:96, :], rC[64:96, :], ALU.mult, [d_bk, d_rC], [d_tA])
            P.tt("dve", tB[64:96, :], bk2[64:96, :], rS[64:96, :], ALU.mult, [d_bk2, d_rS], [d_tB])
            for h in range(6):
                P.tt("pool", KTb[64:96, h, :], tA[64:96, :], tB[64:96, :], ALU.add, [d_tA, d_tB], [d_KTb])

            for g3 in range(3):
                hyb, d_hyb = hyb_r.next()
                for c3 in range(3):
                    c = g3 * 3 + c3
                    bk, d_bk = proj(576 + c * 128, 128)
                    P.cp("act" if c % 2 else "dve", hyb[:, c3, :], bk, [d_bk], [d_hyb])
                P.store(hyT[g3 * 3:(g3 + 1) * 3, :, sl].rearrange("c p s -> p c s"), hyb[:], reads=[d_hyb],
                      writes=[d_hyT], slot=d_hyb)
                if g3 == 0:
                    rms_p2(cq, d_cq, 2, 16, vl, cqn, d_cqn, sqq, d_sqq, rsq, d_rsq, 256)
                    rms_p2(ckv, d_ckv, 1, 18, vl, ckvn, d_ckvn, sqk, d_sqk, rsk, d_rsk, 128)

            nqk, d_nqk = nqk_r.next()
            for c in range(4):
                bk, d_bk = proj(1728 + c * 128, 128)
                P.cp("act" if c % 2 else "dve", nqk[:, c, :], bk, [d_bk], [d_nqk])
            P.store(naQT[:, :, sl].rearrange("c p s -> p c s"), nqk[:, 0:2, :], reads=[d_nqk], writes=[d_naQT],
                  slot=d_nqk)
            P.store(naKT[:, :, sl].rearrange("c p s -> p c s"), nqk[:, 2:4, :], reads=[d_nqk], writes=[d_naKT],
                  slot=d_nqk)
            nvb, d_nvb = nvb_r.next()
            for i in range(4):
                bk, d_bk = banks.next()
                for k in range(8):
                    P.mm(bk[:, 0:256], hT[:, k, i * 128:(i + 1) * 128], winA[:, k, 2240:2496], k == 0, k == 7,
                         [d_w, d_hT], [d_bk])
                P.cp("act" if i % 2 else "dve",
                     nvb[:, i, :].rearrange("p (h c) -> p h c", h=4)[:, :, 0:64],
                     bk[:, 0:256].rearrange("p (h c) -> p h c", h=4), [d_bk], [d_nvb])
            P.store(naV[b * 4:(b + 1) * 4].rearrange("t p c -> p t c"), nvb[:], reads=[d_nvb], writes=[d_naV],
                  slot=d_nvb)

            QTb, d_QTb = QTb_r.next()
            for h in range(6):
                bk, d_bk = banks.next()
                bk2, d_bk2 = banks.next()
                for j in range(2):
                    P.mm(bk[0:96, :], wuq[:, j, h * 192:h * 192 + 96], cqn[:, j, :], j == 0, j == 1, [d_wuq, d_cqn], [d_bk])
                for j in range(2):
                    P.mm(bk2[0:96, :], wuq[:, j, h * 192 + 96:h * 192 + 192], cqn[:, j, :], j == 0, j == 1,
                         [d_wuq, d_cqn], [d_bk2])
                P.cp("act", QTb[0:64, h, :], bk[0:64, :], [d_bk], [d_QTb])
                tA, d_tA = tA_r.next()
                tB, d_tB = tB_r.next()
                P.tt("dve", tA[64:96, :], bk[64:96, :], rC[64:96, :], ALU.mult, [d_bk, d_rC], [d_tA])
                P.tt("dve", tB[64:96, :], bk2[64:96, :], rS[64:96, :], ALU.mult, [d_bk2, d_rS], [d_tB])
                P.tt("pool", QTb[64:96, h, :], tA[64:96, :], tB[64:96, :], ALU.add, [d_tA, d_tB], [d_QTb])
            P.store(QT[:, :, sl].rearrange("h p s -> p h s"), QTb[0:96, :, :], reads=[d_QTb], writes=[d_QT],
                  slot=d_QTb)

            for h in range(6):
                bk, d_bk = banks.next()
                P.mm(bk[0:64, :], wkv[:, h * 64:(h + 1) * 64], ckvn[:, 0, :], True, True, [d_wkv, d_ckvn], [d_bk])
                P.cp("act" if h % 2 else "dve", KTb[0:64, h, :], bk[0:64, :], [d_bk], [d_KTb])
            P.store(KT[:, :, sl].rearrange("h p s -> p h s"), KTb[0:96, :, :], reads=[d_KTb], writes=[d_KT],
                  slot=d_KTb)
            Vb, d_Vb = Vb_r.next()
            for i in range(4):
                bk, d_bk = banks.next()
                P.mm(bk[:, 0:384], ckvn[:, 0, i * 128:(i + 1) * 128], wkv[:, 384:768], True, True,
                     [d_wkv, d_ckvn], [d_bk])
                P.cp("act" if i % 2 else "dve", Vb[:, :, i, 0:64],
                     bk[:, 0:384].rearrange("p (h c) -> p h c", h=6), [d_bk], [d_Vb])
            P.store(Vm[:, :, b * 4:(b + 1) * 4, :].rearrange("h p t c -> p h t c"), Vb[:], reads=[d_Vb], writes=[d_Vm],
                  slot=d_Vb)
        P.pop()
        if stop(f"A{l}"):
            return finish()

        P.push()
        hs = P.sbuf("hs", [128, NT, 768], BF16)
        hd = P.sbuf("hd", [128, NT, 768], BF16)
        d_hsd = Dep("hsd", multi=True)
        P.push()
        zT = P.sbuf("zT", [17, S], F32)
        wf1 = P.sbuf("wf1", [17, 64], F32)
        wf2 = P.sbuf("wf2", [64, 64], F32)
        wf3 = P.sbuf("wf3", [64, 1536], F32)
        d_fw = Dep("fw", multi=True)
        P.dma("sp", zT[:], zT_in, writes=[d_fw])
        P.dma("sp", wf1[:], w_f1[l], writes=[d_fw])
        P.dma("sp", wf2[:], w_f2[l], writes=[d_fw])
        P.dma("sp", wf3[:], w_f3[l], writes=[d_fw])
        hid1 = P.sbuf("hid1", [64, S], F32)
        hid2 = P.sbuf("hid2", [64, S], F32)
        d_h1 = Dep("hid1", multi=True)
        d_h2 = Dep("hid2", multi=True)
        fa_r = Ring(P, "fa", [64, TB], F32, 2)
        ft_r = Ring(P, "ft", [64, TB], F32, 2)

        def sin_block(bk, d_bk, bcol, fcol, out, d_o):
            a, d_a = fa_r.next()
            t, d_t = ft_r.next()
            P.ts("dve", a[:, :], bk[0:64, :], vl[0:64, bcol:bcol + 1], vl[0:64, fcol:fcol + 1], ALU.add, ALU.mult,
                 [d_bk, d_vec], [d_a])
            P.ts("dve", t[:, :], a[:, :], 1.0 / (2.0 * math.pi), MAGIC, ALU.mult, ALU.add, [d_a], [d_t])
            P.ts("dve", t[:, :], t[:, :], MAGIC, -2.0 * math.pi, ALU.subtract, ALU.mult, [d_t], [d_t])
            P.tt("dve", a[:, :], a[:, :], t[:, :], ALU.add, [d_a, d_t], [d_a])
            P.ts("dve", a[:, :], a[:, :], -PI_SAFE, PI_SAFE, ALU.max, ALU.min, [d_a], [d_a])
            P.act(out, a[:, :], AF.Sin, [d_a], [d_o])

        for b in range(NB):
            sl = slice(b * TB, (b + 1) * TB)
            bk, d_bk = banks.next()
            P.mm(bk[0:64, :], wf1[:, :], zT[:, sl], True, True, [d_fw], [d_bk])
            sin_block(bk, d_bk, 71, 72, hid1[:, sl], d_h1)
        for b in range(NB):
            sl = slice(b * TB, (b + 1) * TB)
            bk, d_bk = banks.next()
            P.mm(bk[0:64, :], wf2[:, :], hid1[:, sl], True, True, [d_fw, d_h1], [d_bk])
            sin_block(bk, d_bk, 73, 74, hid2[:, sl], d_h2)
        hraw_r = Ring(P, "hraw", [128, 1536], F32, 2)
        dk_r = Ring(P, "dk", [128, 384], F32, 2)
        fs_r = Ring(P, "fs", [128, 2, 384], F32, 2)
        fd_r = Ring(P, "fd", [128, 2, 384], F32, 2)
        for i in range(NT):
            hraw, d_hr = hraw_r.next()
            for n in range(3):
                bk, d_bk = banks.next()
                P.mm(bk, tcols(hid2[:, :], i), wf3[:, n * 512:(n + 1) * 512], True, True, [d_fw, d_h2], [d_bk])
                P.cp("act", hraw[:, n * 512:(n + 1) * 512], bk, [d_bk], [d_hr])
            dk, d_dk = dk_r.next()
            P.dma("sp", dk[:], trows(decay_in, i), writes=[d_dk])
            fs, d_fs = fs_r.next()
            fd, d_fd = fd_r.next()
            hv = hraw[:, :].rearrange("p (o r c) -> p o r c", o=2, r=2)
            P.tt("dve", fs[:, :, :], hv[:, :, 0, :], hv[:, :, 1, :], ALU.add, [d_hr], [d_fs])
            P.tt("dve", fd[:, :, :], hv[:, :, 0, :], hv[:, :, 1, :], ALU.subtract, [d_hr], [d_fd])
            for o in range(2):
                P.tt("pool", hs[:, i, o * 384:(o + 1) * 384], fs[:, o, :], dk[:, :], ALU.mult, [d_fs, d_dk], [d_hsd])
                P.tt("pool", hd[:, i, o * 384:(o + 1) * 384], fd[:, o, :], dk[:, :], ALU.mult, [d_fd, d_dk], [d_hsd])
        P.pop()
        if stop(f"HF{l}"):
            dbg_h = nc.dram_tensor("dbg_hs", [NT, 128, 768], F32, kind="ExternalOutput").ap()
            dbg_h2 = nc.dram_tensor("dbg_hd", [NT, 128, 768], F32, kind="ExternalOutput").ap()
            d_dbg = Dep("dbg", multi=True)
            P.store(dbg_h.rearrange("t p c -> p t c"), hs[:], reads=[d_hsd], writes=[d_dbg])
            P.store(dbg_h2.rearrange("t p c -> p t c"), hd[:], reads=[d_hsd], writes=[d_dbg])
            return finish()
        P.push()
        skb = P.sbuf("skb", [128, 768], F32)
        d_skb = Dep("skb")
        P.dma("sp", skb[:], skipb[l], writes=[d_skb])
        Cb_r = Ring(P, "CbF", [128, 4096], BF16, 1)
        Sb_r = Ring(P, "SbF", [128, 4096], BF16, 1)
        gst_r = Ring(P, "gst", [128, 4, 768], F32, 1)
        gtmp_r = Ring(P, "gtmp", [128, 384], F32, 2)
        fb = FoldBufs()

        def a2_gen():
            for kc in range(NKF):
                Cb, d_C = Cb_r.next()
                Sb, d_S = Sb_r.next()
                P.dma("sp", Cb[:], CfF[kc], writes=[d_C])
                P.dma("sp", Sb[:], SfF[kc], writes=[d_S])
                for o in range(2):
                    gst, d_gst = gst_r.next()
                    oa = yield from fold_forward_g(fb, lambda tau, o=o: hs[:, tau, o * 384:(o + 1) * 384], d_hsd,
                                                   Cb, Sb, d_C, d_S, "A", bases=(4, 4), cp_eng="dve")
                    for j in range(4):
                        gt_, d_gt_ = gtmp_r.next()
                        aj, d_aj = oa[f"A{j + 1}"]
                        P.tt("pool", gt_[:, :], aj[:, :], skb[:, o * 384:(o + 1) * 384], ALU.add, [d_aj, d_skb],
                             [d_gt_])
                        P.ts("pool", gst[:, j, 0:384], gt_[:, :], wkt[:, kc * 4 + j:kc * 4 + j + 1], None, ALU.mult,
                             None, [d_gt_, d_vec], [d_gst])
                    ob = yield from fold_forward_g(fb, lambda tau, o=o: hd[:, tau, o * 384:(o + 1) * 384], d_hsd,
                                                   Cb, Sb, d_C, d_S, "B", bases=(4, 4), cp_eng="dve")
                    for j in range(4):
                        bj, d_bj = ob[f"B{j + 1}"]
                        P.ts("pool", gst[:, j, 384:768], bj[:, :],
                             wkt[:, 4 * NKF + kc * 4 + j:4 * NKF + kc * 4 + j + 1], None, ALU.mult, None,
                             [d_bj, d_vec], [d_gst])
                    P.dma("sp", Gs[o, kc], gst[:].rearrange("p j c -> p (j c)"), reads=[d_gst], writes=[d_Gs],
                          slot=d_gst)

        a2g = a2_gen()

        def a2_step():
            try:
                next(a2g)
            except StopIteration:
                pass

        Vh_r = Ring(P, "Vh", [128, NT, 65], BF16, 2)
        KTh_r = Ring(P, "KTh", [96, S], BF16, 1)
        QTh_r = Ring(P, "QTh", [96, S], BF16, 1)
        pT_r = Ring(P, "pT", [128, TB], BF16, 6)
        rd_r = Ring(P, "rdM", [128, 4], F32, 2)
        yas_r = Ring(P, "yas", [128, 4, 64], F32, 2)
        sc_m = 1.0 / math.sqrt(96.0)
        s_rot = 0
        for h in range(6):
            KTh, d_K = KTh_r.next()
            QTh, d_Q = QTh_r.next()
            Vh, d_V = Vh_r.next()
            P.dma("sp", KTh[:], KT[h], reads=[d_KT], writes=[d_K])
            P.dma("sp", QTh[:], QT[h], reads=[d_QT], writes=[d_Q])
            P.dma("sp", Vh[:], Vm[h], reads=[d_Vm], writes=[d_V])
            for qb in range(NB):
                po, d_po = banks.get(3)
                pts = []
                for step in range(NT + 2):
                    if step < NT:
                        kt = step
                        sb, d_sb = banks.get(s_rot % 3)
                        s_rot += 1
                        P.mm(sb, KTh[:, kt * 128:(kt + 1) * 128], QTh[:, qb * TB:(qb + 1) * TB], True, True,
                             [d_K, d_Q], [d_sb])
                        pT, d_pT = pT_r.next()
                        P.act(pT[:, :], sb, AF.Exp, [d_sb], [d_pT], scale=sc_m)
                        pts.append((pT, d_pT))
                        a2_step()
                    if step >= 2:
                        kt = step - 2
                        pT, d_pT = pts[kt]
                        for j in range(4):
                            P.mm(po[:, j * 65:(j + 1) * 65], pT[:, j * 128:(j + 1) * 128],
                                 Vh[:, kt, :], kt == 0 and j == 0, kt == NT - 1 and j == 3,
                                 [d_pT, d_V], [d_po], skip=True)
                rdt, d_rd = rd_r.next()
                P.recip(rdt[:, 0:4], po[:, 0:260].rearrange("p (j c) -> p j c", c=65)[:, :, 64], [d_po], [d_rd])
                yas, d_yas = yas_r.next()
                for j in range(4):
                    P.ts("dve", yas[:, j, :], po[:, j * 65:j * 65 + 64], rdt[:, j:j + 1],
                         None, ALU.mult, None, [d_po, d_rd], [d_yas])
                P.dma("sp", ymix[qb * TB:(qb + 1) * TB, h * 64:(h + 1) * 64].rearrange("(t p) c -> p t c", p=128),
                      yas[:], reads=[d_yas], writes=[d_ymix], slot=d_yas)
        for _ in a2g:
            pass
        P.pop()
        P.pop()
        if stop(f"M{l}"):
            return finish()
        P.push()
        nq = P.sbuf("nq", [128, 2, S], BF16)
        nk = P.sbuf("nk", [128, 2, S], BF16)
        nv = P.sbuf("nv", [128, NT, 260], BF16)
        nbt = P.sbuf("nbt", [128, 5, 2560], F32)
        yc = P.sbuf("yc", [128, NT, 256], F32)
        d_nin = Dep("nin", multi=True)
        d_yc = Dep("yc", multi=True)
        P.dma("sp", nq[:], naQT.rearrange("c p s -> p c s"), reads=[d_naQT], writes=[d_nin])
        P.dma("sp", nk[:], naKT.rearrange("c p s -> p c s"), reads=[d_naKT], writes=[d_nin])
        P.dma("sp", nv[:], naV.rearrange("t p c -> p t c"), reads=[d_naV], writes=[d_nin])
        P.dma("sp", nbt[:], nabias[l].rearrange("t p c -> p t c"), writes=[d_nin])
        tS_r = Ring(P, "tS", [128, 640], F32, 4)
        pN_r = Ring(P, "pN", [128, 640], BF16, 4)
        rdn_r = Ring(P, "rdN", [128, 4], F32, 2)
        items = [(m, h) for m in range(NT) for h in range(4)]
        LOOK = 2
        pendq = []
        for idx in range(len(items) + LOOK):
            if idx < len(items):
                m, h = items[idx]
                ty = {0: 0, 1: 1, 30: 2, 31: 3}.get(m, 4)
                c0 = min(max(m - 2, 0), 27)
                j, pb = h // 2, 64 * (h % 2)
                bi = 2 * (idx % 3)
                s2 = banks.ps[:, bi * 512:(bi + 2) * 512]
                d_a, d_b = banks.d[bi], banks.d[bi + 1]
                for i in range(5):
                    P.mm(s2[:, i * 128:(i + 1) * 128], nk[pb:pb + 64, j, (c0 + i) * 128:(c0 + i + 1) * 128],
                         nq[pb:pb + 64, j, m * 128:(m + 1) * 128], True, True, [d_nin], [d_a if i < 4 else d_b])
                tS, d_tS = tS_r.next()
                P.stt(tS[:, :], s2[:, 0:640], 0.125, nbt[:, ty, h * 640:(h + 1) * 640], ALU.mult, ALU.add,
                      [d_a, d_b, d_nin], [d_tS])
                pN, d_pN = pN_r.next()
                P.act(pN[:, :], tS[:, :], AF.Exp, [d_tS], [d_pN])
                pendq.append((m, h, c0, pN, d_pN))
            if idx >= LOOK:
                m, h, c0, pN, d_pN = pendq.pop(0)
                po, d_po = banks.get(6 + m % 2)
                for i in range(5):
                    P.mm(po[:, h * 65:(h + 1) * 65], pN[:, i * 128:(i + 1) * 128], nv[:, c0 + i, h * 65:(h + 1) * 65],
                         h == 0 and i == 0, h == 3 and i == 4, [d_pN, d_nin], [d_po], skip=True)
                if h == 3:
                    rdt, d_rd = rdn_r.next()
                    P.recip(rdt[:, 0:4], po[:, 0:260].rearrange("p (j c) -> p j c", c=65)[:, :, 64], [d_po], [d_rd])
                    for hh in range(4):
                        P.ts("dve", yc[:, m, hh * 64:(hh + 1) * 64], po[:, hh * 65:hh * 65 + 64], rdt[:, hh:hh + 1],
                             None, ALU.mult, None, [d_po, d_rd], [d_yc])
        for q4 in range(4):
            P.store(ymix[q4 * 1024:(q4 + 1) * 1024, 768:1024].rearrange("(t p) c -> p t c", p=128),
                  yc[:, q4 * 8:(q4 + 1) * 8, :], reads=[d_yc], writes=[d_ymix])
        P.pop()
        if stop(f"N{l}"):
            return finish()


        P.push()
        u_tok = P.sbuf("u_tok", [128, NT, 384], BF16)
        d_u = Dep("u_tok", multi=True)
        P.push()
        hyc_r = Ring(P, "hyc", [128, S + 2], F32, 2)
        ucT_r = Ring(P, "ucT", [128, S], F32, 2)
        xst_r = Ring(P, "xst", [128, NT, 128], F32, 2)
        for t_, d_ in zip(hyc_r.t, hyc_r.d):
            P.memset("pool", t_[:, 0:1], 0.0, [d_])
            P.memset("pool", t_[:, S + 1:S + 2], 0.0, [d_])
        for c in range(9):
            hyc, d_hyc = hyc_r.next()
            P.dma("sp", hyc[:, 1:S + 1], hyT[c], reads=[d_hyT], writes=[d_hyc])
            P.flush()
            ucT, d_uc = ucT_r.next()
            w0 = vl[:, 35 + c:36 + c]
            w1 = vl[:, 44 + c:45 + c]
            w2 = vl[:, 53 + c:54 + c]
            bb = vl[:, 62 + c:63 + c]
            P.ts("dve", ucT[:, :], hyc[:, 1:S + 1], w1, bb, ALU.mult, ALU.add, [d_hyc, d_vec], [d_uc])
            P.stt(ucT[:, :], hyc[:, 0:S], w0, ucT[:, :], ALU.mult, ALU.add, [d_hyc, d_uc, d_vec], [d_uc])
            P.stt(ucT[:, :], hyc[:, 2:S + 2], w2, ucT[:, :], ALU.mult, ALU.add, [d_hyc, d_uc, d_vec], [d_uc])
            if c >= 3:
                xst, d_xst = xst_r.next()
            for g4 in range(8):
                bk, d_bk = banks.next()
                for q in range(4):
                    ti = g4 * 4 + q
                    P.tr(bk[:, q * 128:(q + 1) * 128], tcols(ucT[:, :], ti), ident[:, :],
                         [d_uc, d_ident], [d_bk])
                bv = bk.rearrange("p (a b) -> p a b", a=4)
                if c < 3:
                    P.cp("act" if g4 % 2 else "dve", u_tok[:, g4 * 4:(g4 + 1) * 4, c * 128:(c + 1) * 128], bv,
                         [d_bk], [d_u])
                else:
                    P.cp("act" if g4 % 2 else "dve", xst[:, g4 * 4:(g4 + 1) * 4, :], bv, [d_bk], [d_xst])
            if c >= 3:
                o, cc = (c - 3) // 3, (c - 3) % 3
                for q4 in range(4):
                    P.store(xg[o, q4 * 1024:(q4 + 1) * 1024, cc * 128:(cc + 1) * 128].rearrange(
                        "(t p) c -> p t c", p=128), xst[:, q4 * 8:(q4 + 1) * 8, :], reads=[d_xst], writes=[d_xg],
                          slot=d_xst)
        P.pop()
        if stop(f"HP{l}"):
            dbg_u = nc.dram_tensor("dbg_u", [NT, 128, 384], F32, kind="ExternalOutput").ap()
            d_dbg = Dep("dbg", multi=True)
            P.store(dbg_u.rearrange("t p c -> p t c"), u_tok[:], reads=[d_u], writes=[d_dbg])
            return finish()

        Ec = P.sbuf("Ec", [128, 4, NKF, 384], BF16)
        Es = P.sbuf("Es", [128, 4, NKF, 384], BF16)
        d_E = Dep("EcEs", multi=True)
        Cb_r = Ring(P, "CbF", [128, 4096], BF16, 2)
        Sb_r = Ring(P, "SbF", [128, 4096], BF16, 2)
        Ci_r = Ring(P, "CbI", [128, NKF * 128], BF16, 2)
        Si_r = Ring(P, "SbI", [128, NKF * 128], BF16, 2)
        gb_r = Ring(P, "gb", [128, 4, 768], F32, 1)
        fb = FoldBufs()
        tm_r = [Ring(P, f"cv{i}", [128, 384], F32, 1) for i in range(4)]
        tm2_r = [Ring(P, f"cw{i}", [128, 384], F32, 1) for i in range(4)]
        Y_r = {n: Ring(P, "Y" + n, [128, 384], F32, 1) for n in ("r1", "r2", "r3", "r4", "n1", "n2", "n3", "n4")}
        I_r = {n: Ring(P, "I" + n, [128, 384], F32, 1) for n in ("U", "V", "Up", "Vp", "W", "X", "Wp", "Xp")}
        gt_r = Ring(P, "gate", [128, 384], F32, 2)
        yo_r = Ring(P, "yo", [128, 384], F32, 2)
        for o in range(2):
            for kc in range(NKF):
                Cb, d_C = Cb_r.next()
                Sb, d_S = Sb_r.next()
                P.dma("sp", Cb[:], CfF[kc], writes=[d_C])
                P.dma("sp", Sb[:], SfF[kc], writes=[d_S])
                gb, d_gb = gb_r.next()
                P.dma("sp", gb[:].rearrange("p j c -> p (j c)"), Gs[o, kc], reads=[d_Gs], writes=[d_gb])
                ejs = {"1": "dve", "2": "pool", "3": "pool", "4": "dve"}
                ab = fold_forward(fb, lambda tau: u_tok[:, tau, :], d_u, Cb, Sb, d_C, d_S, "AB", l2eng=ejs)
                Y = {}
                for j in range(4):
                    aj, d_aj = ab[f"A{j + 1}"]
                    bj, d_bj = ab[f"B{j + 1}"]
                    ej = ejs[str(j + 1)]
                    t1, t2, t3, t4 = [r.next() for r in (tm_r if ej == "dve" else tm2_r)]
                    P.tt(ej, t1[0][:, :], aj[:, :], gb[:, j, 0:384], ALU.mult, [d_aj, d_gb], [t1[1]])
                    P.tt(ej, t2[0][:, :], bj[:, :], gb[:, j, 384:768], ALU.mult, [d_bj, d_gb], [t2[1]])
                    P.tt(ej, t3[0][:, :], bj[:, :], gb[:, j, 0:384], ALU.mult, [d_bj, d_gb], [t3[1]])
                    P.tt(ej, t4[0][:, :], aj[:, :], gb[:, j, 384:768], ALU.mult, [d_aj, d_gb], [t4[1]])
                    yr, d_yr = Y_r[f"r{j + 1}"].next()
                    P.tt(ej, yr[:, :], t1[0][:, :], t2[0][:, :], ALU.add, [t1[1], t2[1]], [d_yr])
                    yn, d_yn = Y_r[f"n{j + 1}"].next()
                    P.tt(ej, yn[:, :], t3[0][:, :], t4[0][:, :], ALU.subtract, [t3[1], t4[1]], [d_yn])
                    Y[f"r{j + 1}"] = (yr, d_yr)
                    Y[f"n{j + 1}"] = (yn, d_yn)
                I = {}

                def i1(nm, x, y, op):
                    t_, d_t = I_r[nm].next()
                    P.tt(ejs[x[1]], t_[:, :], Y[x][0][:, :], Y[y][0][:, :], op, [Y[x][1], Y[y][1]], [d_t])
                    I[nm] = (t_, d_t)

                i1("U", "r2", "r3", ALU.add)
                i1("V", "r2", "r3", ALU.subtract)
                i1("Up", "n2", "n3", ALU.add)
                i1("Vp", "n2", "n3", ALU.subtract)
                i1("W", "r1", "r4", ALU.add)
                i1("X", "r1", "r4", ALU.subtract)
                i1("Wp", "n1", "n4", ALU.subtract)
                i1("Xp", "n1", "n4", ALU.add)

                def i2(dst, r, x, y, op):
                    P.tt("dve" if dst is Ec else "pool", dst[:, r, kc, :], I[x][0][:, :], I[y][0][:, :], op,
                         [I[x][1], I[y][1]], [d_E])

                i2(Ec, 0, "W", "U", ALU.add)
                i2(Ec, 1, "X", "Up", ALU.add)
                i2(Ec, 2, "W", "U", ALU.subtract)
                i2(Ec, 3, "X", "Up", ALU.subtract)
                i2(Es, 0, "Wp", "Vp", ALU.subtract)
                i2(Es, 1, "Xp", "V", ALU.add)
                i2(Es, 2, "Wp", "Vp", ALU.add)
                i2(Es, 3, "Xp", "V", ALU.subtract)
            for tau in range(NT):
                r = tau // 8
                Ci, d_Ci = Ci_r.next()
                Si, d_Si = Si_r.next()
                P.dma("sp", Ci[:], CfI[tau], writes=[d_Ci])
                P.dma("sp", Si[:], SfI[tau], writes=[d_Si])
                gt, d_gt = gt_r.next()
                P.dma("sp", gt[:], xg[o, tau * 128:(tau + 1) * 128, :], reads=[d_xg], writes=[d_gt])
                P.flush()
                by, d_by = banks.get(tau % 2)
                for kc in range(NKF):
                    P.mm(by[:, 0:384], Ci[:, kc * 128:(kc + 1) * 128], Ec[:, r, kc, :], kc == 0, False, [d_Ci, d_E], [d_by])
                    P.mm(by[:, 0:384], Si[:, kc * 128:(kc + 1) * 128], Es[:, r, kc, :], False, kc == NKF - 1,
                         [d_Si, d_E], [d_by])
                if o == 0:
                    P.tt("dve", u_tok[:, tau, :], by[:, 0:384], gt[:, :], ALU.mult, [d_by, d_gt], [d_u])
                else:
                    yo, d_yo = yo_r.next()
                    P.tt("dve", yo[:, :], by[:, 0:384], gt[:, :], ALU.mult, [d_by, d_gt], [d_yo])
                    P.store(trows(ymix, tau)[:, 384:768], yo[:, :], reads=[d_yo], writes=[d_ymix], slot=d_yo)
            if o == 0 and stop(f"HC{l}"):
                dbg_u = nc.dram_tensor("dbg_u", [NT, 128, 384], F32, kind="ExternalOutput").ap()
                d_dbg = Dep("dbg", multi=True)
                P.store(dbg_u.rearrange("t p c -> p t c"), u_tok[:], reads=[d_u], writes=[d_dbg])
                return finish()
        P.pop()
        if stop(f"H{l}"):
            return finish()

        P.push()
        wg = P.sbuf("wg", [128, 8, DFF], BF16)
        wu = P.sbuf("wu", [128, 8, DFF], BF16)
        d_wg = Dep("wg")
        d_wu = Dep("wu")
        P.push()
        wo = P.sbuf("wo", [128, 8, D], BF16)
        d_wo = Dep("wo")
        P.dma("pool", wo[:], w_out[l].rearrange("(k p) c -> p k c", p=128), writes=[d_wo])
        P.dma("pool", wg[:], w_gate[l].rearrange("(k p) c -> p k c", p=128), writes=[d_wg])
        P.dma("pool", wu[:], w_up[l].rearrange("(k p) c -> p k c", p=128), writes=[d_wu])
        ym_r = Ring(P, "ym", [128, 4, D], F32, 2)
        xtb_r = Ring(P, "xtbD", [128, 8, TB], F32, 2)
        yn_r = Ring(P, "yn", [128, D], F32, 2)
        yT_r = Ring(P, "yT", [128, 8, TB], BF16, 2)
        ssq_r = Ring(P, "ssq", [128, 12], F32, 2)
        rsd_r = Ring(P, "rsd", [128, 12], F32, 2)
        junk = P.sbuf("junk", [128, 384], BF16)
        d_junk = Dep("junk", multi=True)
        groups = [(0, 384), (384, 768), (768, 1024)]
        for b in range(NB):
            ym, d_ym = ym_r.next()
            P.dma("sp", ym[:], ymix[b * TB:(b + 1) * TB, :].rearrange("(t p) c -> p t c", p=128), reads=[d_ymix],
                  writes=[d_ym])
            xtb, d_xtb = xtb_r.next()
            P.dma("sp", xtb[:], xT[:, :, b * TB:(b + 1) * TB].rearrange("k p s -> p k s"), reads=[d_xT[b]],
                  writes=[d_xtb])
            P.flush()
            ssq, d_ssq = ssq_r.next()
            rsd, d_rsd = rsd_r.next()
            for i in range(4):
                for gi, (c0, c1) in enumerate(groups):
                    P.act(junk[:, 0:c1 - c0], ym[:, i, c0:c1], AF.Square, [d_ym], [d_junk, d_ssq],
                          accum=ssq[:, i * 3 + gi:i * 3 + gi + 1])
            sv = ssq[:, :].rearrange("p (i g) -> p i g", g=3)
            rv = rsd[:, :].rearrange("p (i g) -> p i g", g=3)
            for gi, (c0, c1) in enumerate(groups):
                P.act(rv[:, :, gi], sv[:, :, gi], AF.Sqrt, [d_ssq, d_const], [d_rsd], scale=1.0 / (c1 - c0),
                      bias=epsc[:, 0:1])
            P.recip(rsd[:, :], rsd[:, :], [d_rsd], [d_rsd])
            yT, d_yT = yT_r.next()
            for i in range(4):
                yn, d_yn = yn_r.next()
                for gi, (c0, c1) in enumerate(groups):
                    P.ts("dve" if gi < 2 else "pool", yn[:, c0:c1], ym[:, i, c0:c1], rsd[:, i * 3 + gi:i * 3 + gi + 1],
                         None, ALU.mult, None, [d_ym, d_rsd], [d_yn])
                for j in range(2):
                    bk, d_bk = banks.next()
                    for c in range(4):
                        k = j * 4 + c
                        P.tr(bk[:, c * 128:(c + 1) * 128], yn[:, k * 128:(k + 1) * 128], ident[:, :],
                             [d_yn, d_ident], [d_bk])
                    for c in range(4):
                        k = j * 4 + c
                        if c % 2:
                            P.act(yT[:, k, i * 128:(i + 1) * 128], bk[:, c * 128:(c + 1) * 128], AF.Copy,
                                  [d_bk, d_vec], [d_yT], scale=vl[:, 19 + k:20 + k])
                        else:
                            P.ts("dve", yT[:, k, i * 128:(i + 1) * 128], bk[:, c * 128:(c + 1) * 128],
                                 vl[:, 19 + k:20 + k], None, ALU.mult, None, [d_bk, d_vec], [d_yT])
            for mch in range(8):
                bk, d_bk = banks.next()
                for k in range(8):
                    P.mm(bk, wo[:, k, mch * 128:(mch + 1) * 128], yT[:, k, :], k == 0, k == 7, [d_wo, d_yT], [d_bk])
                P.tt("dve", xtb[:, mch, :], bk, xtb[:, mch, :], ALU.add, [d_bk, d_xtb], [d_xtb])
            P.store(xT[:, :, b * TB:(b + 1) * TB].rearrange("k p s -> p k s"), xtb[:], reads=[d_xtb],
                  writes=[d_xT[b]])
        P.pop()
        if stop(f"D1{l}"):
            return finish()

        wd = P.sbuf("wd", [128, 22, D], BF16)
        d_wd = Dep("wd")
        P.dma("pool", wd[:], w_down[l].rearrange("(k p) c -> p k c", p=128), writes=[d_wd])
        xtb = P.sbuf("xtbF", [128, 8, TB], F32)
        d_xtb = Dep("xtbF")
        h2T = P.sbuf("h2T", [128, 8, TB], BF16)
        d_h2 = Dep("h2T")
        actT = P.sbuf("actT", [128, 22, TB], BF16)
        d_act = Dep("actT")
        rs = P.sbuf("rsF", [128, TB], F32)
        d_rs = Dep("rsF")
        sg_r = Ring(P, "sg", [128, TB], F32, 2)
        last = (l == NL - 1)
        if last:
            ot = P.sbuf("ot", [128, 4, D], F32)
            d_ot = Dep("ot")
        for b in range(NB):
            P.dma("sp", xtb[:], xT[:, :, b * TB:(b + 1) * TB].rearrange("k p s -> p k s"), reads=[d_xT[b]],
                  writes=[d_xtb])
            rms_feature_major(xtb, d_xtb, 8, 8, vl, h2T, d_h2, actT, d_act, rs, d_rs, D)
            for f in range(22):
                bg, d_bg = banks.next()
                bu, d_bu = banks.next()
                for k in range(8):
                    P.mm(bg, wg[:, k, f * 128:(f + 1) * 128], h2T[:, k, :], k == 0, k == 7, [d_wg, d_h2], [d_bg])
                for k in range(8):
                    P.mm(bu, wu[:, k, f * 128:(f + 1) * 128], h2T[:, k, :], k == 0, k == 7, [d_wu, d_h2], [d_bu])
                sg, d_sg = sg_r.next()
                P.act(sg[:, :], bg, AF.Silu, [d_bg], [d_sg])
                P.tt("dve", actT[:, f, :], bu, sg[:, :], ALU.mult, [d_bu, d_sg], [d_act])
            for mch in range(8):
                bk, d_bk = banks.next()
                for f in range(22):
                    P.mm(bk, wd[:, f, mch * 128:(mch + 1) * 128], actT[:, f, :], f == 0, f == 21, [d_wd, d_act], [d_bk])
                P.tt("dve", xtb[:, mch, :], bk, xtb[:, mch, :], ALU.add, [d_bk, d_xtb], [d_xtb])
            if not last:
                P.store(xT[:, :, b * TB:(b + 1) * TB].rearrange("k p s -> p k s"), xtb[:], reads=[d_xtb],
                      writes=[d_xT[b]])
                P.flush()
            else:
                rms_feature_major(xtb, d_xtb, 8, 27, vl, xtb, d_xtb, actT, d_act, rs, d_rs, D)
                for i in range(4):
                    for j in range(2):
                        bk, d_bk = banks.next()
                        for c in range(4):
                            k = j * 4 + c
                            P.tr(bk[:, c * 128:(c + 1) * 128], xtb[:, k, i * 128:(i + 1) * 128], ident[:, :],
                                 [d_xtb, d_ident], [d_bk])
                        P.cp("act" if j else "dve", ot[:, i, j * 512:(j + 1) * 512], bk, [d_bk], [d_ot])
                P.store(out_ap[b * TB:(b + 1) * TB, :].rearrange("(t p) c -> p t c", p=128), ot[:], reads=[d_ot],
                      writes=[d_out])
                P.flush()
        P.pop()
        if stop(f"D2{l}"):
            return finish()

    return finish()


_CONST = {}


def host_constants():
    if _CONST:
        return _CONST
    f32 = np.float32
    c = _CONST
    c["ident"] = np.eye(128, dtype=f32)
    pos = np.arange(S, dtype=f32)
    inv = (np.float32(10000.0) ** (-np.arange(0, 32, 2, dtype=f32) / np.float32(32))).astype(f32)
    ang = (pos[:, None] * inv[None, :]).astype(f32)
    cos, sin = np.cos(ang).astype(f32), np.sin(ang).astype(f32)
    c["ropeC"] = np.ascontiguousarray(np.concatenate([cos, cos], 1).T)
    c["ropeS"] = np.ascontiguousarray(np.concatenate([-sin, sin], 1).T)
    t_idx = np.arange(S, dtype=f32)[:, None]
    t_norm = np.linspace(0.0, 1.0, S, dtype=f32)[:, None]
    bands = np.linspace(1e-4, 7, 8, dtype=f32)[None, :]
    angz = (np.float32(2.0 * math.pi) * t_idx * bands / np.float32(S)).astype(f32)
    z = np.concatenate([t_norm, np.cos(angz), np.sin(angz)], -1).astype(f32)
    c["zT"] = np.ascontiguousarray(z.T)
    deltas = np.linspace(math.log(1e-2) / 1.5, math.log(1e-2) / 0.3, 384, dtype=f32)
    c["decay"] = np.exp(-t_norm * np.abs(deltas)[None, :]).astype(f32)
    pp = np.arange(128, dtype=np.int64)
    tt = (512 * np.arange(8)[None, None, :] + 4 * pp[:, None, None] + np.arange(4)[None, :, None])
    kk = (128 * np.arange(NKF)[:, None] + pp[None, :])
    prod = (tt[None, :, :, :, None] * kk[:, None, None, None, :]) % 8192
    th = prod.astype(np.float64) * (2.0 * math.pi / 8192.0)
    c["CfF"] = np.cos(th).astype(f32).astype(ml_dtypes.bfloat16).reshape(NKF, 128, 4096)
    c["SfF"] = np.sin(th).astype(f32).astype(ml_dtypes.bfloat16).reshape(NKF, 128, 4096)
    tau = np.arange(NT)
    t2 = (512 * (tau % 8)[:, None] + 4 * pp[None, :] + (tau // 8)[:, None])
    kq = (128 * np.arange(NKF)[None, :] + pp[:, None])
    prod = (t2[:, None, None, :] * kq[None, :, :, None]) % 8192
    th = prod.astype(np.float64) * (2.0 * math.pi / 8192.0)
    c["CfI"] = np.cos(th).astype(f32).astype(ml_dtypes.bfloat16).reshape(NT, 128, NKF * 128)
    c["SfI"] = np.sin(th).astype(f32).astype(ml_dtypes.bfloat16).reshape(NT, 128, NKF * 128)
    wk = np.zeros((128, 8 * NKF), f32)
    for kc in range(NKF):
        for p in range(128):
            k = kc * 128 + p
            if k > 1024:
                continue
            orbit = [k, 2048 - k, 2048 + k, 4096 - k]
            w = [(1.0 if kp in (0, 4096) else 2.0) / 8192.0 for kp in orbit]
            if k == 0:
                w[2] = 0.0
            if k == 1024:
                w[1] = 0.0
                w[3] = 0.0
            for j in range(4):
                wk[p, kc * 4 + j] = w[j]
                wk[p, 4 * NKF + kc * 4 + j] = -w[j]
    c["wk"] = wk
    return c


def host_layout(inputs):
    f32 = np.float32
    g = {k: np.asarray(v) for k, v in inputs.items()}
    w_in = g["w_in"]
    zpad = np.zeros((NL, D, 64), f32)
    kpe = w_in[:, :, 384:416]
    kpe_sw = np.concatenate([kpe[:, :, 16:32], kpe[:, :, 0:16]], -1)
    w_inA = np.concatenate([w_in[:, :, 0:384], zpad, kpe, zpad, kpe_sw, w_in[:, :, 416:2336]], -1)
    assert w_inA.shape[-1] == WIN_COLS
    wuq = g["mla_w_uq"].reshape(NL, 256, 6, 96)
    zq = np.zeros((NL, 256, 6, 64), f32)
    w_uq2 = np.concatenate([wuq, zq, wuq[..., 80:96], wuq[..., 64:80]], -1).reshape(NL, 256, 1152)
    wkv = g["mla_w_ukv"].reshape(NL, 128, 6, 128)
    w_kv2 = np.concatenate([wkv[..., 0:64].reshape(NL, 128, 384), wkv[..., 64:128].reshape(NL, 128, 384)], -1)
    vecs = np.zeros((NL, 128, NV), f32)
    for l in range(NL):
        vecs[l, :, 0:8] = g["norm1_g"][l].reshape(8, 128).T
        vecs[l, :, 8:16] = g["norm2_g"][l].reshape(8, 128).T
        vecs[l, :, 16:18] = g["mla_q_norm_g"][l].reshape(2, 128).T
        vecs[l, :, 18:19] = g["mla_kv_norm_g"][l].reshape(1, 128).T
        vecs[l, :, 19:27] = g["mix_norm_g"][l].reshape(8, 128).T
        vecs[l, :, 27:35] = g["final_norm_g"].reshape(8, 128).T
        for j in range(3):
            vecs[l, :, 35 + j * 9:35 + (j + 1) * 9] = g["hy_conv_w"][l, j].reshape(9, 128).T
        vecs[l, :, 62:71] = g["hy_conv_b"][l].reshape(9, 128).T
        vecs[l, 0:64, 71] = g["hy_filt_b1"][l]
        vecs[l, 0:64, 72] = g["hy_filt_freq1"][l]
        vecs[l, 0:64, 73] = g["hy_filt_b2"][l]
        vecs[l, 0:64, 74] = g["hy_filt_freq2"][l]
    skipb = np.broadcast_to(g["hy_skip"].reshape(NL, 1, 768), (NL, 128, 768))
    rpb = g["na_rpb"]
    nab = np.full((NL, 5, 128, 4, 5, 128), -30000.0, f32)
    types = [(0, 0), (1, 0), (30, 27), (31, 27), (2, 0)]
    qf = np.arange(128)
    for ti, (m, c0) in enumerate(types):
        rq = 2 * m + qf // 64
        wq = qf % 64
        r0 = np.clip(rq - 4, 0, 56)
        cc0 = np.clip(wq - 8, 0, 48)
        for i in range(5):
            kt = (c0 + i) * 128 + np.arange(128)
            rk = kt // 64
            wkk = kt % 64
            inwin = ((rk[:, None] >= r0[None, :]) & (rk[:, None] < r0[None, :] + 8) &
                     (wkk[:, None] >= cc0[None, :]) & (wkk[:, None] < cc0[None, :] + 16))
            dr = np.clip(rk[:, None] - rq[None, :] + 7, 0, 14)
            dc = np.clip(wkk[:, None] - wq[None, :] + 15, 0, 30)
            for l in range(NL):
                for h in range(4):
                    vals = rpb[l, h][dr, dc]
                    nab[l, ti, :, h, i, :] = np.where(inwin, vals, f32(-30000.0))
    nabias = nab.reshape(NL, 5, 128, 2560)
    shared = dict(w_inA=w_inA, w_uq2=w_uq2, w_kv2=w_kv2, vecs=vecs, w_f1=g["hy_filt_w1"], w_f2=g["hy_filt_w2"],
                  w_f3=g["hy_filt_w3"], skipb=skipb, nabias=nabias, w_out=g["w_out"], w_gate=g["ffn_w_gate"],
                  w_up=g["ffn_w_up"], w_down=g["ffn_w_down"])
    shared = {k: np.ascontiguousarray(v, dtype=f32) for k, v in shared.items()}
    shared.update(host_constants())
    return shared


_NC = {}


def kernel(**inputs):
    shared = host_layout(inputs)
    x = np.asarray(inputs["x"], dtype=np.float32)
    if "nc" not in _NC:
        _NC["nc"] = build_program()
    nc = _NC["nc"]
    in_maps = []
    for c in range(8):
        m = dict(shared)
        m["x"] = np.ascontiguousarray(x[c])
        in_maps.append(m)
    res = run_bass_kernel_spmd(nc, in_maps, core_ids=list(range(8)))
    return np.stack([res.results[c]["out"] for c in range(8)], 0).astype(np.float32)
```

```python
import contextlib
import math
import numpy as np
import ml_dtypes
import concourse.bass as bass
import concourse.mybir as mybir
from concourse.bass_utils import run_bass_kernel_spmd

F32 = mybir.dt.float32
BF16 = mybir.dt.bfloat16
ALU = mybir.AluOpType
AF = mybir.ActivationFunctionType

S = 4096
D = 1024
NL = 2
DFF = 2816
NB = 8
TB = 512
NT = 32
NKF = 9
WIN_COLS = 2496
NV = 80
SEM_ROLL = 20000
MAGIC = 12582912.0
PI_SAFE = 3.1415925


class Dep:
    __slots__ = ("name", "writers", "readers", "multi", "dsem", "dval")
    scope = None

    def __init__(self, name="", multi=False):
        self.name = name
        self.writers = {}
        self.readers = {}
        self.multi = multi
        self.dsem = {}
        self.dval = 0
        if Dep.scope is not None:
            Dep.scope[-1].append(self)


def _merge(d, t):
    k = id(t[0])
    if k not in d or d[k][1] < t[1]:
        d[k] = t


class Prog:
    ENGS = ("pe", "act", "dve", "pool", "sp")

    def __init__(self, nc):
        self.nc = nc
        self.stacks = [contextlib.ExitStack()]
        self.q = {e: [] for e in self.ENGS}
        self.esem = {}
        self.ecnt = {e: 0 for e in self.ENGS}
        self.waited = {e: {} for e in self.ENGS}
        self.nsem = 0
        self.dma_t = {}
        self.uid = 0
        self.deferred = []
        self.free_sems = {"hw": [], "sw": []}
        Dep.scope = [[]]
        for e in self.ENGS:
            self.esem[e] = self.new_sem("c_" + e)

    def new_sem(self, name):
        self.nsem += 1
        return self.stacks[0].enter_context(self.nc.semaphore(f"s{self.nsem}_{name}"))

    def push(self):
        self.stacks.append(contextlib.ExitStack())
        Dep.scope.append([])

    def pop(self):
        self.barrier()
        self.stacks.pop().close()
        for d in Dep.scope.pop():
            for kind, sv in d.dsem.items():
                if sv[1] < SEM_ROLL:
                    self.free_sems[kind].append(sv)
            d.dsem = {}

    def acquire(self, name, kind):
        if self.free_sems[kind]:
            return self.free_sems[kind].pop()
        return [self.new_sem("d_" + kind + name), 0]

    def sbuf(self, name, shape, dt):
        self.uid += 1
        return self.stacks[-1].enter_context(self.nc.sbuf_tensor(f"{name}_{self.uid}", list(shape), dt))

    def psum(self, name, shape, dt=F32):
        return self.stacks[-1].enter_context(self.nc.psum_tensor(name, list(shape), dt))

    def _collect(self, eng, reads, writes):
        tk = {}
        for d in reads:
            for t in d.writers.values():
                _merge(tk, t)
        for d in writes:
            for t in d.readers.values():
                _merge(tk, t)
            if not d.multi:
                for t in d.writers.values():
                    _merge(tk, t)
        waits = []
        w = self.waited[eng]
        for k, (sem, val) in tk.items():
            if eng == "pe" and sem is self.esem["pe"]:
                continue
            if w.get(k, 0) >= val:
                continue
            w[k] = val
            waits.append((sem, val))
        return waits

    def _record(self, t, reads, writes):
        for d in reads:
            _merge(d.readers, t)
        for d in writes:
            if d.multi:
                _merge(d.writers, t)
            else:
                d.writers = {id(t[0]): t}
                d.readers = {}

    def op(self, eng, fn, reads=(), writes=()):
        waits = self._collect(eng, reads, writes)
        if self.ecnt[eng] >= SEM_ROLL:
            self.esem[eng] = self.new_sem("c_" + eng)
            self.ecnt[eng] = 0
        self.ecnt[eng] += 1
        t = (self.esem[eng], self.ecnt[eng])
        self.q[eng].append((waits, fn, self.esem[eng], 1))
        self._record(t, reads, writes)
        return t

    def dma(self, queue, out, in_, reads=(), writes=(), slot=None, **kw):
        waits = self._collect(queue, reads, writes)
        d0 = slot if slot is not None else writes[0]
        kind = "sw" if queue == "pool" else "hw"
        w = self.waited[queue]
        sv = d0.dsem.get(kind)
        if sv is not None and w.get(id(sv[0]), 0) < sv[1]:
            w[id(sv[0])] = sv[1]
            waits.append((sv[0], sv[1]))
        if sv is None or sv[1] >= SEM_ROLL:
            sv = self.acquire(d0.name, kind)
            d0.dsem[kind] = sv
        sv[1] += 16
        dsem = sv[0]
        t = (dsem, sv[1])
        self.dma_t[id(dsem)] = t

        def fn(e, out=out, in_=in_, kw=kw):
            return e.dma_start(out=out, in_=in_, **kw)

        self.q[queue].append((waits, fn, dsem, 16))
        self._record(t, reads, writes)
        return t

    def store(self, out, in_, reads=(), writes=(), slot=None):
        self.deferred.append((out, in_, reads, writes, slot))

    def flush(self):
        for out, in_, reads, writes, slot in self.deferred:
            self.dma("sp", out, in_, reads=reads, writes=writes, slot=slot)
        self.deferred = []

    def barrier(self):
        self.flush()
        for e in self.ENGS:
            w = self.waited[e]
            waits = []
            for e2 in self.ENGS:
                if e2 == e or self.ecnt[e2] == 0:
                    continue
                sem, val = self.esem[e2], self.ecnt[e2]
                if w.get(id(sem), 0) < val:
                    w[id(sem)] = val
                    waits.append((sem, val))
            for k, (sem, val) in self.dma_t.items():
                if w.get(k, 0) < val:
                    w[k] = val
                    waits.append((sem, val))
            self.q[e].append((waits, None, None, 0))

    def emit(self):
        nc = self.nc
        q = self.q
        with nc.Block() as block:
            def run(e, lst):
                for waits, fn, sem, inc in lst:
                    for (s, v) in waits:
                        e.wait_ge(s, v)
                    if fn is not None:
                        fn(e).then_inc(sem, inc)

            @block.tensor
            def _(e):
                run(e, q["pe"])

            @block.scalar
            def _(e):
                run(e, q["act"])

            @block.vector
            def _(e):
                run(e, q["dve"])

            @block.gpsimd
            def _(e):
                run(e, q["pool"])

            @block.sync
            def _(e):
                run(e, q["sp"])

    def close(self):
        while self.stacks:
            self.stacks.pop().close()

    def mm(self, out, lhsT, rhs, start, stop, reads, writes, skip=False):
        return self.op("pe", lambda e: e.matmul(out, lhsT, rhs, start=start, stop=stop,
                                                skip_group_check=skip), reads, writes)

    def tr(self, out, in_, ident, reads, writes):
        return self.op("pe", lambda e: e.transpose(out, in_, ident), reads, writes)

    def act(self, out, in_, func, reads, writes, scale=None, bias=None, accum=None):
        kw = {}
        if scale is not None:
            kw["scale"] = scale
        if bias is not None:
            kw["bias"] = bias
        if accum is not None:
            kw["accum_out"] = accum
        return self.op("act", lambda e: e.activation(out=out, in_=in_, func=func, **kw), reads, writes)

    def tt(self, eng, out, in0, in1, op, reads, writes):
        return self.op(eng, lambda e: e.tensor_tensor(out=out, in0=in0, in1=in1, op=op), reads, writes)

    def ts(self, eng, out, in0, s1, s2, op0, op1, reads, writes):
        if op1 is None and eng == "pool" and op0 == ALU.mult:
            op1, s2 = ALU.add, 0.0
        if op1 is None:
            return self.op(eng, lambda e: e.tensor_scalar(out=out, in0=in0, scalar1=s1, scalar2=None, op0=op0),
                           reads, writes)
        return self.op(eng, lambda e: e.tensor_scalar(out=out, in0=in0, scalar1=s1, scalar2=s2, op0=op0, op1=op1),
                       reads, writes)

    def stt(self, out, in0, scalar, in1, op0, op1, reads, writes):
        return self.op("dve", lambda e: e.scalar_tensor_tensor(out=out, in0=in0, scalar=scalar, in1=in1,
                                                               op0=op0, op1=op1), reads, writes)

    def cp(self, eng, out, in_, reads, writes):
        if eng == "act":
            return self.op("act", lambda e: e.activation(out=out, in_=in_, func=AF.Copy), reads, writes)
        return self.op(eng, lambda e: e.tensor_copy(out=out, in_=in_), reads, writes)

    def memset(self, eng, ap, val, writes):
        return self.op(eng, lambda e: e.memset(ap, val), (), writes)

    def recip(self, out, in_, reads, writes):
        return self.op("dve", lambda e: e.reciprocal(out=out, in_=in_), reads, writes)


class Ring:
    def __init__(self, P, name, shape, dt, n):
        self.t = [P.sbuf(f"{name}{i}", shape, dt) for i in range(n)]
        self.d = [Dep(f"{name}{i}") for i in range(n)]
        self.i = 0
        self.n = n

    def next(self):
        i = self.i
        self.i = (i + 1) % self.n
        return self.t[i], self.d[i]


class Banks:
    def __init__(self, P):
        self.ps = P.psum("psall", [128, 4096], F32)
        self.d = [Dep(f"bank{i}") for i in range(8)]
        self.i = 0

    def next(self):
        i = self.i
        self.i = (i + 1) % 8
        return self.ps[:, i * 512:(i + 1) * 512], self.d[i]

    def get(self, i):
        return self.ps[:, i * 512:(i + 1) * 512], self.d[i]

    def next2(self):
        if self.i % 2:
            self.i = (self.i + 1) % 8
        i = self.i
        self.i = (i + 2) % 8
        return self.ps[:, i * 512:(i + 2) * 512], self.d[i], self.d[i + 1]


def build_program(dbg=None):
    nc = bass.Bass("TRN2", target_bir_lowering=False)
    P = Prog(nc)
    skind = "ExternalOutput" if dbg else "Internal"

    def din(name, shape, dt=F32):
        return nc.dram_tensor(name, list(shape), dt, kind="ExternalInput").ap()

    def dscr(name, shape, dt=F32):
        return nc.dram_tensor(name, list(shape), dt, kind=skind).ap()

    x_in = din("x", [S, D])
    w_inA = din("w_inA", [NL, D, WIN_COLS])
    w_uq2 = din("w_uq2", [NL, 256, 1152])
    w_kv2 = din("w_kv2", [NL, 128, 768])
    vecs = din("vecs", [NL, 128, NV])
    w_f1 = din("w_f1", [NL, 17, 64])
    w_f2 = din("w_f2", [NL, 64, 64])
    w_f3 = din("w_f3", [NL, 64, 1536])
    skipb = din("skipb", [NL, 128, 768])
    nabias = din("nabias", [NL, 5, 128, 2560])
    w_out = din("w_out", [NL, D, D])
    w_gate = din("w_gate", [NL, D, DFF])
    w_up = din("w_up", [NL, D, DFF])
    w_down = din("w_down", [NL, DFF, D])
    ident_in = din("ident", [128, 128])
    ropeC = din("ropeC", [32, S])
    ropeS = din("ropeS", [32, S])
    zT_in = din("zT", [17, S])
    decay_in = din("decay", [S, 384])
    CfF = din("CfF", [NKF, 128, 4096], BF16)
    SfF = din("SfF", [NKF, 128, 4096], BF16)
    CfI = din("CfI", [NT, 128, NKF * 128], BF16)
    SfI = din("SfI", [NT, 128, NKF * 128], BF16)
    wk_in = din("wk", [128, 8 * NKF])
    out_ap = nc.dram_tensor("out", [S, D], F32, kind="ExternalOutput").ap()

    xT = dscr("xT", [8, 128, S])
    QT = dscr("QT", [6, 96, S], BF16)
    KT = dscr("KT", [6, 96, S], BF16)
    Vm = dscr("Vm", [6, 128, NT, 65], BF16)
    hyT = dscr("hyT", [9, 128, S])
    naQT = dscr("naQT", [2, 128, S], BF16)
    naKT = dscr("naKT", [2, 128, S], BF16)
    naV = dscr("naV", [NT, 128, 260], BF16)
    ymix = dscr("ymix", [S, D])
    xg = dscr("xg", [2, S, 384])
    Gs = dscr("Gs", [2, NKF, 128, 4 * 768])
    d_xT = [Dep(f"xT{b}") for b in range(NB)]
    h2s = dscr("h2s", [NB, 128, 8, TB], BF16)
    d_h2s = [Dep(f"h2s{b}") for b in range(NB)]
    d_QT = Dep("QT", multi=True)
    d_KT = Dep("KT", multi=True)
    d_Vm = Dep("Vm", multi=True)
    d_hyT = Dep("hyT", multi=True)
    d_naQT = Dep("naQT", multi=True)
    d_naKT = Dep("naKT", multi=True)
    d_naV = Dep("naV", multi=True)
    d_ymix = Dep("ymix", multi=True)
    d_xg = Dep("xg", multi=True)
    d_Gs = Dep("Gs", multi=True)
    d_out = Dep("out", multi=True)

    banks = Banks(P)
    ident = P.sbuf("ident", [128, 128], F32)
    d_ident = Dep("ident")
    P.dma("sp", ident[:], ident_in, writes=[d_ident])
    ones_b = P.sbuf("ones_b", [128, 128], BF16)
    d_const = Dep("const")
    P.memset("dve", ones_b[:], 1.0, [d_const])
    epsc = P.sbuf("epsc", [128, 1], F32)
    P.memset("dve", epsc[:], 1e-6, [d_const])
    vec = [P.sbuf(f"vec{l}", [128, NV], F32) for l in range(NL)]
    d_vec = Dep("vec")
    for l in range(NL):
        P.dma("sp", vec[l][:], vecs[l], writes=[d_vec])
    wkt = P.sbuf("wkt", [128, 8 * NKF], F32)
    P.dma("sp", wkt[:], wk_in, writes=[d_vec])

    def stop(name):
        return dbg == name

    def finish():
        P.barrier()
        P.emit()
        P.close()
        return nc

    def rms_p1(xt, d_xt, nch, sq, d_sq):
        P.act(sq[:, 0:nch, :], xt[:, 0:nch, :], AF.Square, [d_xt], [d_sq])

    def rms_p2(xt, d_xt, nch, gcol, vl, outT, d_out_, sq, d_sq, rs, d_rs, n_feat):
        bk, d_bk = banks.next()
        for k in range(nch):
            P.mm(bk, ones_b[:, :], sq[:, k, :], k == 0, k == nch - 1, [d_sq, d_const], [d_bk])
        P.act(rs[:, :], bk, AF.Sqrt, [d_bk, d_const], [d_rs], scale=1.0 / n_feat, bias=epsc[:, 0:1])
        P.recip(rs[:, :], rs[:, :], [d_rs], [d_rs])
        for k in range(nch):
            P.stt(outT[:, k, :], xt[:, k, :], vl[:, gcol + k:gcol + k + 1], rs[:, :], ALU.mult, ALU.mult,
                  [d_xt, d_rs, d_vec], [d_out_])

    def rms_feature_major(xt, d_xt, nch, gcol, vl, outT, d_out_, sq, d_sq, rs, d_rs, n_feat):
        rms_p1(xt, d_xt, nch, sq, d_sq)
        rms_p2(xt, d_xt, nch, gcol, vl, outT, d_out_, sq, d_sq, rs, d_rs, n_feat)

    def tcols(ap2, tau):
        return ap2.rearrange("q (a p r) -> q r a p", a=8, p=128, r=4)[:, tau // 8, tau % 8, :]

    def trows(ap2, tau):
        return ap2.rearrange("(a p r) c -> r a p c", a=8, p=128, r=4)[tau // 8, tau % 8]

    class FoldBufs:
        def __init__(self):
            self.cp = Ring(P, "fcp", [128, 384], F32, 4)
            self.l1 = {n: Ring(P, "f1" + n, [128, 384], F32, 1) for n in ("P", "Q", "Pp", "Qp", "R", "T", "Rp", "Tp")}
            self.l2 = {n: Ring(P, "f2" + n, [128, 384], F32, 1) for n in
                       ("A1", "A2", "A3", "A4", "B1", "B2", "B3", "B4")}

    def fold_forward(*a, **k):
        g = fold_forward_g(*a, **k)
        try:
            while True:
                next(g)
        except StopIteration as e:
            return e.value

    def fold_forward_g(fb, src, d_src, Cb, Sb, d_C, d_S, need, l2eng=None, bases=(0, 4), cp_eng="act"):
        Cv = Cb[:, :].rearrange("p (r a f) -> p r a f", r=4, a=8)
        Sv = Sb[:, :].rearrange("p (r a f) -> p r a f", r=4, a=8)
        L1 = {}
        hp = 0
        for ps_, (ra, rb) in enumerate(((0, 2), (1, 3))):
            for hf, (Mv, d_M) in enumerate(((Cv, d_C), (Sv, d_S))):
                if bases[0] == bases[1]:
                    bb = bases[0] + 2 * (hp % 2)
                else:
                    bb = bases[ps_] + 2 * hf
                hp += 1
                bka, bkb = banks.get(bb), banks.get(bb + 1)
                for (r, bk) in ((ra, bka), (rb, bkb)):
                    for a in range(8):
                        P.mm(bk[0][:, 0:384], Mv[:, r, a, :], src(r * 8 + a), a == 0, a == 7, [d_M, d_src], [bk[1]])
                        if a % 2:
                            yield
                cc, d_cc = fb.cp.next()
                P.cp(cp_eng, cc[:, :], bkb[0][:, 0:384], [bkb[1]], [d_cc])
                if ps_ == 0:
                    names = ("P", "Q") if hf == 0 else ("Pp", "Qp")
                else:
                    names = ("R", "T") if hf == 0 else ("Rp", "Tp")
                for nm, op in zip(names, (ALU.add, ALU.subtract)):
                    t_, d_t = fb.l1[nm].next()
                    P.tt("dve", t_[:, :], bka[0][:, 0:384], cc[:, :], op, [bka[1], d_cc], [d_t])
                    L1[nm] = (t_, d_t)
        out = {}

        def l2(nm, x, y, op):
            t_, d_t = fb.l2[nm].next()
            eng = "pool" if l2eng is None else l2eng[nm[1]]
            P.tt(eng, t_[:, :], L1[x][0][:, :], L1[y][0][:, :], op, [L1[x][1], L1[y][1]], [d_t])
            out[nm] = (t_, d_t)

        if "A" in need:
            l2("A1", "P", "R", ALU.add)
            l2("A4", "P", "R", ALU.subtract)
            l2("A2", "Q", "Tp", ALU.add)
            l2("A3", "Q", "Tp", ALU.subtract)
        if "B" in need:
            l2("B1", "Pp", "Rp", ALU.add)
            l2("B4", "Rp", "Pp", ALU.subtract)
            l2("B2", "T", "Qp", ALU.subtract)
            l2("B3", "Qp", "T", ALU.add)
        return out

    P.push()
    xin_r = Ring(P, "xin", [128, D], F32, 2)
    xtb_r = Ring(P, "xtb0", [128, 8, TB], F32, 2)
    for b in range(NB):
        xtb, d_xtb = xtb_r.next()
        for i in range(4):
            ti = b * 4 + i
            xin, d_xin = xin_r.next()
            P.dma("sp", xin[:], x_in[ti * 128:(ti + 1) * 128, :], writes=[d_xin])
            if i == 0:
                P.flush()
            for j in range(2):
                bk, d_bk = banks.next()
                for c in range(4):
                    k = j * 4 + c
                    P.tr(bk[:, c * 128:(c + 1) * 128], xin[:, k * 128:(k + 1) * 128], ident[:, :],
                         [d_xin, d_ident], [d_bk])
                P.cp("act" if j == 0 else "dve", xtb[:, j * 4:(j + 1) * 4, i * 128:(i + 1) * 128],
                     bk.rearrange("p (a b) -> p a b", a=4), [d_bk], [d_xtb])
        P.store(xT[:, :, b * TB:(b + 1) * TB].rearrange("k p s -> p k s"), xtb[:], reads=[d_xtb], writes=[d_xT[b]])
    P.pop()
    if stop("p0"):
        return finish()

    for l in range(NL):
        vl = vec[l]
        P.push()
        winA = P.sbuf("winA", [128, 8, WIN_COLS], BF16)
        d_w = Dep("winA")
        P.dma("pool", winA[:], w_inA[l].rearrange("(k p) c -> p k c", p=128), writes=[d_w])
        wuq = P.sbuf("wuq", [128, 2, 1152], BF16)
        d_wuq = Dep("wuq")
        P.dma("pool", wuq[:], w_uq2[l].rearrange("(j p) c -> p j c", p=128), writes=[d_wuq])
        wkv = P.sbuf("wkv", [128, 768], BF16)
        d_wkv = Dep("wkv")
        P.dma("pool", wkv[:], w_kv2[l], writes=[d_wkv])
        xtb_r = Ring(P, "xtbA", [128, 8, TB], F32, 2)
        sq = P.sbuf("sqA", [128, 8, TB], BF16)
        d_sq = Dep("sqA")
        rs = P.sbuf("rsA", [128, TB], F32)
        d_rs = Dep("rsA")
        sqq = P.sbuf("sqq", [128, 2, TB], BF16)
        d_sqq = Dep("sqq")
        rsq = P.sbuf("rsq", [128, TB], F32)
        d_rsq = Dep("rsq")
        sqk = P.sbuf("sqk", [128, 1, TB], BF16)
        d_sqk = Dep("sqk")
        rsk = P.sbuf("rsk", [128, TB], F32)
        d_rsk = Dep("rsk")
        hT_r = Ring(P, "hT", [128, 8, TB], BF16, 2)
        cq = P.sbuf("cq", [128, 2, TB], F32)
        d_cq = Dep("cq")
        cqn = P.sbuf("cqn", [128, 2, TB], BF16)
        d_cqn = Dep("cqn")
        ckv = P.sbuf("ckv", [128, 1, TB], F32)
        d_ckv = Dep("ckv")
        ckvn = P.sbuf("ckvn", [128, 1, TB], BF16)
        d_ckvn = Dep("ckvn")
        rC_r = Ring(P, "rC", [128, TB], F32, 2)
        rS_r = Ring(P, "rS", [128, TB], F32, 2)
        tA_r = Ring(P, "tA", [128, TB], F32, 2)
        tB_r = Ring(P, "tB", [128, TB], F32, 2)
        QTb_r = Ring(P, "QTb", [128, 6, TB], BF16, 2)
        KTb_r = Ring(P, "KTb", [128, 6, TB], BF16, 2)
        Vb_r = Ring(P, "Vb", [128, 6, 4, 65], BF16, 2)
        hyb_r = Ring(P, "hyb", [128, 3, TB], F32, 3)
        nqk_r = Ring(P, "nqk", [128, 4, TB], BF16, 2)
        nvb_r = Ring(P, "nvb", [128, 4, 260], BF16, 2)
        for r_ in (Vb_r, nvb_r):
            for t_, d_ in zip(r_.t, r_.d):
                P.memset("pool", t_[:], 1.0, [d_])
        for b in range(NB):
            sl = slice(b * TB, (b + 1) * TB)
            xtb, d_xtb = xtb_r.next()
            P.dma("sp", xtb[:], xT[:, :, sl].rearrange("k p s -> p k s"), reads=[d_xT[b]], writes=[d_xtb])
            rC, d_rC = rC_r.next()
            rS, d_rS = rS_r.next()
            P.dma("sp", rC[64:96, :], ropeC[:, sl], writes=[d_rC])
            P.dma("sp", rS[64:96, :], ropeS[:, sl], writes=[d_rS])
            P.flush()
            hT, d_hT = hT_r.next()
            rms_feature_major(xtb, d_xtb, 8, 0, vl, hT, d_hT, sq, d_sq, rs, d_rs, D)

            def proj(c0, M):
                bk, d_bk = banks.next()
                for k in range(8):
                    P.mm(bk[0:M, :], winA[:, k, c0:c0 + M], hT[:, k, :], k == 0, k == 7, [d_w, d_hT], [d_bk])
                return bk, d_bk

            for j in range(2):
                bk, d_bk = proj(j * 128, 128)
                P.cp("act", cq[:, j, :], bk, [d_bk], [d_cq])
            bk, d_bk = proj(256, 128)
            P.cp("act", ckv[:, 0, :], bk, [d_bk], [d_ckv])
            rms_p1(cq, d_cq, 2, sqq, d_sqq)
            rms_p1(ckv, d_ckv, 1, sqk, d_sqk)

            KTb, d_KTb = KTb_r.next()
            bk, d_bk = proj(384, 96)
            bk2, d_bk2 = proj(480, 96)
            tA, d_tA = tA_r.next()
            tB, d_tB = tB_r.next()
            P.tt("dve", tA[64:96, :], bk[64:96, :], rC[64:96, :], ALU.mult, [d_bk, d_rC], [d_tA])
            P.tt("dve", tB[64:96, :], bk2[64:96, :], rS[64:96, :], ALU.mult, [d_bk2, d_rS], [d_tB])
            for h in range(6):
                P.tt("pool", KTb[64:96, h, :], tA[64:96, :], tB[64:96, :], ALU.add, [d_tA, d_tB], [d_KTb])

            for g3 in range(3):
                hyb, d_hyb = hyb_r.next()
                for c3 in range(3):
                    c = g3 * 3 + c3
                    bk, d_bk = proj(576 + c * 128, 128)
                    P.cp("act" if c % 2 else "dve", hyb[:, c3, :], bk, [d_bk], [d_hyb])
                P.store(hyT[g3 * 3:(g3 + 1) * 3, :, sl].rearrange("c p s -> p c s"), hyb[:], reads=[d_hyb],
                      writes=[d_hyT], slot=d_hyb)
                if g3 == 0:
                    rms_p2(cq, d_cq, 2, 16, vl, cqn, d_cqn, sqq, d_sqq, rsq, d_rsq, 256)
                    rms_p2(ckv, d_ckv, 1, 18, vl, ckvn, d_ckvn, sqk, d_sqk, rsk, d_rsk, 128)

            nqk, d_nqk = nqk_r.next()
            for c in range(4):
                bk, d_bk = proj(1728 + c * 128, 128)
                P.cp("act" if c % 2 else "dve", nqk[:, c, :], bk, [d_bk], [d_nqk])
            P.store(naQT[:, :, sl].rearrange("c p s -> p c s"), nqk[:, 0:2, :], reads=[d_nqk], writes=[d_naQT],
                  slot=d_nqk)
            P.store(naKT[:, :, sl].rearrange("c p s -> p c s"), nqk[:, 2:4, :], reads=[d_nqk], writes=[d_naKT],
                  slot=d_nqk)
            nvb, d_nvb = nvb_r.next()
            for i in range(4):
                bk, d_bk = banks.next()
                for k in range(8):
                    P.mm(bk[:, 0:256], hT[:, k, i * 128:(i + 1) * 128], winA[:, k, 2240:2496], k == 0, k == 7,
                         [d_w, d_hT], [d_bk])
                P.cp("act" if i % 2 else "dve",
                     nvb[:, i, :].rearrange("p (h c) -> p h c", h=4)[:, :, 0:64],
                     bk[:, 0:256].rearrange("p (h c) -> p h c", h=4), [d_bk], [d_nvb])
            P.store(naV[b * 4:(b + 1) * 4].rearrange("t p c -> p t c"), nvb[:], reads=[d_nvb], writes=[d_naV],
                  slot=d_nvb)

            QTb, d_QTb = QTb_r.next()
            for h in range(6):
                bk, d_bk = banks.next()
                bk2, d_bk2 = banks.next()
                for j in range(2):
                    P.mm(bk[0:96, :], wuq[:, j, h * 192:h * 192 + 96], cqn[:, j, :], j == 0, j == 1, [d_wuq, d_cqn], [d_bk])
                for j in range(2):
                    P.mm(bk2[0:96, :], wuq[:, j, h * 192 + 96:h * 192 + 192], cqn[:, j, :], j == 0, j == 1,
                         [d_wuq, d_cqn], [d_bk2])
                P.cp("act", QTb[0:64, h, :], bk[0:64, :], [d_bk], [d_QTb])
                tA, d_tA = tA_r.next()
                tB, d_tB = tB_r.next()
                P.tt("dve", tA[64:96, :], bk[64:96, :], rC[64:96, :], ALU.mult, [d_bk, d_rC], [d_tA])
                P.tt("dve", tB[64:96, :], bk2[64:96, :], rS[64:96, :], ALU.mult, [d_bk2, d_rS], [d_tB])
                P.tt("pool", QTb[64:96, h, :], tA[64:96, :], tB[64:96, :], ALU.add, [d_tA, d_tB], [d_QTb])
            P.store(QT[:, :, sl].rearrange("h p s -> p h s"), QTb[0:96, :, :], reads=[d_QTb], writes=[d_QT],
                  slot=d_QTb)

            for h in range(6):
                bk, d_bk = banks.next()
                P.mm(bk[0:64, :], wkv[:, h * 64:(h + 1) * 64], ckvn[:, 0, :], True, True, [d_wkv, d_ckvn], [d_bk])
                P.cp("act" if h % 2 else "dve", KTb[0:64, h, :], bk[0:64, :], [d_bk], [d_KTb])
            P.store(KT[:, :, sl].rearrange("h p s -> p h s"), KTb[0:96, :, :], reads=[d_KTb], writes=[d_KT],
                  slot=d_KTb)
            Vb, d_Vb = Vb_r.next()
            for i in range(4):
                bk, d_bk = banks.next()
                P.mm(bk[:, 0:384], ckvn[:, 0, i * 128:(i + 1) * 128], wkv[:, 384:768], True, True,
                     [d_wkv, d_ckvn], [d_bk])
                P.cp("act" if i % 2 else "dve", Vb[:, :, i, 0:64],
                     bk[:, 0:384].rearrange("p (h c) -> p h c", h=6), [d_bk], [d_Vb])
            P.store(Vm[:, :, b * 4:(b + 1) * 4, :].rearrange("h p t c -> p h t c"), Vb[:], reads=[d_Vb], writes=[d_Vm],
                  slot=d_Vb)
        P.pop()
        if stop(f"A{l}"):
            return finish()

        P.push()
        hs = P.sbuf("hs", [128, NT, 768], BF16)
        hd = P.sbuf("hd", [128, NT, 768], BF16)
        d_hsd = Dep("hsd", multi=True)
        P.push()
        zT = P.sbuf("zT", [17, S], F32)
        wf1 = P.sbuf("wf1", [17, 64], F32)
        wf2 = P.sbuf("wf2", [64, 64], F32)
        wf3 = P.sbuf("wf3", [64, 1536], F32)
        d_fw = Dep("fw", multi=True)
        P.dma("sp", zT[:], zT_in, writes=[d_fw])
        P.dma("sp", wf1[:], w_f1[l], writes=[d_fw])
        P.dma("sp", wf2[:], w_f2[l], writes=[d_fw])
        P.dma("sp", wf3[:], w_f3[l], writes=[d_fw])
        hid1 = P.sbuf("hid1", [64, S], F32)
        hid2 = P.sbuf("hid2", [64, S], F32)
        d_h1 = Dep("hid1", multi=True)
        d_h2 = Dep("hid2", multi=True)
        fa_r = Ring(P, "fa", [64, TB], F32, 2)
        ft_r = Ring(P, "ft", [64, TB], F32, 2)

        def sin_block(bk, d_bk, bcol, fcol, out, d_o):
            a, d_a = fa_r.next()
            t, d_t = ft_r.next()
            P.ts("dve", a[:, :], bk[0:64, :], vl[0:64, bcol:bcol + 1], vl[0:64, fcol:fcol + 1], ALU.add, ALU.mult,
                 [d_bk, d_vec], [d_a])
            P.ts("dve", t[:, :], a[:, :], 1.0 / (2.0 * math.pi), MAGIC, ALU.mult, ALU.add, [d_a], [d_t])
            P.ts("dve", t[:, :], t[:, :], MAGIC, -2.0 * math.pi, ALU.subtract, ALU.mult, [d_t], [d_t])
            P.tt("dve", a[:, :], a[:, :], t[:, :], ALU.add, [d_a, d_t], [d_a])
            P.ts("dve", a[:, :], a[:, :], -PI_SAFE, PI_SAFE, ALU.max, ALU.min, [d_a], [d_a])
            P.act(out, a[:, :], AF.Sin, [d_a], [d_o])

        for b in range(NB):
            sl = slice(b * TB, (b + 1) * TB)
            bk, d_bk = banks.next()
            P.mm(bk[0:64, :], wf1[:, :], zT[:, sl], True, True, [d_fw], [d_bk])
            sin_block(bk, d_bk, 71, 72, hid1[:, sl], d_h1)
        for b in range(NB):
            sl = slice(b * TB, (b + 1) * TB)
            bk, d_bk = banks.next()
            P.mm(bk[0:64, :], wf2[:, :], hid1[:, sl], True, True, [d_fw, d_h1], [d_bk])
            sin_block(bk, d_bk, 73, 74, hid2[:, sl], d_h2)
        hraw_r = Ring(P, "hraw", [128, 1536], F32, 2)
        dk_r = Ring(P, "dk", [128, 384], F32, 2)
        fs_r = Ring(P, "fs", [128, 2, 384], F32, 2)
        fd_r = Ring(P, "fd", [128, 2, 384], F32, 2)
        for i in range(NT):
            hraw, d_hr = hraw_r.next()
            for n in range(3):
                bk, d_bk = banks.next()
                P.mm(bk, tcols(hid2[:, :], i), wf3[:, n * 512:(n + 1) * 512], True, True, [d_fw, d_h2], [d_bk])
                P.cp("act", hraw[:, n * 512:(n + 1) * 512], bk, [d_bk], [d_hr])
            dk, d_dk = dk_r.next()
            P.dma("sp", dk[:], trows(decay_in, i), writes=[d_dk])
            fs, d_fs = fs_r.next()
            fd, d_fd = fd_r.next()
            hv = hraw[:, :].rearrange("p (o r c) -> p o r c", o=2, r=2)
            P.tt("dve", fs[:, :, :], hv[:, :, 0, :], hv[:, :, 1, :], ALU.add, [d_hr], [d_fs])
            P.tt("dve", fd[:, :, :], hv[:, :, 0, :], hv[:, :, 1, :], ALU.subtract, [d_hr], [d_fd])
            for o in range(2):
                P.tt("pool", hs[:, i, o * 384:(o + 1) * 384], fs[:, o, :], dk[:, :], ALU.mult, [d_fs, d_dk], [d_hsd])
                P.tt("pool", hd[:, i, o * 384:(o + 1) * 384], fd[:, o, :], dk[:, :], ALU.mult, [d_fd, d_dk], [d_hsd])
        P.pop()
        if stop(f"HF{l}"):
            dbg_h = nc.dram_tensor("dbg_hs", [NT, 128, 768], F32, kind="ExternalOutput").ap()
            dbg_h2 = nc.dram_tensor("dbg_hd", [NT, 128, 768], F32, kind="ExternalOutput").ap()
            d_dbg = Dep("dbg", multi=True)
            P.store(dbg_h.rearrange("t p c -> p t c"), hs[:], reads=[d_hsd], writes=[d_dbg])
            P.store(dbg_h2.rearrange("t p c -> p t c"), hd[:], reads=[d_hsd], writes=[d_dbg])
            return finish()
        P.push()
        skb = P.sbuf("skb", [128, 768], F32)
        d_skb = Dep("skb")
        P.dma("sp", skb[:], skipb[l], writes=[d_skb])
        Cb_r = Ring(P, "CbF", [128, 4096], BF16, 1)
        Sb_r = Ring(P, "SbF", [128, 4096], BF16, 1)
        gst_r = Ring(P, "gst", [128, 4, 768], F32, 1)
        gtmp_r = Ring(P, "gtmp", [128, 384], F32, 2)
        fb = FoldBufs()

        def a2_gen():
            for kc in range(NKF):
                Cb, d_C = Cb_r.next()
                Sb, d_S = Sb_r.next()
                P.dma("sp", Cb[:], CfF[kc], writes=[d_C])
                P.dma("sp", Sb[:], SfF[kc], writes=[d_S])
                for o in range(2):
                    gst, d_gst = gst_r.next()
                    oa = yield from fold_forward_g(fb, lambda tau, o=o: hs[:, tau, o * 384:(o + 1) * 384], d_hsd,
                                                   Cb, Sb, d_C, d_S, "A", bases=(4, 4), cp_eng="dve")
                    for j in range(4):
                        gt_, d_gt_ = gtmp_r.next()
                        aj, d_aj = oa[f"A{j + 1}"]
                        P.tt("pool", gt_[:, :], aj[:, :], skb[:, o * 384:(o + 1) * 384], ALU.add, [d_aj, d_skb],
                             [d_gt_])
                        P.ts("pool", gst[:, j, 0:384], gt_[:, :], wkt[:, kc * 4 + j:kc * 4 + j + 1], None, ALU.mult,
                             None, [d_gt_, d_vec], [d_gst])
                    ob = yield from fold_forward_g(fb, lambda tau, o=o: hd[:, tau, o * 384:(o + 1) * 384], d_hsd,
                                                   Cb, Sb, d_C, d_S, "B", bases=(4, 4), cp_eng="dve")
                    for j in range(4):
                        bj, d_bj = ob[f"B{j + 1}"]
                        P.ts("pool", gst[:, j, 384:768], bj[:, :],
                             wkt[:, 4 * NKF + kc * 4 + j:4 * NKF + kc * 4 + j + 1], None, ALU.mult, None,
                             [d_bj, d_vec], [d_gst])
                    P.dma("sp", Gs[o, kc], gst[:].rearrange("p j c -> p (j c)"), reads=[d_gst], writes=[d_Gs],
                          slot=d_gst)

        a2g = a2_gen()

        def a2_step():
            try:
                next(a2g)
            except StopIteration:
                pass

        Vh_r = Ring(P, "Vh", [128, NT, 65], BF16, 2)
        KTh_r = Ring(P, "KTh", [96, S], BF16, 1)
        QTh_r = Ring(P, "QTh", [96, S], BF16, 1)
        pT_r = Ring(P, "pT", [128, TB], BF16, 6)
        rd_r = Ring(P, "rdM", [128, 4], F32, 2)
        yas_r = Ring(P, "yas", [128, 4, 64], F32, 2)
        sc_m = 1.0 / math.sqrt(96.0)
        s_rot = 0
        for h in range(6):
            KTh, d_K = KTh_r.next()
            QTh, d_Q = QTh_r.next()
            Vh, d_V = Vh_r.next()
            P.dma("sp", KTh[:], KT[h], reads=[d_KT], writes=[d_K])
            P.dma("sp", QTh[:], QT[h], reads=[d_QT], writes=[d_Q])
            P.dma("sp", Vh[:], Vm[h], reads=[d_Vm], writes=[d_V])
            for qb in range(NB):
                po, d_po = banks.get(3)
                pts = []
                for step in range(NT + 2):
                    if step < NT:
                        kt = step
                        sb, d_sb = banks.get(s_rot % 3)
                        s_rot += 1
                        P.mm(sb, KTh[:, kt * 128:(kt + 1) * 128], QTh[:, qb * TB:(qb + 1) * TB], True, True,
                             [d_K, d_Q], [d_sb])
                        pT, d_pT = pT_r.next()
                        P.act(pT[:, :], sb, AF.Exp, [d_sb], [d_pT], scale=sc_m)
                        pts.append((pT, d_pT))
                        a2_step()
                    if step >= 2:
                        kt = step - 2
                        pT, d_pT = pts[kt]
                        for j in range(4):
                            P.mm(po[:, j * 65:(j + 1) * 65], pT[:, j * 128:(j + 1) * 128],
                                 Vh[:, kt, :], kt == 0 and j == 0, kt == NT - 1 and j == 3,
                                 [d_pT, d_V], [d_po], skip=True)
                rdt, d_rd = rd_r.next()
                P.recip(rdt[:, 0:4], po[:, 0:260].rearrange("p (j c) -> p j c", c=65)[:, :, 64], [d_po], [d_rd])
                yas, d_yas = yas_r.next()
                for j in range(4):
                    P.ts("dve", yas[:, j, :], po[:, j * 65:j * 65 + 64], rdt[:, j:j + 1],
                         None, ALU.mult, None, [d_po, d_rd], [d_yas])
                P.dma("sp", ymix[qb * TB:(qb + 1) * TB, h * 64:(h + 1) * 64].rearrange("(t p) c -> p t c", p=128),
                      yas[:], reads=[d_yas], writes=[d_ymix], slot=d_yas)
        for _ in a2g:
            pass
        P.pop()
        P.pop()
        if stop(f"M{l}"):
            return finish()
        P.push()
        nq = P.sbuf("nq", [128, 2, S], BF16)
        nk = P.sbuf("nk", [128, 2, S], BF16)
        nv = P.sbuf("nv", [128, NT, 260], BF16)
        nbt = P.sbuf("nbt", [128, 5, 2560], F32)
        yc = P.sbuf("yc", [128, NT, 256], F32)
        d_nin = Dep("nin", multi=True)
        d_yc = Dep("yc", multi=True)
        P.dma("sp", nq[:], naQT.rearrange("c p s -> p c s"), reads=[d_naQT], writes=[d_nin])
        P.dma("sp", nk[:], naKT.rearrange("c p s -> p c s"), reads=[d_naKT], writes=[d_nin])
        P.dma("sp", nv[:], naV.rearrange("t p c -> p t c"), reads=[d_naV], writes=[d_nin])
        P.dma("sp", nbt[:], nabias[l].rearrange("t p c -> p t c"), writes=[d_nin])
        tS_r = Ring(P, "tS", [128, 640], F32, 4)
        pN_r = Ring(P, "pN", [128, 640], BF16, 4)
        rdn_r = Ring(P, "rdN", [128, 4], F32, 2)
        items = [(m, h) for m in range(NT) for h in range(4)]
        LOOK = 2
        pendq = []
        for idx in range(len(items) + LOOK):
            if idx < len(items):
                m, h = items[idx]
                ty = {0: 0, 1: 1, 30: 2, 31: 3}.get(m, 4)
                c0 = min(max(m - 2, 0), 27)
                j, pb = h // 2, 64 * (h % 2)
                bi = 2 * (idx % 3)
                s2 = banks.ps[:, bi * 512:(bi + 2) * 512]
                d_a, d_b = banks.d[bi], banks.d[bi + 1]
                for i in range(5):
                    P.mm(s2[:, i * 128:(i + 1) * 128], nk[pb:pb + 64, j, (c0 + i) * 128:(c0 + i + 1) * 128],
                         nq[pb:pb + 64, j, m * 128:(m + 1) * 128], True, True, [d_nin], [d_a if i < 4 else d_b])
                tS, d_tS = tS_r.next()
                P.stt(tS[:, :], s2[:, 0:640], 0.125, nbt[:, ty, h * 640:(h + 1) * 640], ALU.mult, ALU.add,
                      [d_a, d_b, d_nin], [d_tS])
                pN, d_pN = pN_r.next()
                P.act(pN[:, :], tS[:, :], AF.Exp, [d_tS], [d_pN])
                pendq.append((m, h, c0, pN, d_pN))
            if idx >= LOOK:
                m, h, c0, pN, d_pN = pendq.pop(0)
                po, d_po = banks.get(6 + m % 2)
                for i in range(5):
                    P.mm(po[:, h * 65:(h + 1) * 65], pN[:, i * 128:(i + 1) * 128], nv[:, c0 + i, h * 65:(h + 1) * 65],
                         h == 0 and i == 0, h == 3 and i == 4, [d_pN, d_nin], [d_po], skip=True)
                if h == 3:
                    rdt, d_rd = rdn_r.next()
                    P.recip(rdt[:, 0:4], po[:, 0:260].rearrange("p (j c) -> p j c", c=65)[:, :, 64], [d_po], [d_rd])
                    for hh in range(4):
                        P.ts("dve", yc[:, m, hh * 64:(hh + 1) * 64], po[:, hh * 65:hh * 65 + 64], rdt[:, hh:hh + 1],
                             None, ALU.mult, None, [d_po, d_rd], [d_yc])
        for q4 in range(4):
            P.store(ymix[q4 * 1024:(q4 + 1) * 1024, 768:1024].rearrange("(t p) c -> p t c", p=128),
                  yc[:, q4 * 8:(q4 + 1) * 8, :], reads=[d_yc], writes=[d_ymix])
        P.pop()
        if stop(f"N{l}"):
            return finish()


        P.push()
        u_tok = P.sbuf("u_tok", [128, NT, 384], BF16)
        d_u = Dep("u_tok", multi=True)
        P.push()
        hyc_r = Ring(P, "hyc", [128, S + 2], F32, 2)
        ucT_r = Ring(P, "ucT", [128, S], F32, 2)
        xst_r = Ring(P, "xst", [128, NT, 128], F32, 2)
        for t_, d_ in zip(hyc_r.t, hyc_r.d):
            P.memset("pool", t_[:, 0:1], 0.0, [d_])
            P.memset("pool", t_[:, S + 1:S + 2], 0.0, [d_])
        for c in range(9):
            hyc, d_hyc = hyc_r.next()
            P.dma("sp", hyc[:, 1:S + 1], hyT[c], reads=[d_hyT], writes=[d_hyc])
            P.flush()
            ucT, d_uc = ucT_r.next()
            w0 = vl[:, 35 + c:36 + c]
            w1 = vl[:, 44 + c:45 + c]
            w2 = vl[:, 53 + c:54 + c]
            bb = vl[:, 62 + c:63 + c]
            P.ts("dve", ucT[:, :], hyc[:, 1:S + 1], w1, bb, ALU.mult, ALU.add, [d_hyc, d_vec], [d_uc])
            P.stt(ucT[:, :], hyc[:, 0:S], w0, ucT[:, :], ALU.mult, ALU.add, [d_hyc, d_uc, d_vec], [d_uc])
            P.stt(ucT[:, :], hyc[:, 2:S + 2], w2, ucT[:, :], ALU.mult, ALU.add, [d_hyc, d_uc, d_vec], [d_uc])
            if c >= 3:
                xst, d_xst = xst_r.next()
            for g4 in range(8):
                bk, d_bk = banks.next()
                for q in range(4):
                    ti = g4 * 4 + q
                    P.tr(bk[:, q * 128:(q + 1) * 128], tcols(ucT[:, :], ti), ident[:, :],
                         [d_uc, d_ident], [d_bk])
                bv = bk.rearrange("p (a b) -> p a b", a=4)
                if c < 3:
                    P.cp("act" if g4 % 2 else "dve", u_tok[:, g4 * 4:(g4 + 1) * 4, c * 128:(c + 1) * 128], bv,
                         [d_bk], [d_u])
                else:
                    P.cp("act" if g4 % 2 else "dve", xst[:, g4 * 4:(g4 + 1) * 4, :], bv, [d_bk], [d_xst])
            if c >= 3:
                o, cc = (c - 3) // 3, (c - 3) % 3
                for q4 in range(4):
                    P.store(xg[o, q4 * 1024:(q4 + 1) * 1024, cc * 128:(cc + 1) * 128].rearrange(
                        "(t p) c -> p t c", p=128), xst[:, q4 * 8:(q4 + 1) * 8, :], reads=[d_xst], writes=[d_xg],
                          slot=d_xst)
        P.pop()
        if stop(f"HP{l}"):
            dbg_u = nc.dram_tensor("dbg_u", [NT, 128, 384], F32, kind="ExternalOutput").ap()
            d_dbg = Dep("dbg", multi=True)
            P.store(dbg_u.rearrange("t p c -> p t c"), u_tok[:], reads=[d_u], writes=[d_dbg])
            return finish()

        Ec = P.sbuf("Ec", [128, 4, NKF, 384], BF16)
        Es = P.sbuf("Es", [128, 4, NKF, 384], BF16)
        d_E = Dep("EcEs", multi=True)
        Cb_r = Ring(P, "CbF", [128, 4096], BF16, 2)
        Sb_r = Ring(P, "SbF", [128, 4096], BF16, 2)
        Ci_r = Ring(P, "CbI", [128, NKF * 128], BF16, 2)
        Si_r = Ring(P, "SbI", [128, NKF * 128], BF16, 2)
        gb_r = Ring(P, "gb", [128, 4, 768], F32, 1)
        fb = FoldBufs()
        tm_r = [Ring(P, f"cv{i}", [128, 384], F32, 1) for i in range(4)]
        tm2_r = [Ring(P, f"cw{i}", [128, 384], F32, 1) for i in range(4)]
        Y_r = {n: Ring(P, "Y" + n, [128, 384], F32, 1) for n in ("r1", "r2", "r3", "r4", "n1", "n2", "n3", "n4")}
        I_r = {n: Ring(P, "I" + n, [128, 384], F32, 1) for n in ("U", "V", "Up", "Vp", "W", "X", "Wp", "Xp")}
        gt_r = Ring(P, "gate", [128, 384], F32, 2)
        yo_r = Ring(P, "yo", [128, 384], F32, 2)
        for o in range(2):
            for kc in range(NKF):
                Cb, d_C = Cb_r.next()
                Sb, d_S = Sb_r.next()
                P.dma("sp", Cb[:], CfF[kc], writes=[d_C])
                P.dma("sp", Sb[:], SfF[kc], writes=[d_S])
                gb, d_gb = gb_r.next()
                P.dma("sp", gb[:].rearrange("p j c -> p (j c)"), Gs[o, kc], reads=[d_Gs], writes=[d_gb])
                ejs = {"1": "dve", "2": "pool", "3": "pool", "4": "dve"}
                ab = fold_forward(fb, lambda tau: u_tok[:, tau, :], d_u, Cb, Sb, d_C, d_S, "AB", l2eng=ejs)
                Y = {}
                for j in range(4):
                    aj, d_aj = ab[f"A{j + 1}"]
                    bj, d_bj = ab[f"B{j + 1}"]
                    ej = ejs[str(j + 1)]
                    t1, t2, t3, t4 = [r.next() for r in (tm_r if ej == "dve" else tm2_r)]
                    P.tt(ej, t1[0][:, :], aj[:, :], gb[:, j, 0:384], ALU.mult, [d_aj, d_gb], [t1[1]])
                    P.tt(ej, t2[0][:, :], bj[:, :], gb[:, j, 384:768], ALU.mult, [d_bj, d_gb], [t2[1]])
                    P.tt(ej, t3[0][:, :], bj[:, :], gb[:, j, 0:384], ALU.mult, [d_bj, d_gb], [t3[1]])
                    P.tt(ej, t4[0][:, :], aj[:, :], gb[:, j, 384:768], ALU.mult, [d_aj, d_gb], [t4[1]])
                    yr, d_yr = Y_r[f"r{j + 1}"].next()
                    P.tt(ej, yr[:, :], t1[0][:, :], t2[0][:, :], ALU.add, [t1[1], t2[1]], [d_yr])
                    yn, d_yn = Y_r[f"n{j + 1}"].next()
                    P.tt(ej, yn[:, :], t3[0][:, :], t4[0][:, :], ALU.subtract, [t3[1], t4[1]], [d_yn])
                    Y[f"r{j + 1}"] = (yr, d_yr)
                    Y[f"n{j + 1}"] = (yn, d_yn)
                I = {}

                def i1(nm, x, y, op):
                    t_, d_t = I_r[nm].next()
                    P.tt(ejs[x[1]], t_[:, :], Y[x][0][:, :], Y[y][0][:, :], op, [Y[x][1], Y[y][1]], [d_t])
                    I[nm] = (t_, d_t)

                i1("U", "r2", "r3", ALU.add)
                i1("V", "r2", "r3", ALU.subtract)
                i1("Up", "n2", "n3", ALU.add)
                i1("Vp", "n2", "n3", ALU.subtract)
                i1("W", "r1", "r4", ALU.add)
                i1("X", "r1", "r4", ALU.subtract)
                i1("Wp", "n1", "n4", ALU.subtract)
                i1("Xp", "n1", "n4", ALU.add)

                def i2(dst, r, x, y, op):
                    P.tt("dve" if dst is Ec else "pool", dst[:, r, kc, :], I[x][0][:, :], I[y][0][:, :], op,
                         [I[x][1], I[y][1]], [d_E])

                i2(Ec, 0, "W", "U", ALU.add)
                i2(Ec, 1, "X", "Up", ALU.add)
                i2(Ec, 2, "W", "U", ALU.subtract)
                i2(Ec, 3, "X", "Up", ALU.subtract)
                i2(Es, 0, "Wp", "Vp", ALU.subtract)
                i2(Es, 1, "Xp", "V", ALU.add)
                i2(Es, 2, "Wp", "Vp", ALU.add)
                i2(Es, 3, "Xp", "V", ALU.subtract)
            for tau in range(NT):
                r = tau // 8
                Ci, d_Ci = Ci_r.next()
                Si, d_Si = Si_r.next()
                P.dma("sp", Ci[:], CfI[tau], writes=[d_Ci])
                P.dma("sp", Si[:], SfI[tau], writes=[d_Si])
                gt, d_gt = gt_r.next()
                P.dma("sp", gt[:], xg[o, tau * 128:(tau + 1) * 128, :], reads=[d_xg], writes=[d_gt])
                P.flush()
                by, d_by = banks.get(tau % 2)
                for kc in range(NKF):
                    P.mm(by[:, 0:384], Ci[:, kc * 128:(kc + 1) * 128], Ec[:, r, kc, :], kc == 0, False, [d_Ci, d_E], [d_by])
                    P.mm(by[:, 0:384], Si[:, kc * 128:(kc + 1) * 128], Es[:, r, kc, :], False, kc == NKF - 1,
                         [d_Si, d_E], [d_by])
                if o == 0:
                    P.tt("dve", u_tok[:, tau, :], by[:, 0:384], gt[:, :], ALU.mult, [d_by, d_gt], [d_u])
                else:
                    yo, d_yo = yo_r.next()
                    P.tt("dve", yo[:, :], by[:, 0:384], gt[:, :], ALU.mult, [d_by, d_gt], [d_yo])
                    P.store(trows(ymix, tau)[:, 384:768], yo[:, :], reads=[d_yo], writes=[d_ymix], slot=d_yo)
            if o == 0 and stop(f"HC{l}"):
                dbg_u = nc.dram_tensor("dbg_u", [NT, 128, 384], F32, kind="ExternalOutput").ap()
                d_dbg = Dep("dbg", multi=True)
                P.store(dbg_u.rearrange("t p c -> p t c"), u_tok[:], reads=[d_u], writes=[d_dbg])
                return finish()
        P.pop()
        if stop(f"H{l}"):
            return finish()

        P.push()
        wg = P.sbuf("wg", [128, 8, DFF], BF16)
        wu = P.sbuf("wu", [128, 8, DFF], BF16)
        d_wg = Dep("wg")
        d_wu = Dep("wu")
        P.push()
        wo = P.sbuf("wo", [128, 8, D], BF16)
        d_wo = Dep("wo")
        P.dma("pool", wo[:], w_out[l].rearrange("(k p) c -> p k c", p=128), writes=[d_wo])
        P.dma("pool", wg[:], w_gate[l].rearrange("(k p) c -> p k c", p=128), writes=[d_wg])
        P.dma("pool", wu[:], w_up[l].rearrange("(k p) c -> p k c", p=128), writes=[d_wu])
        ym_r = Ring(P, "ym", [128, 4, D], F32, 1)
        h2o_r = Ring(P, "h2o", [128, 8, TB], BF16, 2)
        rsD = P.sbuf("rsD", [128, TB], F32)
        d_rsD = Dep("rsD")
        xtb_r = Ring(P, "xtbD", [128, 8, TB], F32, 2)
        yn_r = Ring(P, "yn", [128, D], F32, 2)
        yT_r = Ring(P, "yT", [128, 8, TB], BF16, 2)
        ssq_r = Ring(P, "ssq", [128, 12], F32, 2)
        rsd_r = Ring(P, "rsd", [128, 12], F32, 2)
        junk = P.sbuf("junk", [128, 384], BF16)
        d_junk = Dep("junk", multi=True)
        groups = [(0, 384), (384, 768), (768, 1024)]
        for b in range(NB):
            ym, d_ym = ym_r.next()
            P.dma("sp", ym[:], ymix[b * TB:(b + 1) * TB, :].rearrange("(t p) c -> p t c", p=128), reads=[d_ymix],
                  writes=[d_ym])
            xtb, d_xtb = xtb_r.next()
            P.dma("sp", xtb[:], xT[:, :, b * TB:(b + 1) * TB].rearrange("k p s -> p k s"), reads=[d_xT[b]],
                  writes=[d_xtb])
            P.flush()
            ssq, d_ssq = ssq_r.next()
            rsd, d_rsd = rsd_r.next()
            for i in range(4):
                for gi, (c0, c1) in enumerate(groups):
                    P.act(junk[:, 0:c1 - c0], ym[:, i, c0:c1], AF.Square, [d_ym], [d_junk, d_ssq],
                          accum=ssq[:, i * 3 + gi:i * 3 + gi + 1])
            sv = ssq[:, :].rearrange("p (i g) -> p i g", g=3)
            rv = rsd[:, :].rearrange("p (i g) -> p i g", g=3)
            for gi, (c0, c1) in enumerate(groups):
                P.act(rv[:, :, gi], sv[:, :, gi], AF.Sqrt, [d_ssq, d_const], [d_rsd], scale=1.0 / (c1 - c0),
                      bias=epsc[:, 0:1])
            P.recip(rsd[:, :], rsd[:, :], [d_rsd], [d_rsd])
            yT, d_yT = yT_r.next()
            for i in range(4):
                yn, d_yn = yn_r.next()
                for gi, (c0, c1) in enumerate(groups):
                    P.ts("dve" if gi < 2 else "pool", yn[:, c0:c1], ym[:, i, c0:c1], rsd[:, i * 3 + gi:i * 3 + gi + 1],
                         None, ALU.mult, None, [d_ym, d_rsd], [d_yn])
                for j in range(2):
                    bk, d_bk = banks.next()
                    for c in range(4):
                        k = j * 4 + c
                        P.tr(bk[:, c * 128:(c + 1) * 128], yn[:, k * 128:(k + 1) * 128], ident[:, :],
                             [d_yn, d_ident], [d_bk])
                    for c in range(4):
                        k = j * 4 + c
                        if c % 2:
                            P.act(yT[:, k, i * 128:(i + 1) * 128], bk[:, c * 128:(c + 1) * 128], AF.Copy,
                                  [d_bk, d_vec], [d_yT], scale=vl[:, 19 + k:20 + k])
                        else:
                            P.ts("dve", yT[:, k, i * 128:(i + 1) * 128], bk[:, c * 128:(c + 1) * 128],
                                 vl[:, 19 + k:20 + k], None, ALU.mult, None, [d_bk, d_vec], [d_yT])
            for mch in range(8):
                bk, d_bk = banks.next()
                for k in range(8):
                    P.mm(bk, wo[:, k, mch * 128:(mch + 1) * 128], yT[:, k, :], k == 0, k == 7, [d_wo, d_yT], [d_bk])
                P.tt("dve", xtb[:, mch, :], bk, xtb[:, mch, :], ALU.add, [d_bk, d_xtb], [d_xtb])
            P.store(xT[:, :, b * TB:(b + 1) * TB].rearrange("k p s -> p k s"), xtb[:], reads=[d_xtb],
                  writes=[d_xT[b]])
            h2o, d_h2o = h2o_r.next()
            rms_feature_major(xtb, d_xtb, 8, 8, vl, h2o, d_h2o, yT, d_yT, rsD, d_rsD, D)
            P.store(h2s[b], h2o[:], reads=[d_h2o], writes=[d_h2s[b]], slot=d_h2o)
        P.pop()
        if stop(f"D1{l}"):
            return finish()

        wd = P.sbuf("wd", [128, 22, D], BF16)
        d_wd = Dep("wd")
        P.dma("pool", wd[:], w_down[l].rearrange("(k p) c -> p k c", p=128), writes=[d_wd])
        xtb = P.sbuf("xtbF", [128, 8, TB], F32)
        d_xtb = Dep("xtbF")
        h2T_r = Ring(P, "h2T", [128, 8, TB], BF16, 2)
        actT = P.sbuf("actT", [128, 22, TB], BF16)
        d_act = Dep("actT")
        rs = P.sbuf("rsF", [128, TB], F32)
        d_rs = Dep("rsF")
        sg_r = Ring(P, "sg", [128, TB], F32, 2)
        last = (l == NL - 1)
        if last:
            ot_r = Ring(P, "ot", [128, D], F32, 2)

        def ld_h2(bb):
            t_, d_ = h2T_r.next()
            P.dma("sp", t_[:], h2s[bb], reads=[d_h2s[bb]], writes=[d_])
            return t_, d_

        nxt_h2 = ld_h2(0)
        for b in range(NB):
            h2T, d_h2 = nxt_h2
            if b + 1 < NB:
                nxt_h2 = ld_h2(b + 1)
            P.dma("sp", xtb[:], xT[:, :, b * TB:(b + 1) * TB].rearrange("k p s -> p k s"), reads=[d_xT[b]],
                  writes=[d_xtb])
            for f in range(22):
                bg, d_bg = banks.next()
                bu, d_bu = banks.next()
                for k in range(8):
                    P.mm(bg, wg[:, k, f * 128:(f + 1) * 128], h2T[:, k, :], k == 0, k == 7, [d_wg, d_h2], [d_bg])
                for k in range(8):
                    P.mm(bu, wu[:, k, f * 128:(f + 1) * 128], h2T[:, k, :], k == 0, k == 7, [d_wu, d_h2], [d_bu])
                sg, d_sg = sg_r.next()
                P.act(sg[:, :], bg, AF.Silu, [d_bg], [d_sg])
                P.tt("dve", actT[:, f, :], bu, sg[:, :], ALU.mult, [d_bu, d_sg], [d_act])
            for mch in range(8):
                bk, d_bk = banks.next()
                for f in range(22):
                    P.mm(bk, wd[:, f, mch * 128:(mch + 1) * 128], actT[:, f, :], f == 0, f == 21, [d_wd, d_act], [d_bk])
                P.tt("dve", xtb[:, mch, :], bk, xtb[:, mch, :], ALU.add, [d_bk, d_xtb], [d_xtb])
            if not last:
                P.store(xT[:, :, b * TB:(b + 1) * TB].rearrange("k p s -> p k s"), xtb[:], reads=[d_xtb],
                      writes=[d_xT[b]])
                P.flush()
            else:
                rms_feature_major(xtb, d_xtb, 8, 27, vl, xtb, d_xtb, actT, d_act, rs, d_rs, D)
                for i in range(4):
                    ot, d_ot = ot_r.next()
                    for j in range(2):
                        bk, d_bk = banks.next()
                        for c in range(4):
                            k = j * 4 + c
                            P.tr(bk[:, c * 128:(c + 1) * 128], xtb[:, k, i * 128:(i + 1) * 128], ident[:, :],
                                 [d_xtb, d_ident], [d_bk])
                        P.cp("act" if j else "dve", ot[:, j * 512:(j + 1) * 512], bk, [d_bk], [d_ot])
                    P.dma("sp", out_ap[b * TB + i * 128:b * TB + (i + 1) * 128, :], ot[:], reads=[d_ot],
                          writes=[d_out], slot=d_ot)
        P.pop()
        if stop(f"D2{l}"):
            return finish()

    return finish()


_CONST = {}


def host_constants():
    if _CONST:
        return _CONST
    f32 = np.float32
    c = _CONST
    c["ident"] = np.eye(128, dtype=f32)
    pos = np.arange(S, dtype=f32)
    inv = (np.float32(10000.0) ** (-np.arange(0, 32, 2, dtype=f32) / np.float32(32))).astype(f32)
    ang = (pos[:, None] * inv[None, :]).astype(f32)
    cos, sin = np.cos(ang).astype(f32), np.sin(ang).astype(f32)
    c["ropeC"] = np.ascontiguousarray(np.concatenate([cos, cos], 1).T)
    c["ropeS"] = np.ascontiguousarray(np.concatenate([-sin, sin], 1).T)
    t_idx = np.arange(S, dtype=f32)[:, None]
    t_norm = np.linspace(0.0, 1.0, S, dtype=f32)[:, None]
    bands = np.linspace(1e-4, 7, 8, dtype=f32)[None, :]
    angz = (np.float32(2.0 * math.pi) * t_idx * bands / np.float32(S)).astype(f32)
    z = np.concatenate([t_norm, np.cos(angz), np.sin(angz)], -1).astype(f32)
    c["zT"] = np.ascontiguousarray(z.T)
    deltas = np.linspace(math.log(1e-2) / 1.5, math.log(1e-2) / 0.3, 384, dtype=f32)
    c["decay"] = np.exp(-t_norm * np.abs(deltas)[None, :]).astype(f32)
    pp = np.arange(128, dtype=np.int64)
    tt = (512 * np.arange(8)[None, None, :] + 4 * pp[:, None, None] + np.arange(4)[None, :, None])
    kk = (128 * np.arange(NKF)[:, None] + pp[None, :])
    prod = (tt[None, :, :, :, None] * kk[:, None, None, None, :]) % 8192
    th = prod.astype(np.float64) * (2.0 * math.pi / 8192.0)
    c["CfF"] = np.cos(th).astype(f32).astype(ml_dtypes.bfloat16).reshape(NKF, 128, 4096)
    c["SfF"] = np.sin(th).astype(f32).astype(ml_dtypes.bfloat16).reshape(NKF, 128, 4096)
    tau = np.arange(NT)
    t2 = (512 * (tau % 8)[:, None] + 4 * pp[None, :] + (tau // 8)[:, None])
    kq = (128 * np.arange(NKF)[None, :] + pp[:, None])
    prod = (t2[:, None, None, :] * kq[None, :, :, None]) % 8192
    th = prod.astype(np.float64) * (2.0 * math.pi / 8192.0)
    c["CfI"] = np.cos(th).astype(f32).astype(ml_dtypes.bfloat16).reshape(NT, 128, NKF * 128)
    c["SfI"] = np.sin(th).astype(f32).astype(ml_dtypes.bfloat16).reshape(NT, 128, NKF * 128)
    wk = np.zeros((128, 8 * NKF), f32)
    for kc in range(NKF):
        for p in range(128):
            k = kc * 128 + p
            if k > 1024:
                continue
            orbit = [k, 2048 - k, 2048 + k, 4096 - k]
            w = [(1.0 if kp in (0, 4096) else 2.0) / 8192.0 for kp in orbit]
            if k == 0:
                w[2] = 0.0
            if k == 1024:
                w[1] = 0.0
                w[3] = 0.0
            for j in range(4):
                wk[p, kc * 4 + j] = w[j]
                wk[p, 4 * NKF + kc * 4 + j] = -w[j]
    c["wk"] = wk
    return c


def host_layout(inputs):
    f32 = np.float32
    g = {k: np.asarray(v) for k, v in inputs.items()}
    w_in = g["w_in"]
    zpad = np.zeros((NL, D, 64), f32)
    kpe = w_in[:, :, 384:416]
    kpe_sw = np.concatenate([kpe[:, :, 16:32], kpe[:, :, 0:16]], -1)
    w_inA = np.concatenate([w_in[:, :, 0:384], zpad, kpe, zpad, kpe_sw, w_in[:, :, 416:2336]], -1)
    assert w_inA.shape[-1] == WIN_COLS
    wuq = g["mla_w_uq"].reshape(NL, 256, 6, 96)
    zq = np.zeros((NL, 256, 6, 64), f32)
    w_uq2 = np.concatenate([wuq, zq, wuq[..., 80:96], wuq[..., 64:80]], -1).reshape(NL, 256, 1152)
    wkv = g["mla_w_ukv"].reshape(NL, 128, 6, 128)
    w_kv2 = np.concatenate([wkv[..., 0:64].reshape(NL, 128, 384), wkv[..., 64:128].reshape(NL, 128, 384)], -1)
    vecs = np.zeros((NL, 128, NV), f32)
    for l in range(NL):
        vecs[l, :, 0:8] = g["norm1_g"][l].reshape(8, 128).T
        vecs[l, :, 8:16] = g["norm2_g"][l].reshape(8, 128).T
        vecs[l, :, 16:18] = g["mla_q_norm_g"][l].reshape(2, 128).T
        vecs[l, :, 18:19] = g["mla_kv_norm_g"][l].reshape(1, 128).T
        vecs[l, :, 19:27] = g["mix_norm_g"][l].reshape(8, 128).T
        vecs[l, :, 27:35] = g["final_norm_g"].reshape(8, 128).T
        for j in range(3):
            vecs[l, :, 35 + j * 9:35 + (j + 1) * 9] = g["hy_conv_w"][l, j].reshape(9, 128).T
        vecs[l, :, 62:71] = g["hy_conv_b"][l].reshape(9, 128).T
        vecs[l, 0:64, 71] = g["hy_filt_b1"][l]
        vecs[l, 0:64, 72] = g["hy_filt_freq1"][l]
        vecs[l, 0:64, 73] = g["hy_filt_b2"][l]
        vecs[l, 0:64, 74] = g["hy_filt_freq2"][l]
    skipb = np.broadcast_to(g["hy_skip"].reshape(NL, 1, 768), (NL, 128, 768))
    rpb = g["na_rpb"]
    nab = np.full((NL, 5, 128, 4, 5, 128), -30000.0, f32)
    types = [(0, 0), (1, 0), (30, 27), (31, 27), (2, 0)]
    qf = np.arange(128)
    for ti, (m, c0) in enumerate(types):
        rq = 2 * m + qf // 64
        wq = qf % 64
        r0 = np.clip(rq - 4, 0, 56)
        cc0 = np.clip(wq - 8, 0, 48)
        for i in range(5):
            kt = (c0 + i) * 128 + np.arange(128)
            rk = kt // 64
            wkk = kt % 64
            inwin = ((rk[:, None] >= r0[None, :]) & (rk[:, None] < r0[None, :] + 8) &
                     (wkk[:, None] >= cc0[None, :]) & (wkk[:, None] < cc0[None, :] + 16))
            dr = np.clip(rk[:, None] - rq[None, :] + 7, 0, 14)
            dc = np.clip(wkk[:, None] - wq[None, :] + 15, 0, 30)
            for l in range(NL):
                for h in range(4):
                    vals = rpb[l, h][dr, dc]
                    nab[l, ti, :, h, i, :] = np.where(inwin, vals, f32(-30000.0))
    nabias = nab.reshape(NL, 5, 128, 2560)
    shared = dict(w_inA=w_inA, w_uq2=w_uq2, w_kv2=w_kv2, vecs=vecs, w_f1=g["hy_filt_w1"], w_f2=g["hy_filt_w2"],
                  w_f3=g["hy_filt_w3"], skipb=skipb, nabias=nabias, w_out=g["w_out"], w_gate=g["ffn_w_gate"],
                  w_up=g["ffn_w_up"], w_down=g["ffn_w_down"])
    shared = {k: np.ascontiguousarray(v, dtype=f32) for k, v in shared.items()}
    shared.update(host_constants())
    return shared


_NC = {}


def kernel(**inputs):
    shared = host_layout(inputs)
    x = np.asarray(inputs["x"], dtype=np.float32)
    if "nc" not in _NC:
        _NC["nc"] = build_program()
    nc = _NC["nc"]
    in_maps = []
    for c in range(8):
        m = dict(shared)
        m["x"] = np.ascontiguousarray(x[c])
        in_maps.append(m)
    res = run_bass_kernel_spmd(nc, in_maps, core_ids=list(range(8)))
    return np.stack([res.results[c]["out"] for c in range(8)], 0).astype(np.float32)
```

```python
import contextlib
import math
import numpy as np
import ml_dtypes
import concourse.bass as bass
import concourse.mybir as mybir
from concourse.bass_utils import run_bass_kernel_spmd

F32 = mybir.dt.float32
BF16 = mybir.dt.bfloat16
ALU = mybir.AluOpType
AF = mybir.ActivationFunctionType

S = 4096
D = 1024
NL = 2
DFF = 2816
NB = 8
TB = 512
NT = 32
NKF = 9
WIN_COLS = 2496
NV = 80
SEM_ROLL = 20000
MAGIC = 12582912.0
PI_SAFE = 3.1415925


class Dep:
    __slots__ = ("name", "writers", "readers", "multi", "dsem", "dval")
    scope = None

    def __init__(self, name="", multi=False):
        self.name = name
        self.writers = {}
        self.readers = {}
        self.multi = multi
        self.dsem = {}
        self.dval = 0
        if Dep.scope is not None:
            Dep.scope[-1].append(self)


def _merge(d, t):
    k = id(t[0])
    if k not in d or d[k][1] < t[1]:
        d[k] = t


class Prog:
    ENGS = ("pe", "act", "dve", "pool", "sp")

    def __init__(self, nc):
        self.nc = nc
        self.stacks = [contextlib.ExitStack()]
        self.q = {e: [] for e in self.ENGS}
        self.esem = {}
        self.ecnt = {e: 0 for e in self.ENGS}
        self.waited = {e: {} for e in self.ENGS}
        self.nsem = 0
        self.dma_t = {}
        self.uid = 0
        self.deferred = []
        self.free_sems = {"hw": [], "sw": []}
        Dep.scope = [[]]
        for e in self.ENGS:
            self.esem[e] = self.new_sem("c_" + e)

    def new_sem(self, name):
        self.nsem += 1
        return self.stacks[0].enter_context(self.nc.semaphore(f"s{self.nsem}_{name}"))

    def push(self):
        self.stacks.append(contextlib.ExitStack())
        Dep.scope.append([])

    def pop(self):
        self.barrier()
        self.stacks.pop().close()
        for d in Dep.scope.pop():
            for kind, sv in d.dsem.items():
                if sv[1] < SEM_ROLL:
                    self.free_sems[kind].append(sv)
            d.dsem = {}

    def acquire(self, name, kind):
        if self.free_sems[kind]:
            return self.free_sems[kind].pop()
        return [self.new_sem("d_" + kind + name), 0]

    def sbuf(self, name, shape, dt):
        self.uid += 1
        return self.stacks[-1].enter_context(self.nc.sbuf_tensor(f"{name}_{self.uid}", list(shape), dt))

    def psum(self, name, shape, dt=F32):
        return self.stacks[-1].enter_context(self.nc.psum_tensor(name, list(shape), dt))

    def _collect(self, eng, reads, writes):
        tk = {}
        for d in reads:
            for t in d.writers.values():
                _merge(tk, t)
        for d in writes:
            for t in d.readers.values():
                _merge(tk, t)
            if not d.multi:
                for t in d.writers.values():
                    _merge(tk, t)
        waits = []
        w = self.waited[eng]
        for k, (sem, val) in tk.items():
            if eng == "pe" and sem is self.esem["pe"]:
                continue
            if w.get(k, 0) >= val:
                continue
            w[k] = val
            waits.append((sem, val))
        return waits

    def _record(self, t, reads, writes):
        for d in reads:
            _merge(d.readers, t)
        for d in writes:
            if d.multi:
                _merge(d.writers, t)
            else:
                d.writers = {id(t[0]): t}
                d.readers = {}

    def op(self, eng, fn, reads=(), writes=()):
        waits = self._collect(eng, reads, writes)
        if self.ecnt[eng] >= SEM_ROLL:
            self.esem[eng] = self.new_sem("c_" + eng)
            self.ecnt[eng] = 0
        self.ecnt[eng] += 1
        t = (self.esem[eng], self.ecnt[eng])
        self.q[eng].append((waits, fn, self.esem[eng], 1))
        self._record(t, reads, writes)
        return t

    def dma(self, queue, out, in_, reads=(), writes=(), slot=None, **kw):
        waits = self._collect(queue, reads, writes)
        d0 = slot if slot is not None else writes[0]
        kind = "sw" if queue == "pool" else "hw"
        w = self.waited[queue]
        sv = d0.dsem.get(kind)
        if sv is not None and w.get(id(sv[0]), 0) < sv[1]:
            w[id(sv[0])] = sv[1]
            waits.append((sv[0], sv[1]))
        if sv is None or sv[1] >= SEM_ROLL:
            sv = self.acquire(d0.name, kind)
            d0.dsem[kind] = sv
        sv[1] += 16
        dsem = sv[0]
        t = (dsem, sv[1])
        self.dma_t[id(dsem)] = t

        def fn(e, out=out, in_=in_, kw=kw):
            return e.dma_start(out=out, in_=in_, **kw)

        self.q[queue].append((waits, fn, dsem, 16))
        self._record(t, reads, writes)
        return t

    def store(self, out, in_, reads=(), writes=(), slot=None):
        self.deferred.append((out, in_, reads, writes, slot))

    def flush(self):
        for out, in_, reads, writes, slot in self.deferred:
            self.dma("sp", out, in_, reads=reads, writes=writes, slot=slot)
        self.deferred = []

    def barrier(self):
        self.flush()
        for e in self.ENGS:
            w = self.waited[e]
            waits = []
            for e2 in self.ENGS:
                if e2 == e or self.ecnt[e2] == 0:
                    continue
                sem, val = self.esem[e2], self.ecnt[e2]
                if w.get(id(sem), 0) < val:
                    w[id(sem)] = val
                    waits.append((sem, val))
            for k, (sem, val) in self.dma_t.items():
                if w.get(k, 0) < val:
                    w[k] = val
                    waits.append((sem, val))
            self.q[e].append((waits, None, None, 0))

    def emit(self):
        nc = self.nc
        q = self.q
        with nc.Block() as block:
            def run(e, lst):
                for waits, fn, sem, inc in lst:
                    for (s, v) in waits:
                        e.wait_ge(s, v)
                    if fn is not None:
                        fn(e).then_inc(sem, inc)

            @block.tensor
            def _(e):
                run(e, q["pe"])

            @block.scalar
            def _(e):
                run(e, q["act"])

            @block.vector
            def _(e):
                run(e, q["dve"])

            @block.gpsimd
            def _(e):
                run(e, q["pool"])

            @block.sync
            def _(e):
                run(e, q["sp"])

    def close(self):
        while self.stacks:
            self.stacks.pop().close()

    def mm(self, out, lhsT, rhs, start, stop, reads, writes, skip=False):
        return self.op("pe", lambda e: e.matmul(out, lhsT, rhs, start=start, stop=stop,
                                                skip_group_check=skip), reads, writes)

    def tr(self, out, in_, ident, reads, writes):
        return self.op("pe", lambda e: e.transpose(out, in_, ident), reads, writes)

    def act(self, out, in_, func, reads, writes, scale=None, bias=None, accum=None):
        kw = {}
        if scale is not None:
            kw["scale"] = scale
        if bias is not None:
            kw["bias"] = bias
        if accum is not None:
            kw["accum_out"] = accum
        return self.op("act", lambda e: e.activation(out=out, in_=in_, func=func, **kw), reads, writes)

    def tt(self, eng, out, in0, in1, op, reads, writes):
        return self.op(eng, lambda e: e.tensor_tensor(out=out, in0=in0, in1=in1, op=op), reads, writes)

    def ts(self, eng, out, in0, s1, s2, op0, op1, reads, writes):
        if op1 is None and eng == "pool" and op0 == ALU.mult:
            op1, s2 = ALU.add, 0.0
        if op1 is None:
            return self.op(eng, lambda e: e.tensor_scalar(out=out, in0=in0, scalar1=s1, scalar2=None, op0=op0),
                           reads, writes)
        return self.op(eng, lambda e: e.tensor_scalar(out=out, in0=in0, scalar1=s1, scalar2=s2, op0=op0, op1=op1),
                       reads, writes)

    def stt(self, out, in0, scalar, in1, op0, op1, reads, writes):
        return self.op("dve", lambda e: e.scalar_tensor_tensor(out=out, in0=in0, scalar=scalar, in1=in1,
                                                               op0=op0, op1=op1), reads, writes)

    def cp(self, eng, out, in_, reads, writes):
        if eng == "act":
            return self.op("act", lambda e: e.activation(out=out, in_=in_, func=AF.Copy), reads, writes)
        return self.op(eng, lambda e: e.tensor_copy(out=out, in_=in_), reads, writes)

    def memset(self, eng, ap, val, writes):
        return self.op(eng, lambda e: e.memset(ap, val), (), writes)

    def recip(self, out, in_, reads, writes):
        return self.op("dve", lambda e: e.reciprocal(out=out, in_=in_), reads, writes)


class Ring:
    def __init__(self, P, name, shape, dt, n):
        self.t = [P.sbuf(f"{name}{i}", shape, dt) for i in range(n)]
        self.d = [Dep(f"{name}{i}") for i in range(n)]
        self.i = 0
        self.n = n

    def next(self):
        i = self.i
        self.i = (i + 1) % self.n
        return self.t[i], self.d[i]


class Banks:
    def __init__(self, P):
        self.ps = P.psum("psall", [128, 4096], F32)
        self.d = [Dep(f"bank{i}") for i in range(8)]
        self.i = 0

    def next(self):
        i = self.i
        self.i = (i + 1) % 8
        return self.ps[:, i * 512:(i + 1) * 512], self.d[i]

    def get(self, i):
        return self.ps[:, i * 512:(i + 1) * 512], self.d[i]

    def next2(self):
        if self.i % 2:
            self.i = (self.i + 1) % 8
        i = self.i
        self.i = (i + 2) % 8
        return self.ps[:, i * 512:(i + 2) * 512], self.d[i], self.d[i + 1]


def build_program(dbg=None):
    nc = bass.Bass("TRN2", target_bir_lowering=False)
    P = Prog(nc)
    skind = "ExternalOutput" if dbg else "Internal"

    def din(name, shape, dt=F32):
        return nc.dram_tensor(name, list(shape), dt, kind="ExternalInput").ap()

    def dscr(name, shape, dt=F32):
        return nc.dram_tensor(name, list(shape), dt, kind=skind).ap()

    x_in = din("x", [S, D])
    w_inA = din("w_inA", [NL, D, WIN_COLS])
    w_uq2 = din("w_uq2", [NL, 256, 1152])
    w_kv2 = din("w_kv2", [NL, 128, 768])
    vecs = din("vecs", [NL, 128, NV])
    w_f1 = din("w_f1", [NL, 17, 64])
    w_f2 = din("w_f2", [NL, 64, 64])
    w_f3 = din("w_f3", [NL, 64, 1536])
    skipb = din("skipb", [NL, 128, 768])
    nabias = din("nabias", [NL, 5, 128, 2560])
    w_out = din("w_out", [NL, D, D])
    w_gate = din("w_gate", [NL, D, DFF])
    w_up = din("w_up", [NL, D, DFF])
    w_down = din("w_down", [NL, DFF, D])
    ident_in = din("ident", [128, 128])
    ropeC = din("ropeC", [32, S])
    ropeS = din("ropeS", [32, S])
    zT_in = din("zT", [17, S])
    decay_in = din("decay", [S, 384])
    CfF = din("CfF", [NKF, 128, 4096], BF16)
    SfF = din("SfF", [NKF, 128, 4096], BF16)
    CfI = din("CfI", [NT, 128, NKF * 128], BF16)
    SfI = din("SfI", [NT, 128, NKF * 128], BF16)
    wk_in = din("wk", [128, 8 * NKF])
    out_ap = nc.dram_tensor("out", [S, D], F32, kind="ExternalOutput").ap()

    xT = dscr("xT", [8, 128, S])
    QT = dscr("QT", [6, 96, S], BF16)
    KT = dscr("KT", [6, 96, S], BF16)
    Vm = dscr("Vm", [6, 128, NT, 65], BF16)
    hyT = dscr("hyT", [9, 128, S])
    naQT = dscr("naQT", [2, 128, S], BF16)
    naKT = dscr("naKT", [2, 128, S], BF16)
    naV = dscr("naV", [NT, 128, 260], BF16)
    ymix = dscr("ymix", [S, D])
    xg = dscr("xg", [2, S, 384])
    Gs = dscr("Gs", [2, NKF, 128, 4 * 768])
    d_xT = [Dep(f"xT{b}") for b in range(NB)]
    hAs = dscr("hAs", [NB, 128, 8, TB], BF16)
    d_hAs = [Dep(f"hAs{b}") for b in range(NB)]
    h2s = dscr("h2s", [NB, 128, 8, TB], BF16)
    d_h2s = [Dep(f"h2s{b}") for b in range(NB)]
    d_QT = Dep("QT", multi=True)
    d_KT = Dep("KT", multi=True)
    d_Vm = Dep("Vm", multi=True)
    d_hyT = Dep("hyT", multi=True)
    d_naQT = Dep("naQT", multi=True)
    d_naKT = Dep("naKT", multi=True)
    d_naV = Dep("naV", multi=True)
    d_ymix = Dep("ymix", multi=True)
    d_xg = Dep("xg", multi=True)
    d_Gs = Dep("Gs", multi=True)
    d_out = Dep("out", multi=True)

    banks = Banks(P)
    ident = P.sbuf("ident", [128, 128], F32)
    d_ident = Dep("ident")
    P.dma("sp", ident[:], ident_in, writes=[d_ident])
    ones_b = P.sbuf("ones_b", [128, 128], BF16)
    d_const = Dep("const")
    P.memset("dve", ones_b[:], 1.0, [d_const])
    epsc = P.sbuf("epsc", [128, 1], F32)
    P.memset("dve", epsc[:], 1e-6, [d_const])
    vec = [P.sbuf(f"vec{l}", [128, NV], F32) for l in range(NL)]
    d_vec = Dep("vec")
    for l in range(NL):
        P.dma("sp", vec[l][:], vecs[l], writes=[d_vec])
    wkt = P.sbuf("wkt", [128, 8 * NKF], F32)
    P.dma("sp", wkt[:], wk_in, writes=[d_vec])

    def stop(name):
        return dbg == name

    def finish():
        P.barrier()
        P.emit()
        P.close()
        return nc

    def rms_p1(xt, d_xt, nch, sq, d_sq):
        P.act(sq[:, 0:nch, :], xt[:, 0:nch, :], AF.Square, [d_xt], [d_sq])

    def rms_p2(xt, d_xt, nch, gcol, vl, outT, d_out_, sq, d_sq, rs, d_rs, n_feat):
        bk, d_bk = banks.next()
        for k in range(nch):
            P.mm(bk, ones_b[:, :], sq[:, k, :], k == 0, k == nch - 1, [d_sq, d_const], [d_bk])
        P.act(rs[:, :], bk, AF.Sqrt, [d_bk, d_const], [d_rs], scale=1.0 / n_feat, bias=epsc[:, 0:1])
        P.recip(rs[:, :], rs[:, :], [d_rs], [d_rs])
        for k in range(nch):
            P.stt(outT[:, k, :], xt[:, k, :], vl[:, gcol + k:gcol + k + 1], rs[:, :], ALU.mult, ALU.mult,
                  [d_xt, d_rs, d_vec], [d_out_])

    def rms_feature_major(xt, d_xt, nch, gcol, vl, outT, d_out_, sq, d_sq, rs, d_rs, n_feat):
        rms_p1(xt, d_xt, nch, sq, d_sq)
        rms_p2(xt, d_xt, nch, gcol, vl, outT, d_out_, sq, d_sq, rs, d_rs, n_feat)

    def tcols(ap2, tau):
        return ap2.rearrange("q (a p r) -> q r a p", a=8, p=128, r=4)[:, tau // 8, tau % 8, :]

    def trows(ap2, tau):
        return ap2.rearrange("(a p r) c -> r a p c", a=8, p=128, r=4)[tau // 8, tau % 8]

    class FoldBufs:
        def __init__(self):
            self.cp = Ring(P, "fcp", [128, 384], F32, 4)
            self.l1 = {n: Ring(P, "f1" + n, [128, 384], F32, 1) for n in ("P", "Q", "Pp", "Qp", "R", "T", "Rp", "Tp")}
            self.l2 = {n: Ring(P, "f2" + n, [128, 384], F32, 1) for n in
                       ("A1", "A2", "A3", "A4", "B1", "B2", "B3", "B4")}

    def fold_forward(*a, **k):
        g = fold_forward_g(*a, **k)
        try:
            while True:
                next(g)
        except StopIteration as e:
            return e.value

    def fold_forward_g(fb, src, d_src, Cb, Sb, d_C, d_S, need, l2eng=None, bases=(0, 4), cp_eng="act"):
        Cv = Cb[:, :].rearrange("p (r a f) -> p r a f", r=4, a=8)
        Sv = Sb[:, :].rearrange("p (r a f) -> p r a f", r=4, a=8)
        L1 = {}
        hp = 0
        for ps_, (ra, rb) in enumerate(((0, 2), (1, 3))):
            for hf, (Mv, d_M) in enumerate(((Cv, d_C), (Sv, d_S))):
                if bases[0] == bases[1]:
                    bb = bases[0] + 2 * (hp % 2)
                else:
                    bb = bases[ps_] + 2 * hf
                hp += 1
                bka, bkb = banks.get(bb), banks.get(bb + 1)
                for (r, bk) in ((ra, bka), (rb, bkb)):
                    for a in range(8):
                        P.mm(bk[0][:, 0:384], Mv[:, r, a, :], src(r * 8 + a), a == 0, a == 7, [d_M, d_src], [bk[1]])
                        if a % 2:
                            yield
                cc, d_cc = fb.cp.next()
                P.cp(cp_eng, cc[:, :], bkb[0][:, 0:384], [bkb[1]], [d_cc])
                if ps_ == 0:
                    names = ("P", "Q") if hf == 0 else ("Pp", "Qp")
                else:
                    names = ("R", "T") if hf == 0 else ("Rp", "Tp")
                for nm, op in zip(names, (ALU.add, ALU.subtract)):
                    t_, d_t = fb.l1[nm].next()
                    P.tt("dve", t_[:, :], bka[0][:, 0:384], cc[:, :], op, [bka[1], d_cc], [d_t])
                    L1[nm] = (t_, d_t)
        out = {}

        def l2(nm, x, y, op):
            t_, d_t = fb.l2[nm].next()
            eng = "pool" if l2eng is None else l2eng[nm[1]]
            P.tt(eng, t_[:, :], L1[x][0][:, :], L1[y][0][:, :], op, [L1[x][1], L1[y][1]], [d_t])
            out[nm] = (t_, d_t)

        if "A" in need:
            l2("A1", "P", "R", ALU.add)
            l2("A4", "P", "R", ALU.subtract)
            l2("A2", "Q", "Tp", ALU.add)
            l2("A3", "Q", "Tp", ALU.subtract)
        if "B" in need:
            l2("B1", "Pp", "Rp", ALU.add)
            l2("B4", "Rp", "Pp", ALU.subtract)
            l2("B2", "T", "Qp", ALU.subtract)
            l2("B3", "Qp", "T", ALU.add)
        return out

    P.push()
    xin_r = Ring(P, "xin", [128, D], F32, 2)
    xtb_r = Ring(P, "xtb0", [128, 8, TB], F32, 2)
    hA0_r = Ring(P, "hA0", [128, 8, TB], BF16, 2)
    sq0 = P.sbuf("sq0", [128, 8, TB], BF16)
    d_sq0 = Dep("sq0")
    rs0 = P.sbuf("rs0", [128, TB], F32)
    d_rs0 = Dep("rs0")
    for b in range(NB):
        xtb, d_xtb = xtb_r.next()
        for i in range(4):
            ti = b * 4 + i
            xin, d_xin = xin_r.next()
            P.dma("sp", xin[:], x_in[ti * 128:(ti + 1) * 128, :], writes=[d_xin])
            if i == 0:
                P.flush()
            for j in range(2):
                bk, d_bk = banks.next()
                for c in range(4):
                    k = j * 4 + c
                    P.tr(bk[:, c * 128:(c + 1) * 128], xin[:, k * 128:(k + 1) * 128], ident[:, :],
                         [d_xin, d_ident], [d_bk])
                P.cp("act" if j == 0 else "dve", xtb[:, j * 4:(j + 1) * 4, i * 128:(i + 1) * 128],
                     bk.rearrange("p (a b) -> p a b", a=4), [d_bk], [d_xtb])
        P.store(xT[:, :, b * TB:(b + 1) * TB].rearrange("k p s -> p k s"), xtb[:], reads=[d_xtb], writes=[d_xT[b]])
        hA0, d_hA0 = hA0_r.next()
        rms_feature_major(xtb, d_xtb, 8, 0, vec[0], hA0, d_hA0, sq0, d_sq0, rs0, d_rs0, D)
        P.store(hAs[b], hA0[:], reads=[d_hA0], writes=[d_hAs[b]], slot=d_hA0)
    P.pop()
    if stop("p0"):
        return finish()

    for l in range(NL):
        vl = vec[l]
        P.push()
        winA = P.sbuf("winA", [128, 8, WIN_COLS], BF16)
        d_w = Dep("winA")
        P.dma("pool", winA[:], w_inA[l].rearrange("(k p) c -> p k c", p=128), writes=[d_w])
        wuq = P.sbuf("wuq", [128, 2, 1152], BF16)
        d_wuq = Dep("wuq")
        P.dma("pool", wuq[:], w_uq2[l].rearrange("(j p) c -> p j c", p=128), writes=[d_wuq])
        wkv = P.sbuf("wkv", [128, 768], BF16)
        d_wkv = Dep("wkv")
        P.dma("pool", wkv[:], w_kv2[l], writes=[d_wkv])
        sqq = P.sbuf("sqq", [128, 2, TB], BF16)
        d_sqq = Dep("sqq")
        rsq = P.sbuf("rsq", [128, TB], F32)
        d_rsq = Dep("rsq")
        sqk = P.sbuf("sqk", [128, 1, TB], BF16)
        d_sqk = Dep("sqk")
        rsk = P.sbuf("rsk", [128, TB], F32)
        d_rsk = Dep("rsk")
        hT_r = Ring(P, "hT", [128, 8, TB], BF16, 3)
        cq = P.sbuf("cq", [128, 2, TB], F32)
        d_cq = Dep("cq")
        cqn = P.sbuf("cqn", [128, 2, TB], BF16)
        d_cqn = Dep("cqn")
        ckv = P.sbuf("ckv", [128, 1, TB], F32)
        d_ckv = Dep("ckv")
        ckvn = P.sbuf("ckvn", [128, 1, TB], BF16)
        d_ckvn = Dep("ckvn")
        rC_r = Ring(P, "rC", [128, TB], F32, 2)
        rS_r = Ring(P, "rS", [128, TB], F32, 2)
        tA_r = Ring(P, "tA", [128, TB], F32, 2)
        tB_r = Ring(P, "tB", [128, TB], F32, 2)
        QTb_r = Ring(P, "QTb", [128, 6, TB], BF16, 2)
        KTb_r = Ring(P, "KTb", [128, 6, TB], BF16, 2)
        Vb_r = Ring(P, "Vb", [128, 6, 4, 65], BF16, 2)
        hyb_r = Ring(P, "hyb", [128, 3, TB], F32, 3)
        nqk_r = Ring(P, "nqk", [128, 4, TB], BF16, 2)
        nvb_r = Ring(P, "nvb", [128, 4, 260], BF16, 2)
        for r_ in (Vb_r, nvb_r):
            for t_, d_ in zip(r_.t, r_.d):
                P.memset("pool", t_[:], 1.0, [d_])
        def ld_hA(bb):
            t_, d_ = hT_r.next()
            P.dma("sp", t_[:], hAs[bb], reads=[d_hAs[bb]], writes=[d_])
            return t_, d_

        nxt_hA = ld_hA(0)
        for b in range(NB):
            sl = slice(b * TB, (b + 1) * TB)
            hT, d_hT = nxt_hA
            if b + 1 < NB:
                nxt_hA = ld_hA(b + 1)
            rC, d_rC = rC_r.next()
            rS, d_rS = rS_r.next()
            P.dma("sp", rC[64:96, :], ropeC[:, sl], writes=[d_rC])
            P.dma("sp", rS[64:96, :], ropeS[:, sl], writes=[d_rS])
            P.flush()

            def proj(c0, M):
                bk, d_bk = banks.next()
                for k in range(8):
                    P.mm(bk[0:M, :], winA[:, k, c0:c0 + M], hT[:, k, :], k == 0, k == 7, [d_w, d_hT], [d_bk])
                return bk, d_bk

            for j in range(2):
                bk, d_bk = proj(j * 128, 128)
                P.cp("act", cq[:, j, :], bk, [d_bk], [d_cq])
            bk, d_bk = proj(256, 128)
            P.cp("act", ckv[:, 0, :], bk, [d_bk], [d_ckv])
            rms_p1(cq, d_cq, 2, sqq, d_sqq)
            rms_p1(ckv, d_ckv, 1, sqk, d_sqk)

            KTb, d_KTb = KTb_r.next()
            bk, d_bk = proj(384, 96)
            bk2, d_bk2 = proj(480, 96)
            tA, d_tA = tA_r.next()
            tB, d_tB = tB_r.next()
            P.tt("dve", tA[64:96, :], bk[64:96, :], rC[64:96, :], ALU.mult, [d_bk, d_rC], [d_tA])
            P.tt("dve", tB[64:96, :], bk2[64:96, :], rS[64:96, :], ALU.mult, [d_bk2, d_rS], [d_tB])
            for h in range(6):
                P.tt("pool", KTb[64:96, h, :], tA[64:96, :], tB[64:96, :], ALU.add, [d_tA, d_tB], [d_KTb])

            for g3 in range(3):
                hyb, d_hyb = hyb_r.next()
                for c3 in range(3):
                    c = g3 * 3 + c3
                    bk, d_bk = proj(576 + c * 128, 128)
                    P.cp("act" if c % 2 else "dve", hyb[:, c3, :], bk, [d_bk], [d_hyb])
                P.store(hyT[g3 * 3:(g3 + 1) * 3, :, sl].rearrange("c p s -> p c s"), hyb[:], reads=[d_hyb],
                      writes=[d_hyT], slot=d_hyb)
                if g3 == 0:
                    rms_p2(cq, d_cq, 2, 16, vl, cqn, d_cqn, sqq, d_sqq, rsq, d_rsq, 256)
                    rms_p2(ckv, d_ckv, 1, 18, vl, ckvn, d_ckvn, sqk, d_sqk, rsk, d_rsk, 128)

            nqk, d_nqk = nqk_r.next()
            for c in range(4):
                bk, d_bk = proj(1728 + c * 128, 128)
                P.cp("act" if c % 2 else "dve", nqk[:, c, :], bk, [d_bk], [d_nqk])
            P.store(naQT[:, :, sl].rearrange("c p s -> p c s"), nqk[:, 0:2, :], reads=[d_nqk], writes=[d_naQT],
                  slot=d_nqk)
            P.store(naKT[:, :, sl].rearrange("c p s -> p c s"), nqk[:, 2:4, :], reads=[d_nqk], writes=[d_naKT],
                  slot=d_nqk)
            nvb, d_nvb = nvb_r.next()
            for i in range(4):
                bk, d_bk = banks.next()
                for k in range(8):
                    P.mm(bk[:, 0:256], hT[:, k, i * 128:(i + 1) * 128], winA[:, k, 2240:2496], k == 0, k == 7,
                         [d_w, d_hT], [d_bk])
                P.cp("act" if i % 2 else "dve",
                     nvb[:, i, :].rearrange("p (h c) -> p h c", h=4)[:, :, 0:64],
                     bk[:, 0:256].rearrange("p (h c) -> p h c", h=4), [d_bk], [d_nvb])
            P.store(naV[b * 4:(b + 1) * 4].rearrange("t p c -> p t c"), nvb[:], reads=[d_nvb], writes=[d_naV],
                  slot=d_nvb)

            QTb, d_QTb = QTb_r.next()
            for h in range(6):
                bk, d_bk = banks.next()
                bk2, d_bk2 = banks.next()
                for j in range(2):
                    P.mm(bk[0:96, :], wuq[:, j, h * 192:h * 192 + 96], cqn[:, j, :], j == 0, j == 1, [d_wuq, d_cqn], [d_bk])
                for j in range(2):
                    P.mm(bk2[0:96, :], wuq[:, j, h * 192 + 96:h * 192 + 192], cqn[:, j, :], j == 0, j == 1,
                         [d_wuq, d_cqn], [d_bk2])
                P.cp("act", QTb[0:64, h, :], bk[0:64, :], [d_bk], [d_QTb])
                tA, d_tA = tA_r.next()
                tB, d_tB = tB_r.next()
                P.tt("dve", tA[64:96, :], bk[64:96, :], rC[64:96, :], ALU.mult, [d_bk, d_rC], [d_tA])
                P.tt("dve", tB[64:96, :], bk2[64:96, :], rS[64:96, :], ALU.mult, [d_bk2, d_rS], [d_tB])
                P.tt("pool", QTb[64:96, h, :], tA[64:96, :], tB[64:96, :], ALU.add, [d_tA, d_tB], [d_QTb])
            P.store(QT[:, :, sl].rearrange("h p s -> p h s"), QTb[0:96, :, :], reads=[d_QTb], writes=[d_QT],
                  slot=d_QTb)

            for h in range(6):
                bk, d_bk = banks.next()
                P.mm(bk[0:64, :], wkv[:, h * 64:(h + 1) * 64], ckvn[:, 0, :], True, True, [d_wkv, d_ckvn], [d_bk])
                P.cp("act" if h % 2 else "dve", KTb[0:64, h, :], bk[0:64, :], [d_bk], [d_KTb])
            P.store(KT[:, :, sl].rearrange("h p s -> p h s"), KTb[0:96, :, :], reads=[d_KTb], writes=[d_KT],
                  slot=d_KTb)
            Vb, d_Vb = Vb_r.next()
            for i in range(4):
                bk, d_bk = banks.next()
                P.mm(bk[:, 0:384], ckvn[:, 0, i * 128:(i + 1) * 128], wkv[:, 384:768], True, True,
                     [d_wkv, d_ckvn], [d_bk])
                P.cp("act" if i % 2 else "dve", Vb[:, :, i, 0:64],
                     bk[:, 0:384].rearrange("p (h c) -> p h c", h=6), [d_bk], [d_Vb])
            P.store(Vm[:, :, b * 4:(b + 1) * 4, :].rearrange("h p t c -> p h t c"), Vb[:], reads=[d_Vb], writes=[d_Vm],
                  slot=d_Vb)
        P.pop()
        if stop(f"A{l}"):
            return finish()

        P.push()
        hs = P.sbuf("hs", [128, NT, 768], BF16)
        hd = P.sbuf("hd", [128, NT, 768], BF16)
        d_hsd = Dep("hsd", multi=True)
        P.push()
        zT = P.sbuf("zT", [17, S], F32)
        wf1 = P.sbuf("wf1", [17, 64], F32)
        wf2 = P.sbuf("wf2", [64, 64], F32)
        wf3 = P.sbuf("wf3", [64, 1536], F32)
        d_fw = Dep("fw", multi=True)
        P.dma("sp", zT[:], zT_in, writes=[d_fw])
        P.dma("sp", wf1[:], w_f1[l], writes=[d_fw])
        P.dma("sp", wf2[:], w_f2[l], writes=[d_fw])
        P.dma("sp", wf3[:], w_f3[l], writes=[d_fw])
        hid1 = P.sbuf("hid1", [64, S], F32)
        hid2 = P.sbuf("hid2", [64, S], F32)
        d_h1 = Dep("hid1", multi=True)
        d_h2 = Dep("hid2", multi=True)
        fa_r = Ring(P, "fa", [64, TB], F32, 2)
        ft_r = Ring(P, "ft", [64, TB], F32, 2)

        def sin_block(bk, d_bk, bcol, fcol, out, d_o):
            a, d_a = fa_r.next()
            t, d_t = ft_r.next()
            P.ts("dve", a[:, :], bk[0:64, :], vl[0:64, bcol:bcol + 1], vl[0:64, fcol:fcol + 1], ALU.add, ALU.mult,
                 [d_bk, d_vec], [d_a])
            P.ts("dve", t[:, :], a[:, :], 1.0 / (2.0 * math.pi), MAGIC, ALU.mult, ALU.add, [d_a], [d_t])
            P.ts("dve", t[:, :], t[:, :], MAGIC, -2.0 * math.pi, ALU.subtract, ALU.mult, [d_t], [d_t])
            P.tt("dve", a[:, :], a[:, :], t[:, :], ALU.add, [d_a, d_t], [d_a])
            P.ts("dve", a[:, :], a[:, :], -PI_SAFE, PI_SAFE, ALU.max, ALU.min, [d_a], [d_a])
            P.act(out, a[:, :], AF.Sin, [d_a], [d_o])

        for b in range(NB):
            sl = slice(b * TB, (b + 1) * TB)
            bk, d_bk = banks.next()
            P.mm(bk[0:64, :], wf1[:, :], zT[:, sl], True, True, [d_fw], [d_bk])
            sin_block(bk, d_bk, 71, 72, hid1[:, sl], d_h1)
        for b in range(NB):
            sl = slice(b * TB, (b + 1) * TB)
            bk, d_bk = banks.next()
            P.mm(bk[0:64, :], wf2[:, :], hid1[:, sl], True, True, [d_fw, d_h1], [d_bk])
            sin_block(bk, d_bk, 73, 74, hid2[:, sl], d_h2)
        hraw_r = Ring(P, "hraw", [128, 1536], F32, 2)
        dk_r = Ring(P, "dk", [128, 384], F32, 2)
        fs_r = Ring(P, "fs", [128, 2, 384], F32, 2)
        fd_r = Ring(P, "fd", [128, 2, 384], F32, 2)
        for i in range(NT):
            hraw, d_hr = hraw_r.next()
            for n in range(3):
                bk, d_bk = banks.next()
                P.mm(bk, tcols(hid2[:, :], i), wf3[:, n * 512:(n + 1) * 512], True, True, [d_fw, d_h2], [d_bk])
                P.cp("act", hraw[:, n * 512:(n + 1) * 512], bk, [d_bk], [d_hr])
            dk, d_dk = dk_r.next()
            P.dma("sp", dk[:], trows(decay_in, i), writes=[d_dk])
            fs, d_fs = fs_r.next()
            fd, d_fd = fd_r.next()
            hv = hraw[:, :].rearrange("p (o r c) -> p o r c", o=2, r=2)
            P.tt("dve", fs[:, :, :], hv[:, :, 0, :], hv[:, :, 1, :], ALU.add, [d_hr], [d_fs])
            P.tt("dve", fd[:, :, :], hv[:, :, 0, :], hv[:, :, 1, :], ALU.subtract, [d_hr], [d_fd])
            for o in range(2):
                P.tt("pool", hs[:, i, o * 384:(o + 1) * 384], fs[:, o, :], dk[:, :], ALU.mult, [d_fs, d_dk], [d_hsd])
                P.tt("pool", hd[:, i, o * 384:(o + 1) * 384], fd[:, o, :], dk[:, :], ALU.mult, [d_fd, d_dk], [d_hsd])
        P.pop()
        if stop(f"HF{l}"):
            dbg_h = nc.dram_tensor("dbg_hs", [NT, 128, 768], F32, kind="ExternalOutput").ap()
            dbg_h2 = nc.dram_tensor("dbg_hd", [NT, 128, 768], F32, kind="ExternalOutput").ap()
            d_dbg = Dep("dbg", multi=True)
            P.store(dbg_h.rearrange("t p c -> p t c"), hs[:], reads=[d_hsd], writes=[d_dbg])
            P.store(dbg_h2.rearrange("t p c -> p t c"), hd[:], reads=[d_hsd], writes=[d_dbg])
            return finish()
        P.push()
        skb = P.sbuf("skb", [128, 768], F32)
        d_skb = Dep("skb")
        P.dma("sp", skb[:], skipb[l], writes=[d_skb])
        Cb_r = Ring(P, "CbF", [128, 4096], BF16, 1)
        Sb_r = Ring(P, "SbF", [128, 4096], BF16, 1)
        gst_r = Ring(P, "gst", [128, 4, 768], F32, 1)
        gtmp_r = Ring(P, "gtmp", [128, 384], F32, 2)
        fb = FoldBufs()

        def a2_gen():
            for kc in range(NKF):
                Cb, d_C = Cb_r.next()
                Sb, d_S = Sb_r.next()
                P.dma("sp", Cb[:], CfF[kc], writes=[d_C])
                P.dma("sp", Sb[:], SfF[kc], writes=[d_S])
                for o in range(2):
                    gst, d_gst = gst_r.next()
                    oa = yield from fold_forward_g(fb, lambda tau, o=o: hs[:, tau, o * 384:(o + 1) * 384], d_hsd,
                                                   Cb, Sb, d_C, d_S, "A", bases=(4, 4), cp_eng="dve")
                    for j in range(4):
                        gt_, d_gt_ = gtmp_r.next()
                        aj, d_aj = oa[f"A{j + 1}"]
                        P.tt("pool", gt_[:, :], aj[:, :], skb[:, o * 384:(o + 1) * 384], ALU.add, [d_aj, d_skb],
                             [d_gt_])
                        P.ts("pool", gst[:, j, 0:384], gt_[:, :], wkt[:, kc * 4 + j:kc * 4 + j + 1], None, ALU.mult,
                             None, [d_gt_, d_vec], [d_gst])
                    ob = yield from fold_forward_g(fb, lambda tau, o=o: hd[:, tau, o * 384:(o + 1) * 384], d_hsd,
                                                   Cb, Sb, d_C, d_S, "B", bases=(4, 4), cp_eng="dve")
                    for j in range(4):
                        bj, d_bj = ob[f"B{j + 1}"]
                        P.ts("pool", gst[:, j, 384:768], bj[:, :],
                             wkt[:, 4 * NKF + kc * 4 + j:4 * NKF + kc * 4 + j + 1], None, ALU.mult, None,
                             [d_bj, d_vec], [d_gst])
                    P.dma("sp", Gs[o, kc], gst[:].rearrange("p j c -> p (j c)"), reads=[d_gst], writes=[d_Gs],
                          slot=d_gst)

        a2g = a2_gen()

        def a2_step():
            try:
                next(a2g)
            except StopIteration:
                pass

        Vh_r = Ring(P, "Vh", [128, NT, 65], BF16, 2)
        KTh_r = Ring(P, "KTh", [96, S], BF16, 1)
        QTh_r = Ring(P, "QTh", [96, S], BF16, 1)
        pT_r = Ring(P, "pT", [128, TB], BF16, 6)
        rd_r = Ring(P, "rdM", [128, 4], F32, 2)
        yas_r = Ring(P, "yas", [128, 4, 64], F32, 2)
        sc_m = 1.0 / math.sqrt(96.0)
        s_rot = 0
        for h in range(6):
            KTh, d_K = KTh_r.next()
            QTh, d_Q = QTh_r.next()
            Vh, d_V = Vh_r.next()
            P.dma("sp", KTh[:], KT[h], reads=[d_KT], writes=[d_K])
            P.dma("sp", QTh[:], QT[h], reads=[d_QT], writes=[d_Q])
            P.dma("sp", Vh[:], Vm[h], reads=[d_Vm], writes=[d_V])
            for qb in range(NB):
                po, d_po = banks.get(3)
                pts = []
                for step in range(NT + 2):
                    if step < NT:
                        kt = step
                        sb, d_sb = banks.get(s_rot % 3)
                        s_rot += 1
                        P.mm(sb, KTh[:, kt * 128:(kt + 1) * 128], QTh[:, qb * TB:(qb + 1) * TB], True, True,
                             [d_K, d_Q], [d_sb])
                        pT, d_pT = pT_r.next()
                        P.act(pT[:, :], sb, AF.Exp, [d_sb], [d_pT], scale=sc_m)
                        pts.append((pT, d_pT))
                        a2_step()
                    if step >= 2:
                        kt = step - 2
                        pT, d_pT = pts[kt]
                        for j in range(4):
                            P.mm(po[:, j * 65:(j + 1) * 65], pT[:, j * 128:(j + 1) * 128],
                                 Vh[:, kt, :], kt == 0 and j == 0, kt == NT - 1 and j == 3,
                                 [d_pT, d_V], [d_po], skip=True)
                rdt, d_rd = rd_r.next()
                P.recip(rdt[:, 0:4], po[:, 0:260].rearrange("p (j c) -> p j c", c=65)[:, :, 64], [d_po], [d_rd])
                yas, d_yas = yas_r.next()
                for j in range(4):
                    P.ts("dve", yas[:, j, :], po[:, j * 65:j * 65 + 64], rdt[:, j:j + 1],
                         None, ALU.mult, None, [d_po, d_rd], [d_yas])
                P.dma("sp", ymix[qb * TB:(qb + 1) * TB, h * 64:(h + 1) * 64].rearrange("(t p) c -> p t c", p=128),
                      yas[:], reads=[d_yas], writes=[d_ymix], slot=d_yas)
        for _ in a2g:
            pass
        P.pop()
        P.pop()
        if stop(f"M{l}"):
            return finish()
        P.push()
        nq = P.sbuf("nq", [128, 2, S], BF16)
        nk = P.sbuf("nk", [128, 2, S], BF16)
        nv = P.sbuf("nv", [128, NT, 260], BF16)
        nbt = P.sbuf("nbt", [128, 5, 2560], F32)
        yc = P.sbuf("yc", [128, NT, 256], F32)
        d_nin = Dep("nin", multi=True)
        d_yc = Dep("yc", multi=True)
        P.dma("sp", nq[:], naQT.rearrange("c p s -> p c s"), reads=[d_naQT], writes=[d_nin])
        P.dma("sp", nk[:], naKT.rearrange("c p s -> p c s"), reads=[d_naKT], writes=[d_nin])
        P.dma("sp", nv[:], naV.rearrange("t p c -> p t c"), reads=[d_naV], writes=[d_nin])
        P.dma("sp", nbt[:], nabias[l].rearrange("t p c -> p t c"), writes=[d_nin])
        tS_r = Ring(P, "tS", [128, 640], F32, 4)
        pN_r = Ring(P, "pN", [128, 640], BF16, 4)
        rdn_r = Ring(P, "rdN", [128, 4], F32, 2)
        items = [(m, h) for m in range(NT) for h in range(4)]
        LOOK = 2
        pendq = []
        for idx in range(len(items) + LOOK):
            if idx < len(items):
                m, h = items[idx]
                ty = {0: 0, 1: 1, 30: 2, 31: 3}.get(m, 4)
                c0 = min(max(m - 2, 0), 27)
                j, pb = h // 2, 64 * (h % 2)
                bi = 2 * (idx % 3)
                s2 = banks.ps[:, bi * 512:(bi + 2) * 512]
                d_a, d_b = banks.d[bi], banks.d[bi + 1]
                for i in range(5):
                    P.mm(s2[:, i * 128:(i + 1) * 128], nk[pb:pb + 64, j, (c0 + i) * 128:(c0 + i + 1) * 128],
                         nq[pb:pb + 64, j, m * 128:(m + 1) * 128], True, True, [d_nin], [d_a if i < 4 else d_b])
                tS, d_tS = tS_r.next()
                P.stt(tS[:, :], s2[:, 0:640], 0.125, nbt[:, ty, h * 640:(h + 1) * 640], ALU.mult, ALU.add,
                      [d_a, d_b, d_nin], [d_tS])
                pN, d_pN = pN_r.next()
                P.act(pN[:, :], tS[:, :], AF.Exp, [d_tS], [d_pN])
                pendq.append((m, h, c0, pN, d_pN))
            if idx >= LOOK:
                m, h, c0, pN, d_pN = pendq.pop(0)
                po, d_po = banks.get(6 + m % 2)
                for i in range(5):
                    P.mm(po[:, h * 65:(h + 1) * 65], pN[:, i * 128:(i + 1) * 128], nv[:, c0 + i, h * 65:(h + 1) * 65],
                         h == 0 and i == 0, h == 3 and i == 4, [d_pN, d_nin], [d_po], skip=True)
                if h == 3:
                    rdt, d_rd = rdn_r.next()
                    P.recip(rdt[:, 0:4], po[:, 0:260].rearrange("p (j c) -> p j c", c=65)[:, :, 64], [d_po], [d_rd])
                    for hh in range(4):
                        P.ts("dve", yc[:, m, hh * 64:(hh + 1) * 64], po[:, hh * 65:hh * 65 + 64], rdt[:, hh:hh + 1],
                             None, ALU.mult, None, [d_po, d_rd], [d_yc])
        for q4 in range(4):
            P.store(ymix[q4 * 1024:(q4 + 1) * 1024, 768:1024].rearrange("(t p) c -> p t c", p=128),
                  yc[:, q4 * 8:(q4 + 1) * 8, :], reads=[d_yc], writes=[d_ymix])
        P.pop()
        if stop(f"N{l}"):
            return finish()


        P.push()
        u_tok = P.sbuf("u_tok", [128, NT, 384], BF16)
        d_u = Dep("u_tok", multi=True)
        P.push()
        hyc_r = Ring(P, "hyc", [128, S + 2], F32, 2)
        ucT_r = Ring(P, "ucT", [128, S], F32, 2)
        xst_r = Ring(P, "xst", [128, NT, 128], F32, 2)
        for t_, d_ in zip(hyc_r.t, hyc_r.d):
            P.memset("pool", t_[:, 0:1], 0.0, [d_])
            P.memset("pool", t_[:, S + 1:S + 2], 0.0, [d_])
        for c in range(9):
            hyc, d_hyc = hyc_r.next()
            P.dma("sp", hyc[:, 1:S + 1], hyT[c], reads=[d_hyT], writes=[d_hyc])
            P.flush()
            ucT, d_uc = ucT_r.next()
            w0 = vl[:, 35 + c:36 + c]
            w1 = vl[:, 44 + c:45 + c]
            w2 = vl[:, 53 + c:54 + c]
            bb = vl[:, 62 + c:63 + c]
            P.ts("dve", ucT[:, :], hyc[:, 1:S + 1], w1, bb, ALU.mult, ALU.add, [d_hyc, d_vec], [d_uc])
            P.stt(ucT[:, :], hyc[:, 0:S], w0, ucT[:, :], ALU.mult, ALU.add, [d_hyc, d_uc, d_vec], [d_uc])
            P.stt(ucT[:, :], hyc[:, 2:S + 2], w2, ucT[:, :], ALU.mult, ALU.add, [d_hyc, d_uc, d_vec], [d_uc])
            if c >= 3:
                xst, d_xst = xst_r.next()
            for g4 in range(8):
                bk, d_bk = banks.next()
                for q in range(4):
                    ti = g4 * 4 + q
                    P.tr(bk[:, q * 128:(q + 1) * 128], tcols(ucT[:, :], ti), ident[:, :],
                         [d_uc, d_ident], [d_bk])
                bv = bk.rearrange("p (a b) -> p a b", a=4)
                if c < 3:
                    P.cp("act" if g4 % 2 else "dve", u_tok[:, g4 * 4:(g4 + 1) * 4, c * 128:(c + 1) * 128], bv,
                         [d_bk], [d_u])
                else:
                    P.cp("act" if g4 % 2 else "dve", xst[:, g4 * 4:(g4 + 1) * 4, :], bv, [d_bk], [d_xst])
            if c >= 3:
                o, cc = (c - 3) // 3, (c - 3) % 3
                for q4 in range(4):
                    P.store(xg[o, q4 * 1024:(q4 + 1) * 1024, cc * 128:(cc + 1) * 128].rearrange(
                        "(t p) c -> p t c", p=128), xst[:, q4 * 8:(q4 + 1) * 8, :], reads=[d_xst], writes=[d_xg],
                          slot=d_xst)
        P.pop()
        if stop(f"HP{l}"):
            dbg_u = nc.dram_tensor("dbg_u", [NT, 128, 384], F32, kind="ExternalOutput").ap()
            d_dbg = Dep("dbg", multi=True)
            P.store(dbg_u.rearrange("t p c -> p t c"), u_tok[:], reads=[d_u], writes=[d_dbg])
            return finish()

        Ec = P.sbuf("Ec", [128, 4, NKF, 384], BF16)
        Es = P.sbuf("Es", [128, 4, NKF, 384], BF16)
        d_E = Dep("EcEs", multi=True)
        Cb_r = Ring(P, "CbF", [128, 4096], BF16, 2)
        Sb_r = Ring(P, "SbF", [128, 4096], BF16, 2)
        Ci_r = Ring(P, "CbI", [128, NKF * 128], BF16, 2)
        Si_r = Ring(P, "SbI", [128, NKF * 128], BF16, 2)
        gb_r = Ring(P, "gb", [128, 4, 768], F32, 1)
        fb = FoldBufs()
        tm_r = [Ring(P, f"cv{i}", [128, 384], F32, 1) for i in range(4)]
        tm2_r = [Ring(P, f"cw{i}", [128, 384], F32, 1) for i in range(4)]
        Y_r = {n: Ring(P, "Y" + n, [128, 384], F32, 1) for n in ("r1", "r2", "r3", "r4", "n1", "n2", "n3", "n4")}
        I_r = {n: Ring(P, "I" + n, [128, 384], F32, 1) for n in ("U", "V", "Up", "Vp", "W", "X", "Wp", "Xp")}
        gt_r = Ring(P, "gate", [128, 384], F32, 2)
        yo_r = Ring(P, "yo", [128, 384], F32, 2)
        for o in range(2):
            for kc in range(NKF):
                Cb, d_C = Cb_r.next()
                Sb, d_S = Sb_r.next()
                P.dma("sp", Cb[:], CfF[kc], writes=[d_C])
                P.dma("sp", Sb[:], SfF[kc], writes=[d_S])
                gb, d_gb = gb_r.next()
                P.dma("sp", gb[:].rearrange("p j c -> p (j c)"), Gs[o, kc], reads=[d_Gs], writes=[d_gb])
                ejs = {"1": "dve", "2": "pool", "3": "pool", "4": "dve"}
                ab = fold_forward(fb, lambda tau: u_tok[:, tau, :], d_u, Cb, Sb, d_C, d_S, "AB", l2eng=ejs)
                Y = {}
                for j in range(4):
                    aj, d_aj = ab[f"A{j + 1}"]
                    bj, d_bj = ab[f"B{j + 1}"]
                    ej = ejs[str(j + 1)]
                    t1, t2, t3, t4 = [r.next() for r in (tm_r if ej == "dve" else tm2_r)]
                    P.tt(ej, t1[0][:, :], aj[:, :], gb[:, j, 0:384], ALU.mult, [d_aj, d_gb], [t1[1]])
                    P.tt(ej, t2[0][:, :], bj[:, :], gb[:, j, 384:768], ALU.mult, [d_bj, d_gb], [t2[1]])
                    P.tt(ej, t3[0][:, :], bj[:, :], gb[:, j, 0:384], ALU.mult, [d_bj, d_gb], [t3[1]])
                    P.tt(ej, t4[0][:, :], aj[:, :], gb[:, j, 384:768], ALU.mult, [d_aj, d_gb], [t4[1]])
                    yr, d_yr = Y_r[f"r{j + 1}"].next()
                    P.tt(ej, yr[:, :], t1[0][:, :], t2[0][:, :], ALU.add, [t1[1], t2[1]], [d_yr])
                    yn, d_yn = Y_r[f"n{j + 1}"].next()
                    P.tt(ej, yn[:, :], t3[0][:, :], t4[0][:, :], ALU.subtract, [t3[1], t4[1]], [d_yn])
                    Y[f"r{j + 1}"] = (yr, d_yr)
                    Y[f"n{j + 1}"] = (yn, d_yn)
                I = {}

                def i1(nm, x, y, op):
                    t_, d_t = I_r[nm].next()
                    P.tt(ejs[x[1]], t_[:, :], Y[x][0][:, :], Y[y][0][:, :], op, [Y[x][1], Y[y][1]], [d_t])
                    I[nm] = (t_, d_t)

                i1("U", "r2", "r3", ALU.add)
                i1("V", "r2", "r3", ALU.subtract)
                i1("Up", "n2", "n3", ALU.add)
                i1("Vp", "n2", "n3", ALU.subtract)
                i1("W", "r1", "r4", ALU.add)
                i1("X", "r1", "r4", ALU.subtract)
                i1("Wp", "n1", "n4", ALU.subtract)
                i1("Xp", "n1", "n4", ALU.add)

                def i2(dst, r, x, y, op):
                    P.tt("dve" if dst is Ec else "pool", dst[:, r, kc, :], I[x][0][:, :], I[y][0][:, :], op,
                         [I[x][1], I[y][1]], [d_E])

                i2(Ec, 0, "W", "U", ALU.add)
                i2(Ec, 1, "X", "Up", ALU.add)
                i2(Ec, 2, "W", "U", ALU.subtract)
                i2(Ec, 3, "X", "Up", ALU.subtract)
                i2(Es, 0, "Wp", "Vp", ALU.subtract)
                i2(Es, 1, "Xp", "V", ALU.add)
                i2(Es, 2, "Wp", "Vp", ALU.add)
                i2(Es, 3, "Xp", "V", ALU.subtract)
            for tau in range(NT):
                r = tau // 8
                Ci, d_Ci = Ci_r.next()
                Si, d_Si = Si_r.next()
                P.dma("sp", Ci[:], CfI[tau], writes=[d_Ci])
                P.dma("sp", Si[:], SfI[tau], writes=[d_Si])
                gt, d_gt = gt_r.next()
                P.dma("sp", gt[:], xg[o, tau * 128:(tau + 1) * 128, :], reads=[d_xg], writes=[d_gt])
                P.flush()
                by, d_by = banks.get(tau % 2)
                for kc in range(NKF):
                    P.mm(by[:, 0:384], Ci[:, kc * 128:(kc + 1) * 128], Ec[:, r, kc, :], kc == 0, False, [d_Ci, d_E], [d_by])
                    P.mm(by[:, 0:384], Si[:, kc * 128:(kc + 1) * 128], Es[:, r, kc, :], False, kc == NKF - 1,
                         [d_Si, d_E], [d_by])
                if o == 0:
                    P.tt("dve", u_tok[:, tau, :], by[:, 0:384], gt[:, :], ALU.mult, [d_by, d_gt], [d_u])
                else:
                    yo, d_yo = yo_r.next()
                    P.tt("dve", yo[:, :], by[:, 0:384], gt[:, :], ALU.mult, [d_by, d_gt], [d_yo])
                    P.store(trows(ymix, tau)[:, 384:768], yo[:, :], reads=[d_yo], writes=[d_ymix], slot=d_yo)
            if o == 0 and stop(f"HC{l}"):
                dbg_u = nc.dram_tensor("dbg_u", [NT, 128, 384], F32, kind="ExternalOutput").ap()
                d_dbg = Dep("dbg", multi=True)
                P.store(dbg_u.rearrange("t p c -> p t c"), u_tok[:], reads=[d_u], writes=[d_dbg])
                return finish()
        P.pop()
        if stop(f"H{l}"):
            return finish()

        P.push()
        wg = P.sbuf("wg", [128, 8, DFF], BF16)
        wu = P.sbuf("wu", [128, 8, DFF], BF16)
        d_wg = Dep("wg")
        d_wu = Dep("wu")
        P.push()
        wo = P.sbuf("wo", [128, 8, D], BF16)
        d_wo = Dep("wo")
        P.dma("pool", wo[:], w_out[l].rearrange("(k p) c -> p k c", p=128), writes=[d_wo])
        P.dma("pool", wg[:], w_gate[l].rearrange("(k p) c -> p k c", p=128), writes=[d_wg])
        P.dma("pool", wu[:], w_up[l].rearrange("(k p) c -> p k c", p=128), writes=[d_wu])
        ym_r = Ring(P, "ym", [128, 4, D], F32, 1)
        h2o_r = Ring(P, "h2o", [128, 8, TB], BF16, 2)
        rsD = P.sbuf("rsD", [128, TB], F32)
        d_rsD = Dep("rsD")
        xtb_r = Ring(P, "xtbD", [128, 8, TB], F32, 2)
        yn_r = Ring(P, "yn", [128, D], F32, 2)
        yT_r = Ring(P, "yT", [128, 8, TB], BF16, 2)
        ssq_r = Ring(P, "ssq", [128, 12], F32, 2)
        rsd_r = Ring(P, "rsd", [128, 12], F32, 2)
        junk = P.sbuf("junk", [128, 384], BF16)
        d_junk = Dep("junk", multi=True)
        groups = [(0, 384), (384, 768), (768, 1024)]
        for b in range(NB):
            ym, d_ym = ym_r.next()
            P.dma("sp", ym[:], ymix[b * TB:(b + 1) * TB, :].rearrange("(t p) c -> p t c", p=128), reads=[d_ymix],
                  writes=[d_ym])
            xtb, d_xtb = xtb_r.next()
            P.dma("sp", xtb[:], xT[:, :, b * TB:(b + 1) * TB].rearrange("k p s -> p k s"), reads=[d_xT[b]],
                  writes=[d_xtb])
            P.flush()
            ssq, d_ssq = ssq_r.next()
            rsd, d_rsd = rsd_r.next()
            for i in range(4):
                for gi, (c0, c1) in enumerate(groups):
                    P.act(junk[:, 0:c1 - c0], ym[:, i, c0:c1], AF.Square, [d_ym], [d_junk, d_ssq],
                          accum=ssq[:, i * 3 + gi:i * 3 + gi + 1])
            sv = ssq[:, :].rearrange("p (i g) -> p i g", g=3)
            rv = rsd[:, :].rearrange("p (i g) -> p i g", g=3)
            for gi, (c0, c1) in enumerate(groups):
                P.act(rv[:, :, gi], sv[:, :, gi], AF.Sqrt, [d_ssq, d_const], [d_rsd], scale=1.0 / (c1 - c0),
                      bias=epsc[:, 0:1])
            P.recip(rsd[:, :], rsd[:, :], [d_rsd], [d_rsd])
            yT, d_yT = yT_r.next()
            for i in range(4):
                yn, d_yn = yn_r.next()
                for gi, (c0, c1) in enumerate(groups):
                    P.ts("dve" if gi < 2 else "pool", yn[:, c0:c1], ym[:, i, c0:c1], rsd[:, i * 3 + gi:i * 3 + gi + 1],
                         None, ALU.mult, None, [d_ym, d_rsd], [d_yn])
                for j in range(2):
                    bk, d_bk = banks.next()
                    for c in range(4):
                        k = j * 4 + c
                        P.tr(bk[:, c * 128:(c + 1) * 128], yn[:, k * 128:(k + 1) * 128], ident[:, :],
                             [d_yn, d_ident], [d_bk])
                    for c in range(4):
                        k = j * 4 + c
                        if c % 2:
                            P.act(yT[:, k, i * 128:(i + 1) * 128], bk[:, c * 128:(c + 1) * 128], AF.Copy,
                                  [d_bk, d_vec], [d_yT], scale=vl[:, 19 + k:20 + k])
                        else:
                            P.ts("dve", yT[:, k, i * 128:(i + 1) * 128], bk[:, c * 128:(c + 1) * 128],
                                 vl[:, 19 + k:20 + k], None, ALU.mult, None, [d_bk, d_vec], [d_yT])
            for mch in range(8):
                bk, d_bk = banks.next()
                for k in range(8):
                    P.mm(bk, wo[:, k, mch * 128:(mch + 1) * 128], yT[:, k, :], k == 0, k == 7, [d_wo, d_yT], [d_bk])
                P.tt("dve", xtb[:, mch, :], bk, xtb[:, mch, :], ALU.add, [d_bk, d_xtb], [d_xtb])
            P.store(xT[:, :, b * TB:(b + 1) * TB].rearrange("k p s -> p k s"), xtb[:], reads=[d_xtb],
                  writes=[d_xT[b]])
            h2o, d_h2o = h2o_r.next()
            rms_feature_major(xtb, d_xtb, 8, 8, vl, h2o, d_h2o, yT, d_yT, rsD, d_rsD, D)
            P.store(h2s[b], h2o[:], reads=[d_h2o], writes=[d_h2s[b]], slot=d_h2o)
        P.pop()
        if stop(f"D1{l}"):
            return finish()

        wd = P.sbuf("wd", [128, 22, D], BF16)
        d_wd = Dep("wd")
        P.dma("pool", wd[:], w_down[l].rearrange("(k p) c -> p k c", p=128), writes=[d_wd])
        xtb = P.sbuf("xtbF", [128, 8, TB], F32)
        d_xtb = Dep("xtbF")
        h2T_r = Ring(P, "h2T", [128, 8, TB], BF16, 2)
        actT = P.sbuf("actT", [128, 22, TB], BF16)
        d_act = Dep("actT")
        rs = P.sbuf("rsF", [128, TB], F32)
        d_rs = Dep("rsF")
        sg_r = Ring(P, "sg", [128, TB], F32, 2)
        last = (l == NL - 1)
        if last:
            ot_r = Ring(P, "ot", [128, D], F32, 2)

        def ld_h2(bb):
            t_, d_ = h2T_r.next()
            P.dma("sp", t_[:], h2s[bb], reads=[d_h2s[bb]], writes=[d_])
            return t_, d_

        nxt_h2 = ld_h2(0)
        for b in range(NB):
            h2T, d_h2 = nxt_h2
            if b + 1 < NB:
                nxt_h2 = ld_h2(b + 1)
            P.dma("sp", xtb[:], xT[:, :, b * TB:(b + 1) * TB].rearrange("k p s -> p k s"), reads=[d_xT[b]],
                  writes=[d_xtb])
            for f in range(22):
                bg, d_bg = banks.next()
                bu, d_bu = banks.next()
                for k in range(8):
                    P.mm(bg, wg[:, k, f * 128:(f + 1) * 128], h2T[:, k, :], k == 0, k == 7, [d_wg, d_h2], [d_bg])
                for k in range(8):
                    P.mm(bu, wu[:, k, f * 128:(f + 1) * 128], h2T[:, k, :], k == 0, k == 7, [d_wu, d_h2], [d_bu])
                sg, d_sg = sg_r.next()
                P.act(sg[:, :], bg, AF.Silu, [d_bg], [d_sg])
                P.tt("dve", actT[:, f, :], bu, sg[:, :], ALU.mult, [d_bu, d_sg], [d_act])
            for mch in range(8):
                bk, d_bk = banks.next()
                for f in range(22):
                    P.mm(bk, wd[:, f, mch * 128:(mch + 1) * 128], actT[:, f, :], f == 0, f == 21, [d_wd, d_act], [d_bk])
                P.tt("dve", xtb[:, mch, :], bk, xtb[:, mch, :], ALU.add, [d_bk, d_xtb], [d_xtb])
            if not last:
                P.store(xT[:, :, b * TB:(b + 1) * TB].rearrange("k p s -> p k s"), xtb[:], reads=[d_xtb],
                      writes=[d_xT[b]])
                P.flush()
                rms_feature_major(xtb, d_xtb, 8, 0, vec[l + 1], h2T, d_h2, actT, d_act, rs, d_rs, D)
                P.dma("sp", hAs[b], h2T[:], reads=[d_h2], writes=[d_hAs[b]], slot=d_h2)
            else:
                rms_feature_major(xtb, d_xtb, 8, 27, vl, xtb, d_xtb, actT, d_act, rs, d_rs, D)
                for i in range(4):
                    ot, d_ot = ot_r.next()
                    for j in range(2):
                        bk, d_bk = banks.next()
                        for c in range(4):
                            k = j * 4 + c
                            P.tr(bk[:, c * 128:(c + 1) * 128], xtb[:, k, i * 128:(i + 1) * 128], ident[:, :],
                                 [d_xtb, d_ident], [d_bk])
                        P.cp("act" if j else "dve", ot[:, j * 512:(j + 1) * 512], bk, [d_bk], [d_ot])
                    P.dma("sp", out_ap[b * TB + i * 128:b * TB + (i + 1) * 128, :], ot[:], reads=[d_ot],
                          writes=[d_out], slot=d_ot)
        P.pop()
        if stop(f"D2{l}"):
            return finish()

    return finish()


_CONST = {}


def host_constants():
    if _CONST:
        return _CONST
    f32 = np.float32
    c = _CONST
    c["ident"] = np.eye(128, dtype=f32)
    pos = np.arange(S, dtype=f32)
    inv = (np.float32(10000.0) ** (-np.arange(0, 32, 2, dtype=f32) / np.float32(32))).astype(f32)
    ang = (pos[:, None] * inv[None, :]).astype(f32)
    cos, sin = np.cos(ang).astype(f32), np.sin(ang).astype(f32)
    c["ropeC"] = np.ascontiguousarray(np.concatenate([cos, cos], 1).T)
    c["ropeS"] = np.ascontiguousarray(np.concatenate([-sin, sin], 1).T)
    t_idx = np.arange(S, dtype=f32)[:, None]
    t_norm = np.linspace(0.0, 1.0, S, dtype=f32)[:, None]
    bands = np.linspace(1e-4, 7, 8, dtype=f32)[None, :]
    angz = (np.float32(2.0 * math.pi) * t_idx * bands / np.float32(S)).astype(f32)
    z = np.concatenate([t_norm, np.cos(angz), np.sin(angz)], -1).astype(f32)
    c["zT"] = np.ascontiguousarray(z.T)
    deltas = np.linspace(math.log(1e-2) / 1.5, math.log(1e-2) / 0.3, 384, dtype=f32)
    c["decay"] = np.exp(-t_norm * np.abs(deltas)[None, :]).astype(f32)
    pp = np.arange(128, dtype=np.int64)
    tt = (512 * np.arange(8)[None, None, :] + 4 * pp[:, None, None] + np.arange(4)[None, :, None])
    kk = (128 * np.arange(NKF)[:, None] + pp[None, :])
    prod = (tt[None, :, :, :, None] * kk[:, None, None, None, :]) % 8192
    th = prod.astype(np.float64) * (2.0 * math.pi / 8192.0)
    c["CfF"] = np.cos(th).astype(f32).astype(ml_dtypes.bfloat16).reshape(NKF, 128, 4096)
    c["SfF"] = np.sin(th).astype(f32).astype(ml_dtypes.bfloat16).reshape(NKF, 128, 4096)
    tau = np.arange(NT)
    t2 = (512 * (tau % 8)[:, None] + 4 * pp[None, :] + (tau // 8)[:, None])
    kq = (128 * np.arange(NKF)[None, :] + pp[:, None])
    prod = (t2[:, None, None, :] * kq[None, :, :, None]) % 8192
    th = prod.astype(np.float64) * (2.0 * math.pi / 8192.0)
    c["CfI"] = np.cos(th).astype(f32).astype(ml_dtypes.bfloat16).reshape(NT, 128, NKF * 128)
    c["SfI"] = np.sin(th).astype(f32).astype(ml_dtypes.bfloat16).reshape(NT, 128, NKF * 128)
    wk = np.zeros((128, 8 * NKF), f32)
    for kc in range(NKF):
        for p in range(128):
            k = kc * 128 + p
            if k > 1024:
                continue
            orbit = [k, 2048 - k, 2048 + k, 4096 - k]
            w = [(1.0 if kp in (0, 4096) else 2.0) / 8192.0 for kp in orbit]
            if k == 0:
                w[2] = 0.0
            if k == 1024:
                w[1] = 0.0
                w[3] = 0.0
            for j in range(4):
                wk[p, kc * 4 + j] = w[j]
                wk[p, 4 * NKF + kc * 4 + j] = -w[j]
    c["wk"] = wk
    return c


def host_layout(inputs):
    f32 = np.float32
    g = {k: np.asarray(v) for k, v in inputs.items()}
    w_in = g["w_in"]
    zpad = np.zeros((NL, D, 64), f32)
    kpe = w_in[:, :, 384:416]
    kpe_sw = np.concatenate([kpe[:, :, 16:32], kpe[:, :, 0:16]], -1)
    w_inA = np.concatenate([w_in[:, :, 0:384], zpad, kpe, zpad, kpe_sw, w_in[:, :, 416:2336]], -1)
    assert w_inA.shape[-1] == WIN_COLS
    wuq = g["mla_w_uq"].reshape(NL, 256, 6, 96)
    zq = np.zeros((NL, 256, 6, 64), f32)
    w_uq2 = np.concatenate([wuq, zq, wuq[..., 80:96], wuq[..., 64:80]], -1).reshape(NL, 256, 1152)
    wkv = g["mla_w_ukv"].reshape(NL, 128, 6, 128)
    w_kv2 = np.concatenate([wkv[..., 0:64].reshape(NL, 128, 384), wkv[..., 64:128].reshape(NL, 128, 384)], -1)
    vecs = np.zeros((NL, 128, NV), f32)
    for l in range(NL):
        vecs[l, :, 0:8] = g["norm1_g"][l].reshape(8, 128).T
        vecs[l, :, 8:16] = g["norm2_g"][l].reshape(8, 128).T
        vecs[l, :, 16:18] = g["mla_q_norm_g"][l].reshape(2, 128).T
        vecs[l, :, 18:19] = g["mla_kv_norm_g"][l].reshape(1, 128).T
        vecs[l, :, 19:27] = g["mix_norm_g"][l].reshape(8, 128).T
        vecs[l, :, 27:35] = g["final_norm_g"].reshape(8, 128).T
        for j in range(3):
            vecs[l, :, 35 + j * 9:35 + (j + 1) * 9] = g["hy_conv_w"][l, j].reshape(9, 128).T
        vecs[l, :, 62:71] = g["hy_conv_b"][l].reshape(9, 128).T
        vecs[l, 0:64, 71] = g["hy_filt_b1"][l]
        vecs[l, 0:64, 72] = g["hy_filt_freq1"][l]
        vecs[l, 0:64, 73] = g["hy_filt_b2"][l]
        vecs[l, 0:64, 74] = g["hy_filt_freq2"][l]
    skipb = np.broadcast_to(g["hy_skip"].reshape(NL, 1, 768), (NL, 128, 768))
    rpb = g["na_rpb"]
    nab = np.full((NL, 5, 128, 4, 5, 128), -30000.0, f32)
    types = [(0, 0), (1, 0), (30, 27), (31, 27), (2, 0)]
    qf = np.arange(128)
    for ti, (m, c0) in enumerate(types):
        rq = 2 * m + qf // 64
        wq = qf % 64
        r0 = np.clip(rq - 4, 0, 56)
        cc0 = np.clip(wq - 8, 0, 48)
        for i in range(5):
            kt = (c0 + i) * 128 + np.arange(128)
            rk = kt // 64
            wkk = kt % 64
            inwin = ((rk[:, None] >= r0[None, :]) & (rk[:, None] < r0[None, :] + 8) &
                     (wkk[:, None] >= cc0[None, :]) & (wkk[:, None] < cc0[None, :] + 16))
            dr = np.clip(rk[:, None] - rq[None, :] + 7, 0, 14)
            dc = np.clip(wkk[:, None] - wq[None, :] + 15, 0, 30)
            for l in range(NL):
                for h in range(4):
                    vals = rpb[l, h][dr, dc]
                    nab[l, ti, :, h, i, :] = np.where(inwin, vals, f32(-30000.0))
    nabias = nab.reshape(NL, 5, 128, 2560)
    shared = dict(w_inA=w_inA, w_uq2=w_uq2, w_kv2=w_kv2, vecs=vecs, w_f1=g["hy_filt_w1"], w_f2=g["hy_filt_w2"],
                  w_f3=g["hy_filt_w3"], skipb=skipb, nabias=nabias, w_out=g["w_out"], w_gate=g["ffn_w_gate"],
                  w_up=g["ffn_w_up"], w_down=g["ffn_w_down"])
    shared = {k: np.ascontiguousarray(v, dtype=f32) for k, v in shared.items()}
    shared.update(host_constants())
    return shared


_NC = {}


def kernel(**inputs):
    shared = host_layout(inputs)
    x = np.asarray(inputs["x"], dtype=np.float32)
    if "nc" not in _NC:
        _NC["nc"] = build_program()
    nc = _NC["nc"]
    in_maps = []
    for c in range(8):
        m = dict(shared)
        m["x"] = np.ascontiguousarray(x[c])
        in_maps.append(m)
    res = run_bass_kernel_spmd(nc, in_maps, core_ids=list(range(8)))
    return np.stack([res.results[c]["out"] for c in range(8)], 0).astype(np.float32)
```
